# Optimizing a Trainium2 kernel written in Bass

```python
import jax, jax.numpy as jnp
from jax import lax
import numpy as np

D_MODEL = 1024
BATCH = 4
SEQ = 4096
DEPTH = 1
DEC_BATCH = 32
DEC_SEQ = 4
PAST_LEN = 16384
PAGE_SIZE = 128

RET_HEADS = 4
RET_DK = 256
RET_DV = 512
RET_CHUNK = 128
ATT_HEADS = 8
ATT_KV_HEADS = 2
ATT_HD = 128
IDX_HEADS = 8
IDX_HD = 64
TOPK_ATTN = 256
Q_BLOCK = 128
ROPE_THETA = 10000.0
PEER_HEADS = 8
PEER_NKEYS = 128
PEER_N = PEER_NKEYS * PEER_NKEYS
PEER_DKEY = 256
PEER_TOPK = 16
PEER_BLOCK = 128
PLE_DIM = 256
LN_EPS = 1e-5
ALPHA = (2.0 * DEPTH) ** 0.25
BETA = (8.0 * DEPTH) ** -0.25
NEG = -1e30
SPLITS = (RET_HEADS * RET_DK, RET_HEADS * RET_DK, RET_HEADS * RET_DV, RET_HEADS * RET_DV,
          ATT_HEADS * ATT_HD, ATT_KV_HEADS * ATT_HD, ATT_KV_HEADS * ATT_HD,
          IDX_HEADS * IDX_HD, IDX_HD, IDX_HEADS, D_MODEL, D_MODEL)
VALUE_SPLITS = (2, 6)

kernel_name = 'hybrid_retention_dsa_peer_step'


def layer_norm(x, g, b):
    xf = x.astype(jnp.float32)
    mu = jnp.mean(xf, axis=-1, keepdims=True)
    var = jnp.mean(jnp.square(xf - mu), axis=-1, keepdims=True)
    y = (xf - mu) * lax.rsqrt(var + LN_EPS) * g.astype(jnp.float32) + b.astype(jnp.float32)
    return y.astype(x.dtype)


def rope(x, pos):
    half = x.shape[-1] // 2
    inv = ROPE_THETA ** (-jnp.arange(half, dtype=jnp.float32) / half)
    ang = pos.astype(jnp.float32)[:, None] * inv[None, :]
    cos, sin = jnp.cos(ang)[:, None, :], jnp.sin(ang)[:, None, :]
    xf = x.astype(jnp.float32)
    x1, x2 = xf[..., :half], xf[..., half:]
    return jnp.concatenate([x1 * cos - x2 * sin, x2 * cos + x1 * sin], axis=-1).astype(x.dtype)


def project_in(x, w_in, pos, ik_g, ik_b):
    B, S, _ = x.shape
    offs = np.cumsum(SPLITS)[:-1].tolist()
    rq, rk, rv, rg, aq, ak, av, iq, ik, iw, ga, gb = jnp.split(x @ w_in, offs, axis=-1)
    rq = rope(rq.reshape(B, S, RET_HEADS, RET_DK), pos)
    rk = rope(rk.reshape(B, S, RET_HEADS, RET_DK), pos) * (RET_DK ** -0.5)
    rv = rv.reshape(B, S, RET_HEADS, RET_DV)
    aq = rope(aq.reshape(B, S, ATT_HEADS, ATT_HD), pos)
    ak = rope(ak.reshape(B, S, ATT_KV_HEADS, ATT_HD), pos)
    av = av.reshape(B, S, ATT_KV_HEADS, ATT_HD)
    iq = rope(iq.reshape(B, S, IDX_HEADS, IDX_HD), pos)
    ik = rope(layer_norm(ik, ik_g, ik_b)[:, :, None, :], pos)[:, :, 0, :]
    iw = iw * (IDX_HEADS ** -0.5)
    return rq, rk, rv, rg, aq, ak, av, iq, ik, iw, ga, gb


def retention_log_decay():
    return jnp.log1p(-jnp.exp2(-5.0 - jnp.arange(RET_HEADS, dtype=jnp.float32)))


def retention_chunk(state, qkv):
    q, k, v = (t.astype(jnp.float32) for t in qkv)
    C = q.shape[2]
    lg = retention_log_decay()[:, None]
    i = jnp.arange(C, dtype=jnp.float32)
    diff = i[:, None] - i[None, :]
    decay = jnp.where(diff >= 0, jnp.exp(lg[:, :, None] * jnp.maximum(diff, 0.0)), 0.0)
    inner = jnp.einsum('bhij,bhjv->bhiv', jnp.einsum('bhid,bhjd->bhij', q, k) * decay, v)
    cross = jnp.einsum('bhid,bhdv->bhiv', q, state) * jnp.exp(lg * (i + 1.0))[None, :, :, None]
    k_dec = k * jnp.exp(lg * (C - 1.0 - i))[None, :, :, None]
    new_state = state * jnp.exp(lg * C)[None, :, :, None] + jnp.einsum('bhjd,bhjv->bhdv', k_dec, v)
    return new_state, inner + cross


def retention_prompt(q, k, v):
    B, S = q.shape[:2]
    nc = S // RET_CHUNK

    def chunks(t):
        return t.reshape(B, nc, RET_CHUNK, RET_HEADS, t.shape[-1]).transpose(1, 0, 3, 2, 4)

    s0 = jnp.zeros((B, RET_HEADS, RET_DK, RET_DV), jnp.float32)
    s_fin, o = lax.scan(retention_chunk, s0, (chunks(q), chunks(k), chunks(v)))
    return o.transpose(1, 0, 3, 2, 4).reshape(B, S, RET_HEADS, RET_DV), s_fin


def retention_sample(q, k, v, state):
    s_new, o = retention_chunk(state.astype(jnp.float32),
                               (q.transpose(0, 2, 1, 3), k.transpose(0, 2, 1, 3), v.transpose(0, 2, 1, 3)))
    return o.transpose(0, 2, 1, 3), s_new


def retention_output(o, gate, gn_g, gn_b):
    B, S = o.shape[:2]
    mu = jnp.mean(o, axis=-1, keepdims=True)
    var = jnp.mean(jnp.square(o - mu), axis=-1, keepdims=True)
    on = ((o - mu) * lax.rsqrt(var + LN_EPS)).reshape(B, S, RET_HEADS * RET_DV)
    on = on * gn_g.astype(jnp.float32) + gn_b.astype(jnp.float32)
    return (jax.nn.silu(gate.astype(jnp.float32)) * on).astype(gate.dtype)


def indexer_scores(q_idx, w_idx, k_idx):
    s = jax.nn.relu(jnp.einsum('bthd,bld->bthl', q_idx.astype(jnp.float32), k_idx.astype(jnp.float32)) * (IDX_HD ** -0.5))
    return jnp.einsum('bth,bthl->btl', w_idx.astype(jnp.float32), s)


def sparse_attend(q, k_sel, v_sel, valid):
    B, T = q.shape[:2]
    qg = q.reshape(B, T, ATT_KV_HEADS, ATT_HEADS // ATT_KV_HEADS, ATT_HD).astype(jnp.float32)
    s = jnp.einsum('btgrd,btkgd->btgrk', qg, k_sel.astype(jnp.float32)) * (ATT_HD ** -0.5)
    s = jnp.where(valid[:, :, None, None, :], s, NEG)
    p = jax.nn.softmax(s, axis=-1)
    o = jnp.einsum('btgrk,btkgd->btgrd', p, v_sel.astype(jnp.float32))
    return o.reshape(B, T, ATT_HEADS * ATT_HD).astype(q.dtype)


gather_rows = jax.vmap(lambda t, i: t[i])


def attention_prompt(q, k, v, q_idx, w_idx, k_idx):
    B, S = q.shape[:2]
    topk = min(TOPK_ATTN, S // 4)
    nb = S // Q_BLOCK
    key_pos = jnp.arange(S, dtype=jnp.int32)

    def block(args):
        qb, qib, wb, b0 = args
        qpos = b0 + jnp.arange(Q_BLOCK, dtype=jnp.int32)
        scores = indexer_scores(qib, wb, k_idx)
        scores = jnp.where((key_pos[None, :] <= qpos[:, None])[None], scores, -jnp.inf)
        _, idx = lax.top_k(scores, topk)
        valid = idx <= qpos[None, :, None]
        return sparse_attend(qb, gather_rows(k, idx), gather_rows(v, idx), valid)

    def blocks(t):
        return t.reshape(B, nb, Q_BLOCK, *t.shape[2:]).swapaxes(0, 1)

    starts = jnp.arange(nb, dtype=jnp.int32) * Q_BLOCK
    o = lax.map(block, (blocks(q), blocks(q_idx), blocks(w_idx), starts))
    return o.swapaxes(0, 1).reshape(B, S, ATT_HEADS * ATT_HD)


def attention_sample(q, k_new, v_new, q_idx, w_idx, k_idx_new, cache_k, cache_v, cache_kidx, page_table):
    Bd, T = q.shape[:2]
    L = PAST_LEN + T
    topk = min(TOPK_ATTN, L // 4)
    kidx_past = cache_kidx[page_table].reshape(Bd, PAST_LEN, IDX_HD)
    kidx_all = jnp.concatenate([kidx_past, k_idx_new.astype(kidx_past.dtype)], axis=1)
    qpos = PAST_LEN + jnp.arange(T, dtype=jnp.int32)
    scores = indexer_scores(q_idx, w_idx, kidx_all)
    scores = jnp.where((jnp.arange(L, dtype=jnp.int32)[None, :] <= qpos[:, None])[None], scores, -jnp.inf)
    _, idx = lax.top_k(scores, topk)
    valid = idx <= qpos[None, :, None]
    in_past = (idx < PAST_LEN)[..., None, None]
    pidx = jnp.minimum(idx, PAST_LEN - 1)
    phys = page_table[jnp.arange(Bd)[:, None, None], pidx // PAGE_SIZE]
    off = pidx % PAGE_SIZE
    nidx = jnp.clip(idx - PAST_LEN, 0, T - 1)
    k_sel = jnp.where(in_past, cache_k[phys, off], gather_rows(k_new, nidx).astype(cache_k.dtype))
    v_sel = jnp.where(in_past, cache_v[phys, off], gather_rows(v_new, nidx).astype(cache_v.dtype))
    return sparse_attend(q, k_sel, v_sel, valid)


def peer(x, wq, subkeys, u, v):
    B, S, D = x.shape
    xt = x.reshape(B * S, D)
    n = xt.shape[0]
    xt = jnp.pad(xt, ((0, (-n) % PEER_BLOCK), (0, 0)))
    half = PEER_DKEY // 2

    def block(xb):
        T = xb.shape[0]
        q = (xb @ wq).reshape(T, PEER_HEADS, PEER_DKEY).astype(jnp.float32)
        s1 = jnp.einsum('thd,hnd->thn', q[..., :half], subkeys[:, 0].astype(jnp.float32))
        s2 = jnp.einsum('thd,hnd->thn', q[..., half:], subkeys[:, 1].astype(jnp.float32))
        v1, i1 = lax.top_k(s1, PEER_TOPK)
        v2, i2 = lax.top_k(s2, PEER_TOPK)
        cand = (v1[..., :, None] + v2[..., None, :]).reshape(T, PEER_HEADS, PEER_TOPK * PEER_TOPK)
        cidx = (i1[..., :, None] * PEER_NKEYS + i2[..., None, :]).reshape(T, PEER_HEADS, PEER_TOPK * PEER_TOPK)
        sc, sel = lax.top_k(cand, PEER_TOPK)
        eidx = jnp.take_along_axis(cidx, sel, axis=-1)
        g = jax.nn.softmax(sc, axis=-1)
        act = jax.nn.gelu(jnp.einsum('td,thkd->thk', xb.astype(jnp.float32), u[eidx].astype(jnp.float32)), approximate=False)
        return jnp.einsum('thk,thkd->td', g * act, v[eidx].astype(jnp.float32)).astype(x.dtype)

    out = lax.map(block, xt.reshape(-1, PEER_BLOCK, D))
    return out.reshape(-1, D)[:n].reshape(B, S, D)


def finish_layer(x, p_emb, ret_o, att_o, ga, gb, w_ret_o, w_att_o, w_out, ln1_g, ln1_b,
                 peer_wq, peer_subkeys, peer_u, peer_v, w_ple_gate, w_ple, ln2_g, ln2_b):
    branch = jax.nn.sigmoid(ga) * (ret_o @ w_ret_o) + jax.nn.sigmoid(gb) * (att_o @ w_att_o)
    x = layer_norm(ALPHA * x + branch @ w_out, ln1_g, ln1_b)
    ple = jax.nn.sigmoid(x @ w_ple_gate) * (p_emb @ w_ple)
    return layer_norm(ALPHA * x + peer(x, peer_wq, peer_subkeys, peer_u, peer_v) + ple, ln2_g, ln2_b)


def setup_inputs(seed: int = 0) -> dict:
    key = jax.random.key(seed)
    ks = jax.random.split(key, 32)
    f32 = jnp.float32

    def nrm(k, shape, s):
        return jax.random.normal(k, shape, f32) * s

    n_pages = PAST_LEN // PAGE_SIZE
    n_used = DEC_BATCH * n_pages
    n_phys = n_used + n_used // 4 + 1
    page_table = jax.random.permutation(ks[0], n_phys)[:n_used].reshape(DEC_BATCH, n_pages).astype(jnp.int32)
    in_cols = sum(SPLITS)
    col_scale = jnp.concatenate([jnp.full((c,), BETA if j in VALUE_SPLITS else 1.0, f32) for j, c in enumerate(SPLITS)])
    return {
        'x_prompt': nrm(ks[1], (BATCH, SEQ, D_MODEL), 1.0),
        'x_sample': nrm(ks[2], (DEC_BATCH, DEC_SEQ, D_MODEL), 1.0),
        'cache_k': nrm(ks[3], (DEPTH, n_phys, PAGE_SIZE, ATT_KV_HEADS, ATT_HD), 1.0),
        'cache_v': nrm(ks[4], (DEPTH, n_phys, PAGE_SIZE, ATT_KV_HEADS, ATT_HD), BETA),
        'cache_kidx': nrm(ks[5], (DEPTH, n_phys, PAGE_SIZE, IDX_HD), 1.0),
        'state_ret': nrm(ks[6], (DEPTH, DEC_BATCH, RET_HEADS, RET_DK, RET_DV), 0.3),
        'page_table': page_table,
        'p_prompt': nrm(ks[7], (DEPTH, BATCH, SEQ, PLE_DIM), 1.0),
        'p_sample': nrm(ks[8], (DEPTH, DEC_BATCH, DEC_SEQ, PLE_DIM), 1.0),
        'w_in': nrm(ks[9], (DEPTH, D_MODEL, in_cols), D_MODEL ** -0.5) * col_scale,
        'idx_k_g': 1.0 + nrm(ks[10], (DEPTH, IDX_HD), 0.02),
        'idx_k_b': nrm(ks[11], (DEPTH, IDX_HD), 0.02),
        'gn_g': 1.0 + nrm(ks[12], (DEPTH, RET_HEADS * RET_DV), 0.02),
        'gn_b': nrm(ks[13], (DEPTH, RET_HEADS * RET_DV), 0.02),
        'w_ret_o': nrm(ks[14], (DEPTH, RET_HEADS * RET_DV, D_MODEL), BETA * (RET_HEADS * RET_DV) ** -0.5),
        'w_att_o': nrm(ks[15], (DEPTH, ATT_HEADS * ATT_HD, D_MODEL), BETA * (ATT_HEADS * ATT_HD) ** -0.5),
        'w_out': nrm(ks[16], (DEPTH, D_MODEL, D_MODEL), BETA * D_MODEL ** -0.5),
        'ln1_g': 1.0 + nrm(ks[17], (DEPTH, D_MODEL), 0.02),
        'ln1_b': nrm(ks[18], (DEPTH, D_MODEL), 0.02),
        'peer_wq': nrm(ks[19], (DEPTH, D_MODEL, PEER_HEADS * PEER_DKEY), D_MODEL ** -0.5),
        'peer_subkeys': nrm(ks[20], (DEPTH, PEER_HEADS, 2, PEER_NKEYS, PEER_DKEY // 2), (PEER_DKEY // 2) ** -0.5),
        'peer_u': nrm(ks[21], (DEPTH, PEER_N, D_MODEL), D_MODEL ** -0.5),
        'peer_v': nrm(ks[22], (DEPTH, PEER_N, D_MODEL), BETA * PEER_HEADS ** -0.5),
        'w_ple_gate': nrm(ks[23], (DEPTH, D_MODEL, D_MODEL), D_MODEL ** -0.5),
        'w_ple': nrm(ks[24], (DEPTH, PLE_DIM, D_MODEL), BETA * PLE_DIM ** -0.5),
        'ln2_g': 1.0 + nrm(ks[25], (DEPTH, D_MODEL), 0.02),
        'ln2_b': nrm(ks[26], (DEPTH, D_MODEL), 0.02),
    }


def reference(x_prompt, x_sample, cache_k, cache_v, cache_kidx, state_ret, page_table,
              p_prompt, p_sample, w_in, idx_k_g, idx_k_b, gn_g, gn_b, w_ret_o, w_att_o, w_out,
              ln1_g, ln1_b, peer_wq, peer_subkeys, peer_u, peer_v, w_ple_gate, w_ple, ln2_g, ln2_b):
    pos_p = jnp.arange(x_prompt.shape[1], dtype=jnp.int32)
    pos_s = PAST_LEN + jnp.arange(x_sample.shape[1], dtype=jnp.int32)
    xp, xs = x_prompt, x_sample
    kp, vp, ikp, rp, ksm, vsm, iksm, rsm = [], [], [], [], [], [], [], []
    for i in range(DEPTH):
        tail = (w_ret_o[i], w_att_o[i], w_out[i], ln1_g[i], ln1_b[i], peer_wq[i], peer_subkeys[i],
                peer_u[i], peer_v[i], w_ple_gate[i], w_ple[i], ln2_g[i], ln2_b[i])
        rq, rk, rv, rg, aq, ak, av, iq, ik, iw, ga, gb = project_in(xp, w_in[i], pos_p, idx_k_g[i], idx_k_b[i])
        o_ret, s_fin = retention_prompt(rq, rk, rv)
        ret_o = retention_output(o_ret, rg, gn_g[i], gn_b[i])
        att_o = attention_prompt(aq, ak, av, iq, iw, ik)
        xp = finish_layer(xp, p_prompt[i], ret_o, att_o, ga, gb, *tail)
        kp.append(ak)
        vp.append(av)
        ikp.append(ik)
        rp.append(s_fin.astype(state_ret.dtype))
        rq, rk, rv, rg, aq, ak, av, iq, ik, iw, ga, gb = project_in(xs, w_in[i], pos_s, idx_k_g[i], idx_k_b[i])
        o_ret, s_new = retention_sample(rq, rk, rv, state_ret[i])
        ret_o = retention_output(o_ret, rg, gn_g[i], gn_b[i])
        att_o = attention_sample(aq, ak, av, iq, iw, ik, cache_k[i], cache_v[i], cache_kidx[i], page_table)
        xs = finish_layer(xs, p_sample[i], ret_o, att_o, ga, gb, *tail)
        ksm.append(ak)
        vsm.append(av)
        iksm.append(ik)
        rsm.append(s_new.astype(state_ret.dtype))
    return (xp, xs, jnp.stack(kp), jnp.stack(vp), jnp.stack(ikp), jnp.stack(rp),
            jnp.stack(ksm), jnp.stack(vsm), jnp.stack(iksm), jnp.stack(rsm))
```

```python
import numpy as np
from contextlib import ExitStack
import concourse.bass as bass
import concourse.mybir as mybir
from concourse.bass_utils import run_bass_kernel_spmd

F32 = mybir.dt.float32
BF16 = mybir.dt.bfloat16
I32 = mybir.dt.int32
U32 = mybir.dt.uint32
AF = mybir.ActivationFunctionType
ALU = mybir.AluOpType
AX = mybir.AxisListType

D = 1024
NCOLS = 10312
O_RQ, O_RK, O_RV, O_RG, O_AQ, O_AK, O_AV, O_IQ, O_IK, O_IW, O_GA, O_GB = (
    0, 1024, 2048, 4096, 6144, 7168, 7424, 7680, 8192, 8256, 8264, 9288)
ALPHA = 2.0 ** 0.25
LN_EPS = 1e-5
NEG = -1e30
NIT = 20
import os as _os
DBG_STOP = _os.environ.get('DBG_STOP', '')


class _Sem:
    __slots__ = ("h", "id")

    def __init__(self, h, i):
        self.h = h
        self.id = i


class Sched:
    ENG = ("pe", "act", "dve", "pool", "sp")
    LIMIT = 30000

    def __init__(self, nc, stack, n_dma_sems=40):
        self.nc = nc
        self.stack = stack
        self._nsem = 0
        self.q = {e: [] for e in self.ENG}
        self.esem = {e: self._newsem("e_" + e) for e in self.ENG if e != "sp"}
        self.ecnt = {e: 0 for e in self.ENG}
        self.waited = {e: {} for e in self.ENG}
        self.lastw = {}
        self.readers = {}
        self.dpool = [self._newsem("d%d" % i) for i in range(n_dma_sems)]
        self.dcnt = {s.id: 0 for s in self.dpool}
        self.drr = 0
        self.drr_pool = 0
        self.named_dsem = {}
        self.nops = 0
        self.all_esems = list(self.esem.values())

    def _newsem(self, name):
        h = self.stack.enter_context(self.nc.semaphore(name + "_%d" % self._nsem))
        s = _Sem(h, self._nsem)
        self._nsem += 1
        return s

    @staticmethod
    def _key(x):
        if isinstance(x, str):
            return x
        if isinstance(x, tuple):
            return Sched._key(x[0]) + "/" + str(x[1])
        n = getattr(x, "name", None)
        if n is None:
            n = getattr(getattr(x, "tensor", None), "name", None)
        assert n is not None, x
        return n

    def dsem(self, name):
        if name not in self.named_dsem:
            s = self._newsem("n_" + name)
            self.named_dsem[name] = s
            self.dcnt[s.id] = 0
        return self.named_dsem[name]

    def op(self, eng, fn, r=(), w=(), dma=False, dsem=None):
        self.nops += 1
        rk = [self._key(x) for x in r]
        wk = [self._key(x) for x in w]
        deps = []
        for k in rk:
            deps.extend(self.lastw.get(k, ()))
        for k in wk:
            deps.extend(self.lastw.get(k, ()))
            deps.extend(self.readers.get(k, ()))
        if dma:
            if dsem is None:
                half = len(self.dpool) // 2
                if eng == "pool":
                    s = self.dpool[half + self.drr_pool % half]
                    self.drr_pool += 1
                else:
                    s = self.dpool[self.drr % half]
                    self.drr += 1
            else:
                s = self.dsem(dsem) if isinstance(dsem, str) else dsem
            if self.dcnt[s.id] > 0:
                deps.append((s, 16 * self.dcnt[s.id], "dma"))
            self.dcnt[s.id] += 1
            tok = (s, 16 * self.dcnt[s.id], "dma")
        else:
            if self.ecnt[eng] >= self.LIMIT:
                self.esem[eng] = self._newsem("e_" + eng)
                self.all_esems.append(self.esem[eng])
                self.ecnt[eng] = 0
            self.ecnt[eng] += 1
            tok = (self.esem[eng], self.ecnt[eng], eng)
        need = {}
        for (s, v, e) in deps:
            if eng == "pe" and e == "pe":
                continue
            if self.waited[eng].get(s.id, 0) >= v:
                continue
            if need.get(s.id, (None, 0))[1] < v:
                need[s.id] = (s, v)
        waits = []
        for sid, (s, v) in need.items():
            self.waited[eng][sid] = v
            waits.append((s, v))
        self.q[eng].append((waits, fn, tok, dma))
        for k in wk:
            self.lastw[k] = [tok]
            self.readers[k] = []
        for k in rk:
            if k not in wk:
                self.readers.setdefault(k, []).append(tok)
        return tok

    def barrier(self):
        fin = []
        for s in list(self.dpool) + list(self.named_dsem.values()):
            if self.dcnt[s.id] > 0:
                fin.append((s, 16 * self.dcnt[s.id]))
        for e in ("pe", "act", "dve", "pool"):
            if self.ecnt[e] > 0:
                fin.append((self.esem[e], self.ecnt[e]))
        for e in self.ENG:
            waits = []
            for (s, v) in fin:
                if self.waited[e].get(s.id, 0) < v:
                    self.waited[e][s.id] = v
                    waits.append((s, v))
            if waits:
                self.q[e].append((waits, None, None, False))
        self.lastw = {}
        self.readers = {}

    def emit(self):
        self.barrier()
        q = self.q

        def run(eng_name, engine):
            for (waits, fn, tok, dma) in q[eng_name]:
                for (s, v) in waits:
                    engine.wait_ge(s.h, v)
                if fn is None:
                    continue
                ins = fn(engine)
                ins.then_inc(tok[0].h, 16 if dma else 1)

        with self.nc.Block() as block:
            @block.tensor
            def _(e):
                run("pe", e)

            @block.scalar
            def _(e):
                run("act", e)

            @block.vector
            def _(e):
                run("dve", e)

            @block.gpsimd
            def _(e):
                run("pool", e)

            @block.sync
            def _(e):
                run("sp", e)
        self.q = {e: [] for e in self.ENG}


class Cfg:
    def __init__(self, NL=16, NO=16, TOPK=256, sample=True, NSEQ=32, PAST=16384, NPHYS=5121,
                 TOPK_S=256, prompt=True):
        self.prompt = prompt
        if not prompt:
            NL = NO = 0
        self.NL, self.NO, self.TOPK = NL, NO, TOPK
        self.NBLK = NL + NO
        self.sample = sample
        self.NSEQ, self.PAST, self.NPHYS, self.TOPK_S = NSEQ, PAST, NPHYS, TOPK_S
        self.NPAGE = PAST // 128


GROUPS = [("rq", O_RQ, 1024), ("rk", O_RK, 1024), ("rv", O_RV, 2048), ("rg", O_RG, 2048),
          ("aq", O_AQ, 1024), ("kv", O_AK, 512), ("iq", O_IQ, 512), ("ikw", O_IK, 72),
          ("ga", O_GA, 1024), ("gb", O_GB, 1024)]
LIGHT_GROUPS = ("rk", "rv", "kv", "ikw")


def chunk_table():
    ch = []
    idx = {}

    def add(name, src, r0, nkt, c0, ncols):
        idx.setdefault(name, []).append(len(ch))
        ch.append(dict(name=name, src=src, r0=r0, nkt=nkt, c0=c0, nc=ncols))

    for (g, c0, n) in GROUPS:
        for c in range(0, n, 512):
            add("in_" + g, "w_in", 0, 8, c0 + c, min(512, n - c))
    for chh in range(2):
        for kh in range(2):
            add("ret_o%d" % chh, "w_ret_o", kh * 1024, 8, chh * 512, 512)
    for nm, src in (("att_o", "w_att_o"), ("out", "w_out"), ("pg", "w_ple_gate")):
        for chh in range(2):
            add(nm, src, 0, 8, chh * 512, 512)
    for c in range(4):
        add("wq", "peer_wq", 0, 8, c * 512, 512)
    add("ple", "w_ple", 0, 2, 0, 1024)
    add("skT", "skT", 0, 1, 0, 2048)
    return ch, idx


def _layout(items):
    lay = {}
    o = 0
    for name, n in items:
        lay[name] = (o, n)
        o += n
    return lay, o


def cst_layout():
    return _layout((("ident", 128), ("decayT", 512), ("rowdec", 512), ("kdec", 4), ("causal", 128),
                    ("iota16", 16), ("ikg", 64), ("ikb", 64), ("lightbias", 1), ("iota128", 128)))


CS2, NCS2 = _layout((("decayT", 512), ("rowdec", 512), ("kdec", 4), ("tmask", 4), ("seqmask", 32),
                     ("causal4", 4), ("onesw", 252), ("iota256", 256)))
CST, NCST = cst_layout()


class TV:
    def __init__(self, name, ap):
        self.name = name
        self.base = ap

    def __getitem__(self, k):
        return self.base[k]


class Prog:
    def __init__(self, cfg):
        self.cfg = cfg
        self.nc = bass.Bass("TRN2", target_bir_lowering=False)
        self.chunks, self.cidx = chunk_table()

    def declare(self):
        nc, cfg = self.nc, self.cfg
        di = lambda n, s, d=F32: nc.dram_tensor(n, list(s), d, kind="ExternalInput").ap()
        do = lambda n, s, d=F32: nc.dram_tensor(n, list(s), d, kind="ExternalOutput").ap()
        self.d = d = {}
        if cfg.prompt:
            d["xb"] = di("xb", [cfg.NBLK, 128, D])
            d["pb"] = di("pb", [cfg.NO, 128, 256])
        d["csb"] = di("csb", [cfg.NBLK + 1, 128, 448])
        d["cst"] = di("cst", [128, NCST])
        d["params"] = di("params", [128, 8192])
        d["w_in"] = di("w_in", [D, NCOLS])
        d["w_ret_o"] = di("w_ret_o", [2048, D])
        d["w_att_o"] = di("w_att_o", [D, D])
        d["w_out"] = di("w_out", [D, D])
        d["w_ple_gate"] = di("w_ple_gate", [D, D])
        d["w_ple"] = di("w_ple", [256, D])
        d["peer_wq"] = di("peer_wq", [D, 2048])
        d["skT"] = di("skT", [128, 2048])
        d["peer_u"] = di("peer_u", [16384, D])
        d["peer_v"] = di("peer_v", [16384, D])
        if cfg.prompt:
            d["y"] = do("y", [cfg.NO, 128, D])
            d["kout"] = do("kout", [cfg.NO, 128, 256])
            d["vout"] = do("vout", [cfg.NO, 128, 256])
            d["kidxout"] = do("kidxout", [cfg.NO, 128, 64])
            d["stout"] = do("stout", [4, 256, 512])
        if cfg.sample:
            d["xs"] = di("xs", [128, D])
            d["ps"] = di("ps", [128, 256])
            d["cache_k"] = di("cache_k", [cfg.NPHYS * 128, 256])
            d["cache_v"] = di("cache_v", [cfg.NPHYS * 128, 256])
            d["cache_kidx"] = di("cache_kidx", [cfg.NPHYS, 8192])
            d["state_in"] = di("state_in", [cfg.NSEQ, 4, 256, 512])
            d["ptcol"] = di("ptcol", [cfg.NPAGE, cfg.NSEQ], I32)
            d["ptrow"] = di("ptrow", [128, cfg.NPAGE], I32)
            d["cst2"] = di("cst2", [128, NCS2])
            d["ys"] = do("ys", [128, D])
            d["ks"] = do("ks", [128, 256])
            d["vs"] = do("vs", [128, 256])
            d["kidxs"] = do("kidxs", [128, 64])
            d["state_out"] = do("state_out", [cfg.NSEQ, 4, 256, 512])
        self.wsc = nc.dram_tensor("wsc", [len(self.chunks), 128, 4096], BF16).ap()

    def sb(self, name, shape, dtype, stack=None):
        return (stack or self.st).enter_context(self.nc.sbuf_tensor("s_" + name, list(shape), dtype))

    def ps(self, name, shape, dtype):
        return self.st.enter_context(self.nc.psum_tensor("p_" + name, list(shape), dtype))

    def c(self, name, lo=0, n=None):
        o, sz = CST[name]
        n = sz - lo if n is None else n
        return self.cst[:, o + lo:o + lo + n]

    def load_chunk(self, ci):
        S = self.S
        ch = self.chunks[ci]
        n = ch["nkt"] * ch["nc"]
        slot = self.wr_next % len(self.wring)
        self.wr_next += 1
        wt = self.wring[slot]
        S.op("sp", lambda e, wt=wt, ci=ci, n=n: e.dma_start(out=wt[:, 0:n], in_=self.wsc[ci, :, 0:n]),
             w=[wt], dma=True, dsem="wr%d" % slot)
        return wt

    def pmm_next(self):
        p = self.pmm[self.pmm_i % 2]
        self.pmm_i += 1
        return p

    def transposes(self, srcs, dst3, alt=0, dkey=None):
        S = self.S
        for g0 in range(0, len(srcs), 8):
            grp = srcs[g0:g0 + 8]
            pt = self.ptr[self.ptr_i % 2]
            self.ptr_i += 1
            ptb = pt[:].bitcast(BF16)
            for i, (ap, rkeys) in enumerate(grp):
                ccols = ap.shape[1]
                S.op("pe", lambda e, ap=ap, i=i, ptb=ptb, ccols=ccols: e.transpose(
                    out=ptb[0:ccols, i * 128:(i + 1) * 128], in_=ap, identity=self.identb[:]),
                    r=list(rkeys) + [self.identb], w=[pt])
            ccols = grp[0][0].shape[1]
            n = len(grp)
            dview = dst3[0:ccols, g0:g0 + n, :]
            sview = ptb[0:ccols, 0:n * 128].rearrange("p (a b) -> p a b", a=n)
            eng = "act" if (alt + g0 // 8) % 2 == 0 else "dve"
            if eng == "act":
                S.op("act", lambda e, dview=dview, sview=sview: e.copy(out=dview, in_=sview), r=[pt], w=[dkey or dst3])
            else:
                S.op("dve", lambda e, dview=dview, sview=sview: e.tensor_copy(out=dview, in_=sview), r=[pt], w=[dkey or dst3])

    def rope(self, src3, dst3, H, half, cos, sin, rkeys, wkeys):
        S = self.S
        n = H * half
        R = [r[:, 0:n].rearrange("p (h f) -> p h f", h=H) for r in self.R]
        cb = cos.unsqueeze(1).to_broadcast([128, H, half])
        sbb = sin.unsqueeze(1).to_broadcast([128, H, half])
        x1 = src3[:, :, 0:half]
        x2 = src3[:, :, half:2 * half]
        M = ALU.mult
        S.op("dve", lambda e: e.tensor_tensor(out=R[0], in0=x1, in1=cb, op=M), r=rkeys + [self.cs], w=[self.R[0]])
        S.op("pool", lambda e: e.tensor_tensor(out=R[1], in0=x2, in1=sbb, op=M), r=rkeys + [self.cs], w=[self.R[1]])
        S.op("dve", lambda e: e.tensor_tensor(out=R[2], in0=x2, in1=cb, op=M), r=rkeys + [self.cs], w=[self.R[2]])
        S.op("pool", lambda e: e.tensor_tensor(out=R[3], in0=x1, in1=sbb, op=M), r=rkeys + [self.cs], w=[self.R[3]])
        S.op("dve", lambda e: e.tensor_tensor(out=dst3[:, :, 0:half], in0=R[0], in1=R[1], op=ALU.subtract),
             r=[self.R[0], self.R[1]], w=wkeys)
        S.op("dve", lambda e: e.tensor_tensor(out=dst3[:, :, half:2 * half], in0=R[2], in1=R[3], op=ALU.add),
             r=[self.R[2], self.R[3]], w=wkeys)

    def layer_norm(self, src, dst, n, g, b, rkeys, wkeys, gkeys):
        S = self.S
        st = self.lnst
        S.op("dve", lambda e: e.reduce_sum(out=st[:, 0:1], in_=src, axis=AX.X), r=rkeys, w=[(st, 0)])
        S.op("act", lambda e: e.activation(out=self.junka[:, 0:n], in_=src, func=AF.Square, accum_out=st[:, 1:2]),
             r=rkeys, w=[(st, 1), self.junka])
        S.op("dve", lambda e: e.tensor_scalar(out=st[:, 2:3], in0=st[:, 0:1], scalar1=1.0 / n, scalar2=None, op0=ALU.mult),
             r=[(st, 0)], w=[(st, 2)])
        S.op("dve", lambda e: e.tensor_tensor(out=st[:, 3:4], in0=st[:, 2:3], in1=st[:, 2:3], op=ALU.mult),
             r=[(st, 2)], w=[(st, 3)])
        S.op("dve", lambda e: e.scalar_tensor_tensor(out=st[:, 4:5], in0=st[:, 1:2], scalar=1.0 / n, in1=st[:, 3:4],
                                                     op0=ALU.mult, op1=ALU.subtract),
             r=[(st, 1), (st, 3)], w=[(st, 4)])
        S.op("dve", lambda e: e.tensor_scalar(out=st[:, 4:5], in0=st[:, 4:5], scalar1=LN_EPS, scalar2=None, op0=ALU.add),
             r=[(st, 4)], w=[(st, 4)])
        S.op("act", lambda e: e.activation(out=st[:, 5:6], in_=st[:, 4:5], func=AF.Sqrt), r=[(st, 4)], w=[(st, 5)])
        S.op("dve", lambda e: e.reciprocal(out=st[:, 6:7], in_=st[:, 5:6]), r=[(st, 5)], w=[(st, 6)])
        S.op("dve", lambda e: e.tensor_scalar(out=dst, in0=src, scalar1=st[:, 2:3], scalar2=st[:, 6:7],
                                              op0=ALU.subtract, op1=ALU.mult),
             r=rkeys + [(st, 2), (st, 6)], w=wkeys)
        if g is not None:
            S.op("dve", lambda e: e.tensor_tensor(out=dst, in0=dst, in1=g, op=ALU.mult), r=wkeys + gkeys, w=wkeys)
            S.op("dve", lambda e: e.tensor_tensor(out=dst, in0=dst, in1=b, op=ALU.add), r=wkeys + gkeys, w=wkeys)

    def phase_prep(self):
        S, d = self.S, self.d
        with ExitStack() as st2:
            sf = [self.sb("prep_f%d" % i, [128, 4096], F32, st2) for i in range(2)]
            sbf = [self.sb("prep_b%d" % i, [128, 4096], BF16, st2) for i in range(2)]
            for ci, ch in enumerate(self.chunks):
                f, bt = sf[ci % 2], sbf[ci % 2]
                n = ch["nkt"] * ch["nc"]
                if ch["src"] == "skT":
                    src = d["skT"]
                    dstv = f[:, 0:n]
                else:
                    src = d[ch["src"]][ch["r0"]:ch["r0"] + ch["nkt"] * 128, ch["c0"]:ch["c0"] + ch["nc"]].rearrange(
                        "(kt p) n -> p kt n", p=128)
                    dstv = f[:, 0:n].rearrange("p (kt n) -> p kt n", kt=ch["nkt"])
                S.op("sp", lambda e, dstv=dstv, src=src: e.dma_start(out=dstv, in_=src), w=[f], dma=True)
                if ci % 2 == 0:
                    S.op("act", lambda e, bt=bt, f=f, n=n: e.copy(out=bt[:, 0:n], in_=f[:, 0:n]), r=[f], w=[bt])
                else:
                    S.op("dve", lambda e, bt=bt, f=f, n=n: e.tensor_copy(out=bt[:, 0:n], in_=f[:, 0:n]), r=[f], w=[bt])
                S.op("pool", lambda e, bt=bt, ci=ci, n=n: e.dma_start(out=self.wsc[ci, :, 0:n], in_=bt[:, 0:n]),
                     r=[bt], w=["wsc"], dma=True)
            S.emit()

    def alloc_common(self):
        S, d = self.S, self.d
        sb, ps = self.sb, self.ps
        self.cst = sb("cst", [128, NCST], F32)
        self.identb = sb("identb", [128, 128], BF16)
        self.ARENA = sb("ARENA", [128, 16384], F32)
        self._ar_off = 0

        def carve(name, shape, dtype):
            n = int(np.prod(shape[1:]))
            n32 = n if dtype in (F32, I32, U32) else n // 2
            ap = self.ARENA[:, self._ar_off:self._ar_off + n32]
            self._ar_off += n32
            assert self._ar_off <= 16384
            if dtype != F32:
                ap = ap.bitcast(dtype)
            if len(shape) == 3:
                ap = ap.rearrange("p (a b) -> p a b", a=shape[1])
            return TV("v_" + name, ap)
        self.carve = carve
        self.wring = [carve("wring%d" % i, [128, 4096], BF16) for i in range(2)]
        self.wr_next = 0
        self.pmm = [ps("pmm%d" % i, [128, 512], F32) for i in range(2)]
        self.ptr = [ps("ptr%d" % i, [128, 512], F32) for i in range(2)]
        self.pS = [ps("pS%d" % i, [128, 512], F32) for i in range(2)]
        self.pO = [ps("pO%d" % i, [128, 512], F32) for i in range(2)]
        self.pmm_i = self.ptr_i = self.pS_i = 0
        self.junka = sb("junka", [128, 1024], BF16)
        self.junkd = sb("junkd", [128, 1024], BF16)
        self.lnst = sb("lnst", [128, 8], F32)
        self.R = [carve("R%d" % i, [128, 512], F32) for i in range(4)]
        self.F = [carve("F%d" % i, [128, 512], F32) for i in range(2)]
        self.cs = sb("cs", [128, 448], F32)
        self.X1 = sb("X1", [128, D], F32)
        self.XY = sb("XY", [128, D], F32)
        self.xblk = self.XY
        self.Y = self.XY
        self.stA = self.X1
        self.stB = self.XY
        self.xT = sb("xT", [128, 8, 128], BF16)
        self.TT2 = self.xT
        self.hkv = sb("hkv", [128, 512], F32)
        self.pblk = self.hkv
        self.hikw = sb("hikw", [128, 72], F32)
        self.rv_bf = sb("rv_bf", [128, 4, 512], BF16)
        self.sgate = sb("sgate", [128, 2048], BF16)
        self.ret_o = self.sgate
        self.sga = sb("sga", [128, D], BF16)
        self.sgb = sb("sgb", [128, D], BF16)
        self.B = [carve("B%d" % i, [128, 1024], BF16) for i in range(4)]
        self.akf = sb("akf", [128, 256], F32)
        self.ikf = sb("ikf", [128, 64], F32)
        self.ikd = sb("ikd", [128, 128], BF16)
        self.RQK = sb("RQK", [128, 16, 128], BF16)
        self.rqT = self.RQK[:, 0:8, :]
        self.rkT = self.RQK[:, 8:16, :]
        self.qT = self.RQK
        self.aqT = sb("aqT", [128, 8, 128], BF16)
        self.iqT = sb("iqT", [128, 4, 128], BF16)
        self.PT = [sb("PT%d" % i, [128, 128], BF16) for i in range(2)]
        self.qd = sb("qd", [128, 2, 128], BF16)
        self.sbf = sb("sbf", [128, 2, 512], BF16)
        self.kd = sb("kd", [128, 256], BF16)
        self.att_o = sb("att_o", [128, 8, 128], BF16)
        self.TT = carve("TT", [128, 16, 128], BF16)
        self.GI = carve("GI", [128, 4096], F32)
        self.ppT = sb("ppT", [128, 2, 128], BF16)
        self.wsm = sb("wsm", [128, 32], F32)
        self.bis = sb("bis", [128, 16], F32)
        self.cnt4 = sb("cnt4", [128, 16], F32)
        self.Ebuf = [sb("Ebuf%d" % i, [128, 512], BF16) for i in range(2)]
        self.Pbuf = [sb("Pbuf%d" % i, [128, 4, 128], BF16) for i in range(2)]
        self.mch = sb("mch", [128, 512], BF16)
        self.rsum = sb("rsum", [128, 8], F32)
        self.V16 = sb("V16", [128, 16, 16], F32)
        self.I16 = sb("I16", [128, 16, 16], U32)
        self.TMPC = sb("TMPC", [128, 256], F32)
        self.S16 = sb("S16", [128, 8, 16], F32)
        self.SEL = sb("SEL", [128, 8, 16], U32)
        self.pm = [sb("pm%d" % i, [128, 128], F32) for i in range(12)]
        self.pmi = [sb("pmi%d" % i, [128, 128], I32) for i in range(3)]
        self.UR = [carve("UR%d" % i, [128, D], F32) for i in range(2)]
        self.VR = self.UR
        S.op("sp", lambda e: e.dma_start(out=self.cst[:], in_=d["cst"]), w=[self.cst], dma=True)
        S.op("dve", lambda e: e.tensor_copy(out=self.identb[:], in_=self.c("ident")), r=[self.cst], w=[self.identb])

    def inproj(self, light):
        S = self.S
        xT = self.xT
        for (g, c0, n) in GROUPS:
            if light and g not in LIGHT_GROUPS:
                continue
            for k, ci in enumerate(self.cidx["in_" + g]):
                ch = self.chunks[ci]
                ncol = ch["nc"]
                wt = self.load_chunk(ci)
                pm = self.pmm_next()
                for kt in range(8):
                    S.op("pe", lambda e, pm=pm, wt=wt, kt=kt, ncol=ncol: e.matmul(
                        pm[:, 0:ncol], lhsT=xT[:, kt, :], rhs=wt[:, kt * ncol:(kt + 1) * ncol],
                        start=(kt == 0), stop=(kt == 7)), r=[xT, wt], w=[pm])
                sl = slice(k * 512, k * 512 + ncol)
                if g in ("rq", "aq"):
                    S.op("act", lambda e, pm=pm, sl=sl, ncol=ncol: e.copy(out=self.stA[:, sl], in_=pm[:, 0:ncol]), r=[pm], w=[self.stA])
                elif g in ("rk", "iq"):
                    S.op("act", lambda e, pm=pm, sl=sl, ncol=ncol: e.copy(out=self.stB[:, sl], in_=pm[:, 0:ncol]), r=[pm], w=[self.stB])
                elif g == "rv":
                    S.op("act", lambda e, pm=pm, k=k: e.copy(out=self.rv_bf[:, k, :], in_=pm[:, 0:512]), r=[pm], w=[self.rv_bf])
                elif g == "rg":
                    S.op("act", lambda e, pm=pm, sl=sl: e.activation(out=self.sgate[:, sl], in_=pm[:, 0:512], func=AF.Silu),
                         r=[pm], w=[(self.sgate, k)])
                elif g == "kv":
                    S.op("act", lambda e, pm=pm: e.copy(out=self.hkv[:], in_=pm[:, 0:512]), r=[pm], w=[self.hkv])
                elif g == "ikw":
                    S.op("act", lambda e, pm=pm: e.copy(out=self.hikw[:], in_=pm[:, 0:72]), r=[pm], w=[self.hikw])
                elif g == "ga":
                    S.op("act", lambda e, pm=pm, sl=sl: e.activation(out=self.sga[:, sl], in_=pm[:, 0:512], func=AF.Sigmoid),
                         r=[pm], w=[self.sga])
                elif g == "gb":
                    S.op("act", lambda e, pm=pm, sl=sl: e.activation(out=self.sgb[:, sl], in_=pm[:, 0:512], func=AF.Sigmoid),
                         r=[pm], w=[self.sgb])
            self.post_group(g, light)

    def post_group(self, g, light):
        S = self.S
        cs = self.cs
        cos256, sin256 = cs[:, 0:128], cs[:, 224:352]
        cos128, sin128 = cs[:, 128:192], cs[:, 352:416]
        cos64, sin64 = cs[:, 192:224], cs[:, 416:448]
        if g == "rq":
            self.rope(self.stA[:].rearrange("p (h f) -> p h f", h=4), self.B[0][:].rearrange("p (h f) -> p h f", h=4),
                      4, 128, cos256, sin256, [self.stA], [self.B[0]])
            srcs = [(self.B[0][:, i * 128:(i + 1) * 128], [self.B[0]]) for i in range(8)]
            self.transposes(srcs, self.rqT, 0)
        elif g == "rk":
            self.rope(self.stB[:].rearrange("p (h f) -> p h f", h=4), self.B[1][:].rearrange("p (h f) -> p h f", h=4),
                      4, 128, cos256, sin256, [self.stB], [self.B[1]])
            if not light:
                srcs = [(self.B[1][:, i * 128:(i + 1) * 128], [self.B[1]]) for i in range(8)]
                self.transposes(srcs, self.rkT, 1)
        elif g == "aq":
            self.rope(self.stA[:].rearrange("p (h f) -> p h f", h=8), self.B[2][:].rearrange("p (h f) -> p h f", h=8),
                      8, 64, cos128, sin128, [self.stA], [self.B[2]])
            srcs = [(self.B[2][:, i * 128:(i + 1) * 128], [self.B[2]]) for i in range(8)]
            self.transposes(srcs, self.aqT, 0)
        elif g == "kv":
            bi = self.cur_blk
            self.rope(self.hkv[:, 0:256].rearrange("p (h f) -> p h f", h=2), self.akf[:].rearrange("p (h f) -> p h f", h=2),
                      2, 64, cos128, sin128, [self.hkv], [self.akf])
            akb = self.B[3][:, 0:256]
            S.op("act", lambda e: e.copy(out=akb, in_=self.akf[:]), r=[self.akf], w=[(self.B[3], "ak")])
            srcs = [(self.B[3][:, i * 128:(i + 1) * 128], [(self.B[3], "ak")]) for i in range(2)]
            self.transposes_into(srcs, self.KT_dst(bi), self.KT_key(bi))
            vd = self.V_dst(bi)
            S.op("act", lambda e: e.copy(out=vd[:, :, 0:128], in_=self.hkv[:, 256:512].rearrange("p (g f) -> p g f", g=2)),
                 r=[self.hkv], w=[self.V_key(bi)])
            S.op("pool", lambda e: e.memset(vd[:, :, 128:129], 1.0), w=[(self.V_key(bi), "one")])
            if self.out_k is not None:
                ok_, ov_ = self.out_k, self.out_v
                S.op("pool", lambda e: e.dma_start(out=ok_, in_=self.akf[:]), r=[self.akf], dma=True)
                S.op("pool", lambda e: e.dma_start(out=ov_, in_=self.hkv[:, 256:512]), r=[self.hkv], dma=True)
        elif g == "iq":
            self.rope(self.stB[:, 0:512].rearrange("p (h f) -> p h f", h=8), self.stA[:, 0:512].rearrange("p (h f) -> p h f", h=8),
                      8, 32, cos64, sin64, [self.stB], [self.stA])
        elif g == "ikw":
            bi = self.cur_blk
            self.layer_norm(self.hikw[:, 0:64], self.F[0][:, 0:64], 64, self.c("ikg"), self.c("ikb"),
                            [self.hikw], [self.F[0]], [self.cst])
            self.rope(self.F[0][:, 0:64].rearrange("p (h f) -> p h f", h=1), self.ikf[:].rearrange("p (h f) -> p h f", h=1),
                      1, 32, cos64, sin64, [self.F[0]], [self.ikf])
            S.op("act", lambda e: e.copy(out=self.ikd[:, 0:64], in_=self.ikf[:]), r=[self.ikf], w=[(self.ikd, 0)])
            S.op("act", lambda e: e.copy(out=self.ikd[:, 64:128], in_=self.ikf[:]), r=[self.ikf], w=[(self.ikd, 1)])
            self.transposes_into([(self.ikd[:], [(self.ikd, 0), (self.ikd, 1)])], self.kidxT_dst(bi), self.kidxT_key(bi))
            if self.out_kidx is not None:
                oki_ = self.out_kidx
                S.op("pool", lambda e: e.dma_start(out=oki_, in_=self.ikf[:]), r=[self.ikf], dma=True)
            if not light:
                w8 = self.hikw[:, 64:72]
                wsm = self.wsm
                S.op("act", lambda e: e.activation(out=wsm[:, 0:8], in_=w8, func=AF.Abs, scale=8.0 ** -0.5),
                     r=[self.hikw], w=[(wsm, "aw")])
                S.op("act", lambda e: e.activation(out=wsm[:, 8:16], in_=w8, func=AF.Sign), r=[self.hikw], w=[(wsm, "sg")])
                iqb = self.B[3][:, 256:768]
                S.op("dve", lambda e: e.tensor_tensor(
                    out=iqb.rearrange("p (h f) -> p h f", h=8), in0=self.stA[:, 0:512].rearrange("p (h f) -> p h f", h=8),
                    in1=wsm[:, 0:8].unsqueeze(2).to_broadcast([128, 8, 64]), op=ALU.mult),
                    r=[self.stA, (wsm, "aw")], w=[(self.B[3], "iq")])
                srcs = [(self.B[3][:, 256 + i * 128:256 + (i + 1) * 128], [(self.B[3], "iq")]) for i in range(4)]
                self.transposes(srcs, self.iqT, 1)

    def transposes_into(self, srcs, dst3, dkey):
        S = self.S
        pt = self.ptr[self.ptr_i % 2]
        self.ptr_i += 1
        ptb = pt[:].bitcast(BF16)
        for i, (ap, rkeys) in enumerate(srcs):
            ccols = ap.shape[1]
            S.op("pe", lambda e, ap=ap, i=i, ccols=ccols: e.transpose(
                out=ptb[0:ccols, i * 128:(i + 1) * 128], in_=ap, identity=self.identb[:]),
                r=list(rkeys) + [self.identb], w=[pt])
        n = len(srcs)
        ccols = srcs[0][0].shape[1]
        sview = ptb[0:ccols, 0:n * 128].rearrange("p (a b) -> p a b", a=n)
        S.op("dve", lambda e: e.tensor_copy(out=dst3[0:ccols], in_=sview), r=[pt], w=[dkey])

    def KT_dst(self, bi):
        return self.KT[:, bi, :, :]

    def KT_key(self, bi):
        return (self.KT, bi)

    def V_dst(self, bi):
        return self.V[:, bi, :, :]

    def V_key(self, bi):
        return (self.V, bi)

    def kidxT_dst(self, bi):
        return self.kidxT[:, bi * 128:(bi + 1) * 128].unsqueeze(1)

    def kidxT_key(self, bi):
        return (self.kidxT, bi)

    def block_front(self, xsrc, csi):
        S, d = self.S, self.d
        self.xsrc = xsrc
        S.op("sp", lambda e: e.dma_start(out=self.xblk[:], in_=xsrc), w=[self.xblk], dma=True)
        S.op("sp", lambda e: e.dma_start(out=self.cs[:], in_=d["csb"][csi]), w=[self.cs], dma=True)
        xb16 = self.B[2]
        S.op("act", lambda e: e.copy(out=xb16[:], in_=self.xblk[:]), r=[self.xblk], w=[xb16])
        srcs = [(xb16[:, i * 128:(i + 1) * 128], [xb16]) for i in range(8)]
        self.transposes(srcs, self.xT, 0)

    def retention(self, owned, decayT, rowdec, kdec, gC, state_f32, skey, sample_b=None):
        S = self.S
        rqT, rkT, rv_bf = self.rqT, self.rkT, self.rv_bf
        rk_r = self.B[1]
        if owned:
            for h in range(4):
                pSx = self.pS[self.pS_i % 2]
                self.pS_i += 1
                for kt in range(2):
                    S.op("pe", lambda e, pSx=pSx, h=h, kt=kt: e.matmul(
                        pSx[:, 0:128], lhsT=rkT[:, 2 * h + kt, :], rhs=rqT[:, 2 * h + kt, :], start=(kt == 0), stop=(kt == 1)),
                        r=[rkT, rqT], w=[pSx])
                PTx = self.PT[h % 2]
                S.op("dve", lambda e, pSx=pSx, PTx=PTx, h=h: e.tensor_tensor(
                    out=PTx[:], in0=pSx[:, 0:128], in1=decayT[:, h * 128:(h + 1) * 128], op=ALU.mult),
                    r=[pSx, self.cst], w=[PTx])
                S.op("dve", lambda e, h=h: e.tensor_tensor(
                    out=self.qd[:], in0=rqT[:, 2 * h:2 * h + 2, :],
                    in1=rowdec[:, h * 128:(h + 1) * 128].unsqueeze(1).to_broadcast([128, 2, 128]), op=ALU.mult),
                    r=[rqT, self.cst], w=[self.qd])
                S.op("act", lambda e, h=h: e.copy(out=self.sbf[:], in_=state_f32[:, h, :, :]), r=[(skey, h)], w=[self.sbf])
                pm = self.pmm_next()
                S.op("pe", lambda e, pm=pm, PTx=PTx, h=h: e.matmul(pm[:], lhsT=PTx[:], rhs=rv_bf[:, h, :], start=True, stop=False),
                     r=[PTx, rv_bf], w=[pm])
                for kt in range(2):
                    S.op("pe", lambda e, pm=pm, kt=kt: e.matmul(pm[:], lhsT=self.qd[:, kt, :], rhs=self.sbf[:, kt, :],
                                                                start=False, stop=(kt == 1)),
                         r=[self.qd, self.sbf], w=[pm])
                yh = self.F[h % 2]
                g = self.GI[:, h * 512:(h + 1) * 512]
                b = self.GI[:, 2048 + h * 512:2048 + (h + 1) * 512]
                S.op("act", lambda e, pm=pm, yh=yh: e.copy(out=yh[:], in_=pm[:]), r=[pm], w=[yh])
                self.layer_norm(yh[:], yh[:], 512, g, b, [yh], [yh], [self.GI])
                S.op("dve", lambda e, yh=yh, h=h: e.tensor_tensor(
                    out=self.ret_o[:, h * 512:(h + 1) * 512], in0=yh[:], in1=self.sgate[:, h * 512:(h + 1) * 512], op=ALU.mult),
                    r=[yh, (self.sgate, h)], w=[(self.sgate, h)])
        for h in range(4):
            S.op("dve", lambda e, h=h: e.tensor_scalar(
                out=self.kd[:], in0=rk_r[:, h * 256:(h + 1) * 256], scalar1=kdec[:, h:h + 1], scalar2=None, op0=ALU.mult),
                r=[rk_r, self.cst], w=[self.kd])
            for kt in range(2):
                pm = self.pmm_next()
                S.op("pe", lambda e, pm=pm, h=h, kt=kt: e.matmul(
                    pm[:], lhsT=self.kd[:, kt * 128:(kt + 1) * 128], rhs=rv_bf[:, h, :], start=True, stop=True),
                    r=[self.kd, rv_bf], w=[pm])
                S.op("dve", lambda e, pm=pm, h=h, kt=kt: e.scalar_tensor_tensor(
                    out=state_f32[:, h, kt, :], in0=state_f32[:, h, kt, :], scalar=float(gC[h]), in1=pm[:],
                    op0=ALU.mult, op1=ALU.add), r=[pm, (skey, h)], w=[(skey, h)])

    def attention(self, bi):
        S, cfg = self.S, self.cfg
        nk = bi + 1
        N = nk * 128
        GI, wsm, bis = self.GI, self.wsm, self.bis
        nkc = (nk + 3) // 4
        for c in range(nkc):
            ncols = min(512, N - c * 512)
            kb0 = c * 4
            kkeys = [self.kidxT_key(kb) for kb in range(kb0, min(nk, kb0 + 4))]
            for hd in range(8):
                hp, s = hd // 2, hd % 2
                pSx = self.pS[self.pS_i % 2]
                self.pS_i += 1
                S.op("pe", lambda e, pSx=pSx, hp=hp, s=s, c=c, ncols=ncols: e.matmul(
                    pSx[:, 0:ncols], lhsT=self.iqT[64 * s:64 * s + 64, hp, :],
                    rhs=self.kidxT[64 * s:64 * s + 64, c * 512:c * 512 + ncols], start=True, stop=True),
                    r=[self.iqT] + kkeys, w=[pSx])
                Fx = self.F[hd % 2]
                S.op("act", lambda e, pSx=pSx, Fx=Fx, ncols=ncols: e.activation(out=Fx[:, 0:ncols], in_=pSx[:, 0:ncols], func=AF.Relu),
                     r=[pSx], w=[Fx])
                gsl = GI[:, c * 512:c * 512 + ncols]
                if hd == 0:
                    S.op("dve", lambda e, Fx=Fx, gsl=gsl, ncols=ncols: e.tensor_scalar(
                        out=gsl, in0=Fx[:, 0:ncols], scalar1=wsm[:, 8:9], scalar2=None, op0=ALU.mult),
                        r=[Fx, (wsm, "sg")], w=[GI])
                else:
                    S.op("dve", lambda e, Fx=Fx, gsl=gsl, ncols=ncols, hd=hd: e.scalar_tensor_tensor(
                        out=gsl, in0=Fx[:, 0:ncols], scalar=wsm[:, 8 + hd:9 + hd], in1=gsl, op0=ALU.mult, op1=ALU.add),
                        r=[Fx, (wsm, "sg"), GI], w=[GI])
        S.op("dve", lambda e: e.tensor_reduce(out=bis[:, 0:1], in_=GI[:, 0:N], axis=AX.X, op=ALU.max, apply_absolute_value=True),
             r=[GI], w=[bis])
        S.op("dve", lambda e: e.tensor_scalar(out=bis[:, 1:2], in0=bis[:, 0:1], scalar1=1.0, scalar2=None, op0=ALU.add),
             r=[bis], w=[bis])
        S.op("dve", lambda e: e.tensor_scalar(out=bis[:, 2:3], in0=bis[:, 1:2], scalar1=-1.0, scalar2=None, op0=ALU.mult),
             r=[bis], w=[bis])
        S.op("dve", lambda e: e.memset(bis[:, 3:4], 0.0), w=[bis])
        if cfg.NL > 0:
            S.op("dve", lambda e: e.tensor_scalar(out=GI[:, 0:cfg.NL * 128], in0=GI[:, 0:cfg.NL * 128],
                                                  scalar1=self.c("lightbias"), scalar2=None, op0=ALU.add),
                 r=[GI, self.cst], w=[GI])
        S.op("dve", lambda e: e.tensor_tensor(out=GI[:, bi * 128:N], in0=GI[:, bi * 128:N], in1=self.c("causal"), op=ALU.add),
             r=[GI, self.cst], w=[GI])
        nch = (N + 1023) // 1024
        thr = float(N - 2 * cfg.TOPK)
        for it in range(NIT):
            for ch in range(nch):
                n = min(1024, N - ch * 1024)
                S.op("act", lambda e, ch=ch, n=n: e.activation(
                    out=self.junka[:, 0:n], in_=GI[:, ch * 1024:ch * 1024 + n], func=AF.Sign, scale=-1.0, bias=bis[:, 3:4],
                    accum_out=self.cnt4[:, ch:ch + 1]), r=[GI, bis], w=[(self.cnt4, ch), self.junka])
            ckeys = [(self.cnt4, ch) for ch in range(nch)]
            if nch > 1:
                S.op("dve", lambda e: e.reduce_sum(out=bis[:, 4:5], in_=self.cnt4[:, 0:nch], axis=AX.X), r=ckeys, w=[bis])
                ssrc = bis[:, 4:5]
            else:
                ssrc = self.cnt4[:, 0:1]
            S.op("dve", lambda e, ssrc=ssrc: e.tensor_scalar(out=bis[:, 5:6], in0=ssrc, scalar1=thr, scalar2=None, op0=ALU.is_le),
                 r=ckeys + [bis], w=[bis])
            S.op("dve", lambda e: e.tensor_tensor(out=bis[:, 6:7], in0=bis[:, 3:4], in1=bis[:, 2:3], op=ALU.subtract), r=[bis], w=[bis])
            S.op("dve", lambda e: e.tensor_tensor(out=bis[:, 7:8], in0=bis[:, 1:2], in1=bis[:, 3:4], op=ALU.subtract), r=[bis], w=[bis])
            S.op("dve", lambda e: e.scalar_tensor_tensor(out=bis[:, 2:3], in0=bis[:, 6:7], scalar=bis[:, 5:6], in1=bis[:, 2:3],
                                                         op0=ALU.mult, op1=ALU.add), r=[bis], w=[bis])
            S.op("dve", lambda e: e.scalar_tensor_tensor(out=bis[:, 1:2], in0=bis[:, 7:8], scalar=bis[:, 5:6], in1=bis[:, 3:4],
                                                         op0=ALU.mult, op1=ALU.add), r=[bis], w=[bis])
            S.op("dve", lambda e: e.tensor_scalar(out=bis[:, 3:4], in0=bis[:, 2:3], scalar1=bis[:, 1:2], scalar2=0.5,
                                                  op0=ALU.add, op1=ALU.mult), r=[bis], w=[bis])
        for c in range(nkc):
            ncols = min(512, N - c * 512)
            nb = ncols // 128
            S.op("dve", lambda e, c=c, ncols=ncols: e.tensor_scalar(
                out=self.mch[:, 0:ncols], in0=GI[:, c * 512:c * 512 + ncols], scalar1=bis[:, 2:3], scalar2=None, op0=ALU.is_ge),
                r=[GI, bis], w=[self.mch])
            srcs = [(self.mch[:, i * 128:(i + 1) * 128], [self.mch]) for i in range(nb)]
            self.transposes_into(srcs, self.mT[:, c * 4:c * 4 + nb, :], (self.mT, c))
        for g in range(2):
            for kb in range(nk):
                pSx = self.pS[self.pS_i % 2]
                self.pS_i += 1
                S.op("pe", lambda e, pSx=pSx, kb=kb, g=g: e.matmul(
                    pSx[:], lhsT=self.KT[:, kb, g, :], rhs=self.aqT[:, 4 * g:4 * g + 4, :], start=True, stop=True),
                    r=[self.KT_key(kb), self.aqT], w=[pSx])
                Ex = self.Ebuf[kb % 2]
                S.op("act", lambda e, pSx=pSx, Ex=Ex: e.activation(out=Ex[:], in_=pSx[:], func=AF.Exp, scale=128.0 ** -0.5),
                     r=[pSx], w=[Ex])
                Px = self.Pbuf[kb % 2]
                S.op("dve", lambda e, Ex=Ex, Px=Px, kb=kb: e.tensor_tensor(
                    out=Px[:], in0=Ex[:].rearrange("p (r t) -> p r t", r=4),
                    in1=self.mT[:, kb, :].unsqueeze(1).to_broadcast([128, 4, 128]), op=ALU.mult),
                    r=[Ex, (self.mT, kb // 4)], w=[Px])
                for r in range(4):
                    pOx = self.pO[r // 2]
                    S.op("pe", lambda e, pOx=pOx, Px=Px, r=r, kb=kb, g=g: e.matmul(
                        pOx[:, (r % 2) * 129:(r % 2) * 129 + 129], lhsT=Px[:, r, :], rhs=self.V[:, kb, g, :],
                        start=(kb == 0 and r % 2 == 0), stop=(kb == nk - 1), skip_group_check=True),
                        r=[Px, self.V_key(kb), (self.V_key(kb), "one")], w=[pOx])
            for r in range(4):
                pOx = self.pO[r // 2]
                o0 = (r % 2) * 129
                hd = 4 * g + r
                S.op("dve", lambda e, pOx=pOx, o0=o0, hd=hd: e.reciprocal(out=self.rsum[:, hd:hd + 1], in_=pOx[:, o0 + 128:o0 + 129]),
                     r=[pOx], w=[(self.rsum, hd)])
                S.op("dve", lambda e, pOx=pOx, o0=o0, hd=hd: e.tensor_scalar(
                    out=self.att_o[:, hd, :], in0=pOx[:, o0:o0 + 128], scalar1=self.rsum[:, hd:hd + 1], scalar2=None, op0=ALU.mult),
                    r=[pOx, (self.rsum, hd)], w=[(self.att_o, hd)])

    def mm_chunk(self, name, k, lhsT3, lkeys, nkt=8, first=True, last=True, pm=None, rhs_off=0, rhs_w=512):
        S = self.S
        ci = self.cidx[name][k]
        ncol = self.chunks[ci]["nc"]
        wt = self.load_chunk(ci)
        if pm is None:
            pm = self.pmm_next()
        for kt in range(nkt):
            S.op("pe", lambda e, pm=pm, wt=wt, kt=kt: e.matmul(
                pm[:, 0:rhs_w], lhsT=lhsT3[:, kt, :], rhs=wt[:, kt * ncol + rhs_off:kt * ncol + rhs_off + rhs_w],
                start=(first and kt == 0), stop=(last and kt == nkt - 1)), r=lkeys + [wt], w=[pm])
        return pm

    def load_params(self, col0, n, dst):
        S, d = self.S, self.d
        S.op("sp", lambda e: e.dma_start(out=dst, in_=d["params"][:, col0:col0 + n]), w=[self.GI], dma=True)

    def finish(self, psrc, ydst):
        S, d = self.S, self.d
        GI = self.GI
        srcs = [(self.ret_o[:, i * 128:(i + 1) * 128], [(self.sgate, i // 4)]) for i in range(16)]
        self.transposes(srcs, self.TT, 0)
        srcs = [(self.att_o[:, i, :], [(self.att_o, i)]) for i in range(8)]
        self.transposes(srcs, self.TT2, 1)
        branch = self.B[0]
        for chh in range(2):
            sl = slice(chh * 512, (chh + 1) * 512)
            pm = self.mm_chunk("ret_o%d" % chh, 0, self.TT[:, 0:8, :], [self.TT], first=True, last=False)
            self.mm_chunk("ret_o%d" % chh, 1, self.TT[:, 8:16, :], [self.TT], first=False, last=True, pm=pm)
            S.op("dve", lambda e, pm=pm, sl=sl: e.tensor_tensor(out=self.F[0][:], in0=pm[:], in1=self.sga[:, sl], op=ALU.mult),
                 r=[pm, self.sga], w=[self.F[0]])
            pm2 = self.mm_chunk("att_o", chh, self.TT2, [self.TT2])
            S.op("dve", lambda e, pm2=pm2, sl=sl: e.tensor_tensor(out=self.F[1][:], in0=pm2[:], in1=self.sgb[:, sl], op=ALU.mult),
                 r=[pm2, self.sgb], w=[self.F[1]])
            S.op("dve", lambda e, sl=sl: e.tensor_tensor(out=branch[:, sl], in0=self.F[0][:], in1=self.F[1][:], op=ALU.add),
                 r=[self.F[0], self.F[1]], w=[branch])
        srcs = [(branch[:, i * 128:(i + 1) * 128], [branch]) for i in range(8)]
        self.transposes(srcs, self.TT2, 0)
        self.load_params(4096, 2048, GI[:, 0:2048])
        X1 = self.X1
        xsrc = self.xsrc
        S.op("sp", lambda e: e.dma_start(out=self.xblk[:], in_=xsrc), w=[self.xblk], dma=True)
        for chh in range(2):
            sl = slice(chh * 512, (chh + 1) * 512)
            pm = self.mm_chunk("out", chh, self.TT2, [self.TT2])
            S.op("dve", lambda e, pm=pm, sl=sl: e.scalar_tensor_tensor(
                out=X1[:, sl], in0=self.xblk[:, sl], scalar=ALPHA, in1=pm[:], op0=ALU.mult, op1=ALU.add),
                r=[pm, self.xblk], w=[X1])
        self.layer_norm(X1[:], X1[:], D, GI[:, 0:1024], GI[:, 1024:2048], [X1], [X1], [GI])
        x1b = self.B[2]
        S.op("act", lambda e: e.copy(out=x1b[:], in_=X1[:]), r=[X1], w=[x1b])
        srcs = [(x1b[:, i * 128:(i + 1) * 128], [x1b]) for i in range(8)]
        x1T = self.TT[:, 0:8, :]
        self.transposes(srcs, x1T, 1, dkey=self.TT)
        self.x1T = x1T
        S.op("sp", lambda e: e.dma_start(out=self.pblk[:, 0:256], in_=psrc), w=[self.pblk], dma=True)
        pb16 = self.B[0][:, 0:256]
        S.op("act", lambda e: e.copy(out=pb16, in_=self.pblk[:, 0:256]), r=[self.pblk], w=[self.B[0]])
        srcs = [(self.B[0][:, i * 128:(i + 1) * 128], [self.B[0]]) for i in range(2)]
        self.transposes(srcs, self.ppT, 0)
        Y = self.Y
        ci_ple = self.cidx["ple"][0]
        for chh in range(2):
            sl = slice(chh * 512, (chh + 1) * 512)
            pm = self.mm_chunk("pg", chh, x1T, [self.TT])
            S.op("act", lambda e, pm=pm: e.activation(out=self.F[0][:], in_=pm[:], func=AF.Sigmoid), r=[pm], w=[self.F[0]])
            pm2 = self.mm_chunk("ple", 0, self.ppT, [self.ppT], nkt=2, rhs_off=chh * 512)
            S.op("dve", lambda e, pm2=pm2: e.tensor_tensor(out=self.F[1][:], in0=pm2[:], in1=self.F[0][:], op=ALU.mult),
                 r=[pm2, self.F[0]], w=[self.F[1]])
            S.op("dve", lambda e, sl=sl: e.scalar_tensor_tensor(out=Y[:, sl], in0=X1[:, sl], scalar=ALPHA, in1=self.F[1][:],
                                                                op0=ALU.mult, op1=ALU.add),
                 r=[X1, self.F[1]], w=[Y])
        if DBG_STOP != "nopeer":
            self.peer()
        self.load_params(6144, 2048, GI[:, 0:2048])
        self.layer_norm(Y[:], Y[:], D, GI[:, 0:1024], GI[:, 1024:2048], [Y], [Y], [GI])
        S.op("pool", lambda e: e.dma_start(out=ydst, in_=Y[:]), r=[Y], dma=True)

    def peer(self):
        S, d = self.S, self.d
        GI, x1T, X1, Y = self.GI, self.x1T, self.X1, self.Y
        qT = self.qT
        for c4 in range(4):
            ci = self.cidx["wq"][c4]
            wt = self.load_chunk(ci)
            pm = self.pmm_next()
            for ct in range(4):
                for kt in range(8):
                    S.op("pe", lambda e, pm=pm, wt=wt, ct=ct, kt=kt: e.matmul(
                        pm[:, ct * 128:(ct + 1) * 128], lhsT=wt[:, kt * 512 + ct * 128:kt * 512 + (ct + 1) * 128],
                        rhs=x1T[:, kt, :], start=(kt == 0), stop=(kt == 7), skip_group_check=True), r=[wt, self.TT], w=[pm])
            S.op("act", lambda e, pm=pm, c4=c4: e.copy(out=qT[:, c4 * 4:(c4 + 1) * 4, :], in_=pm[:].rearrange("p (a b) -> p a b", a=4)),
                 r=[pm], w=[qT])
        wt = self.load_chunk(self.cidx["skT"][0])
        SC = GI[:, 0:2048]
        for j4 in range(4):
            pm = self.pmm_next()
            for jj in range(4):
                j = j4 * 4 + jj
                S.op("pe", lambda e, pm=pm, wt=wt, j=j, jj=jj: e.matmul(
                    pm[:, jj * 128:(jj + 1) * 128], lhsT=qT[:, j, :], rhs=wt[:, j * 128:(j + 1) * 128], start=True, stop=True,
                    skip_group_check=True), r=[qT, wt], w=[pm])
            S.op("act", lambda e, pm=pm, j4=j4: e.copy(out=GI[:, j4 * 512:(j4 + 1) * 512], in_=pm[:]), r=[pm], w=[GI])
        V16, I16 = self.V16, self.I16
        TMPS = self.pm[11]
        for j in range(16):
            scj = GI[:, j * 128:(j + 1) * 128]
            S.op("dve", lambda e, j=j, scj=scj: e.max(out=V16[:, j, 0:8], in_=scj), r=[GI], w=[(V16, j)])
            S.op("dve", lambda e, j=j, scj=scj: e.max_index(out=I16[:, j, 0:8], in_max=V16[:, j, 0:8], in_values=scj),
                 r=[GI, (V16, j)], w=[(I16, j)])
            S.op("dve", lambda e, j=j, scj=scj: e.match_replace(out=TMPS[:], in_to_replace=V16[:, j, 0:8], in_values=scj, imm_value=NEG),
                 r=[GI, (V16, j)], w=[TMPS])
            S.op("dve", lambda e, j=j: e.max(out=V16[:, j, 8:16], in_=TMPS[:]), r=[TMPS], w=[(V16, j)])
            S.op("dve", lambda e, j=j: e.max_index(out=I16[:, j, 8:16], in_max=V16[:, j, 8:16], in_values=TMPS[:]),
                 r=[TMPS, (V16, j)], w=[(I16, j)])
        vkeys = [(V16, j) for j in range(16)]
        ikeys = [(I16, j) for j in range(16)]
        V4 = V16[:].rearrange("p (h s) k -> p h s k", s=2)
        CAND = GI[:, 2048:4096]
        S.op("dve", lambda e: e.tensor_tensor(
            out=CAND.rearrange("p (h a b) -> p h a b", h=8, a=16),
            in0=V4[:, :, 0, :].unsqueeze(3).to_broadcast([128, 8, 16, 16]),
            in1=V4[:, :, 1, :].unsqueeze(2).to_broadcast([128, 8, 16, 16]), op=ALU.add), r=vkeys + [GI], w=[GI])
        S16, SEL, TMPC = self.S16, self.SEL, self.TMPC
        for h in range(8):
            ch = GI[:, 2048 + h * 256:2048 + (h + 1) * 256]
            S.op("dve", lambda e, h=h, ch=ch: e.max(out=S16[:, h, 0:8], in_=ch), r=[GI], w=[(S16, h)])
            S.op("dve", lambda e, h=h, ch=ch: e.max_index(out=SEL[:, h, 0:8], in_max=S16[:, h, 0:8], in_values=ch),
                 r=[GI, (S16, h)], w=[(SEL, h)])
            S.op("dve", lambda e, h=h, ch=ch: e.match_replace(out=TMPC[:], in_to_replace=S16[:, h, 0:8], in_values=ch, imm_value=NEG),
                 r=[GI, (S16, h)], w=[TMPC])
            S.op("dve", lambda e, h=h: e.max(out=S16[:, h, 8:16], in_=TMPC[:]), r=[TMPC], w=[(S16, h)])
            S.op("dve", lambda e, h=h: e.max_index(out=SEL[:, h, 8:16], in_max=S16[:, h, 8:16], in_values=TMPC[:]),
                 r=[TMPC, (S16, h)], w=[(SEL, h)])
        skeys = [(S16, h) for h in range(8)]
        selkeys = [(SEL, h) for h in range(8)]
        pm_ = self.pm
        E16, Z, RZ, G16 = pm_[0], self.wsm[:, 16:24], self.wsm[:, 24:32], pm_[1]
        E3 = E16[:].rearrange("p (h k) -> p h k", h=8)
        S.op("dve", lambda e: e.tensor_tensor(out=E3, in0=S16[:], in1=S16[:, :, 0:1].to_broadcast([128, 8, 16]), op=ALU.subtract),
             r=skeys, w=[E16])
        S.op("act", lambda e: e.activation(out=E16[:], in_=E16[:], func=AF.Exp), r=[E16], w=[E16])
        S.op("dve", lambda e: e.reduce_sum(out=Z, in_=E3, axis=AX.X), r=[E16], w=[(self.wsm, "z")])
        S.op("dve", lambda e: e.reciprocal(out=RZ, in_=Z), r=[(self.wsm, "z")], w=[(self.wsm, "rz")])
        S.op("dve", lambda e: e.tensor_tensor(out=G16[:].rearrange("p (h k) -> p h k", h=8), in0=E3,
                                              in1=RZ.unsqueeze(2).to_broadcast([128, 8, 16]), op=ALU.mult),
             r=[E16, (self.wsm, "rz")], w=[G16])
        SELi = SEL[:].rearrange("p h k -> p (h k)").bitcast(I32)
        Ai, Bi_, Ei = self.pmi
        Af, Bf, I1f, I2f, I1s, I2s = pm_[2], pm_[3], pm_[4], pm_[5], pm_[6], pm_[7]
        S.op("dve", lambda e: e.tensor_single_scalar(out=Ai[:], in_=SELi, scalar=4, op=ALU.logical_shift_right), r=selkeys, w=[Ai])
        S.op("dve", lambda e: e.tensor_single_scalar(out=Bi_[:], in_=SELi, scalar=15, op=ALU.bitwise_and), r=selkeys, w=[Bi_])
        S.op("dve", lambda e: e.tensor_copy(out=Af[:], in_=Ai[:]), r=[Ai], w=[Af])
        S.op("dve", lambda e: e.tensor_copy(out=Bf[:], in_=Bi_[:]), r=[Bi_], w=[Bf])
        I4 = I16[:].rearrange("p j k -> p (j k)").bitcast(I32).rearrange("p (h s k) -> p h s k", h=8, s=2)
        S.op("dve", lambda e: e.tensor_copy(out=I1f[:].rearrange("p (h k) -> p h k", h=8), in_=I4[:, :, 0, :]),
             r=ikeys, w=[I1f])
        S.op("dve", lambda e: e.tensor_copy(out=I2f[:].rearrange("p (h k) -> p h k", h=8), in_=I4[:, :, 1, :]),
             r=ikeys, w=[I2f])
        OH = GI[:, 0:2048].rearrange("p (h j a) -> p h j a", h=8, j=16)
        PR = GI[:, 2048:4096].rearrange("p (h j a) -> p h j a", h=8, j=16)
        iot = self.c("iota16").unsqueeze(1).unsqueeze(1).to_broadcast([128, 8, 16, 16])
        for (Xf, If, Is) in ((Af, I1f, I1s), (Bf, I2f, I2s)):
            S.op("dve", lambda e, Xf=Xf: e.tensor_tensor(
                out=OH, in0=Xf[:].rearrange("p (h j) -> p h j", h=8).unsqueeze(3).to_broadcast([128, 8, 16, 16]), in1=iot,
                op=ALU.is_equal), r=[Xf, self.cst, GI], w=[GI])
            S.op("dve", lambda e, If=If: e.tensor_tensor(
                out=PR, in0=OH, in1=If[:].rearrange("p (h a) -> p h a", h=8).unsqueeze(2).to_broadcast([128, 8, 16, 16]),
                op=ALU.mult), r=[If, GI], w=[GI])
            S.op("dve", lambda e, Is=Is: e.reduce_sum(out=Is[:].rearrange("p (h j) -> p h j", h=8), in_=PR, axis=AX.X),
                 r=[GI], w=[Is])
        S.op("dve", lambda e: e.scalar_tensor_tensor(out=I1s[:], in0=I1s[:], scalar=128.0, in1=I2s[:], op0=ALU.mult, op1=ALU.add),
             r=[I1s, I2s], w=[I1s])
        S.op("dve", lambda e: e.tensor_copy(out=Ei[:], in_=I1s[:]), r=[I1s], w=[Ei])
        ACTP = pm_[8]
        for j in range(128):
            ur = self.UR[j % 2]
            S.op("pool", lambda e, ur=ur, j=j: e.indirect_dma_start(
                out=ur[:], out_offset=None, in_=d["peer_u"], in_offset=bass.IndirectOffsetOnAxis(ap=Ei[:, j:j + 1], axis=0)),
                r=[Ei], w=[ur], dma=True, dsem="ur%d" % (j % 2))
            S.op("dve", lambda e, ur=ur, j=j: e.scalar_tensor_tensor(
                out=self.junkd[:], in0=ur[:], scalar=1.0, in1=X1[:], op0=ALU.mult, op1=ALU.mult,
                accum_out=ACTP[:, j:j + 1]), r=[ur, X1], w=[(ACTP, j), self.junkd])
        akeys = [(ACTP, j) for j in range(128)]
        COEF = pm_[10]
        S.op("act", lambda e: e.activation(out=pm_[9][:], in_=ACTP[:], func=AF.Gelu), r=akeys, w=[pm_[9]])
        S.op("dve", lambda e: e.tensor_tensor(out=COEF[:], in0=pm_[9][:], in1=G16[:], op=ALU.mult), r=[pm_[9], G16], w=[COEF])
        for j in range(128):
            vr = self.VR[j % 2]
            S.op("pool", lambda e, vr=vr, j=j: e.indirect_dma_start(
                out=vr[:], out_offset=None, in_=d["peer_v"], in_offset=bass.IndirectOffsetOnAxis(ap=Ei[:, j:j + 1], axis=0)),
                r=[Ei], w=[vr], dma=True, dsem="vr%d" % (j % 2))
            S.op("dve", lambda e, vr=vr, j=j: e.scalar_tensor_tensor(
                out=Y[:], in0=vr[:], scalar=COEF[:, j:j + 1], in1=Y[:], op0=ALU.mult, op1=ALU.add), r=[vr, COEF, Y], w=[Y])

    def phase_prompt(self):
        S, d, cfg = self.S, self.d, self.cfg
        with ExitStack() as st2:
            self.KT = self.sb("KT", [128, cfg.NBLK, 2, 128], BF16, st2)
            self.V = self.sb("V", [128, cfg.NBLK, 2, 129], BF16, st2)
            self.kidxT = self.sb("kidxT", [128, cfg.NBLK * 128], BF16, st2)
            self.mT = self.sb("mT", [128, cfg.NBLK, 128], BF16, st2)
            state = self.sb("state", [128, 4, 2, 512], F32, st2)
            for h in range(4):
                S.op("pool", lambda e, h=h: e.memset(state[:, h, :, :], 0.0), w=[(state, h)])
            lg = np.log1p(-np.exp2(-5.0 - np.arange(4, dtype=np.float32))).astype(np.float32)
            gC = np.exp(lg * 128.0)
            for bi in range(cfg.NBLK):
                light = bi < cfg.NL
                self.cur_blk = bi
                oi = bi - cfg.NL
                if light:
                    self.out_k = self.out_v = self.out_kidx = None
                else:
                    self.out_k, self.out_v, self.out_kidx = d["kout"][oi], d["vout"][oi], d["kidxout"][oi]
                self.block_front(d["xb"][bi], bi)
                if DBG_STOP == "front":
                    continue
                if not light:
                    self.load_params(0, 4096, self.GI[:, 0:4096])
                self.inproj(light)
                if DBG_STOP == "inproj":
                    continue
                self.retention(not light, self.c("decayT"), self.c("rowdec"), self.c("kdec"), gC, state, state)
                if DBG_STOP == "ret":
                    continue
                if not light:
                    self.attention(bi)
                    if DBG_STOP == "att":
                        continue
                    self.finish(d["pb"][oi], d["y"][oi])
            for h in range(4):
                S.op("pool", lambda e, h=h: e.dma_start(
                    out=d["stout"][h].rearrange("(kt p) v -> p kt v", p=128), in_=state[:, h, :, :]),
                    r=[(state, h)], dma=True)
            S.emit()

    def c2(self, name, lo=0, n=None):
        o, sz = CS2[name]
        n = sz - lo if n is None else n
        return self.cst2[:, o + lo:o + lo + n]

    def phase_sample(self):
        S, d, cfg = self.S, self.d, self.cfg
        NP, PAST, K = cfg.NPAGE, cfg.PAST, cfg.TOPK_S
        NR = (K + 7) // 8
        NSL = NR * 8
        lgNP = int(round(np.log2(NP)))
        assert (1 << lgNP) == NP and NSL <= 256
        lg = np.log1p(-np.exp2(-5.0 - np.arange(4, dtype=np.float32))).astype(np.float32)
        gC4 = np.exp(lg * 4.0)
        pm_ = self.pm
        with ExitStack() as st2:
            sb2 = lambda n, s, dt: self.sb(n, s, dt, st2)
            self.KT = sb2("KTs", [128, 1, 2, 128], BF16)
            self.V = sb2("Vs", [128, 1, 2, 129], BF16)
            self.kidxT = sb2("kidxTs", [128, 128], BF16)
            self.mT = sb2("mTs", [128, 1, 128], BF16)
            self.cst2 = sb2("cst2", [128, NCS2], F32)
            QDb = sb2("QDb", [128, 32, 32], BF16)
            Wpad = sb2("Wpad", [32, 32, 128], BF16)
            AQS = sb2("AQS", [128, 1024], BF16)
            VALS = sb2("VALS", [128, NSL], F32)
            IDX = sb2("IDX", [128, NSL], U32)
            INEW = sb2("INEW", [128, 8], F32)
            ptc = sb2("ptc", [128, 32], I32)
            S.op("sp", lambda e: e.dma_start(out=self.cst2[:], in_=d["cst2"]), w=[self.cst2], dma=True)
            S.op("dve", lambda e: e.memset(ptc[:], 0), w=[ptc])
            S.op("sp", lambda e: e.dma_start(out=ptc[0:NP, :], in_=d["ptcol"]), w=[ptc], dma=True)
            self.cur_blk = 0
            self.out_k, self.out_v, self.out_kidx = d["ks"], d["vs"], d["kidxs"]
            self.block_front(d["xs"], cfg.NBLK)
            self.load_params(0, 4096, self.GI[:, 0:4096])
            self.inproj(False)
            S.op("act", lambda e: e.copy(out=AQS[:], in_=self.B[2][:]), r=[self.B[2]], w=[AQS])
            iqd = self.B[0]
            iqs = self.B[3][:, 256:768].rearrange("p (h f) -> p h f", h=8)
            iqd4 = iqd[:].rearrange("p (h s f) -> p h s f", h=8, s=2)
            for s_ in range(2):
                S.op("dve", lambda e, s_=s_: e.tensor_copy(out=iqd4[:, :, s_, :], in_=iqs), r=[(self.B[3], "iq")], w=[iqd])
            for g0 in range(0, 8, 4):
                pt = self.ptr[self.ptr_i % 2]
                self.ptr_i += 1
                ptb = pt[:].bitcast(BF16)
                for i in range(4):
                    h = g0 + i
                    S.op("pe", lambda e, h=h, i=i, ptb=ptb: e.transpose(out=ptb[:, i * 128:(i + 1) * 128], in_=iqd[:, h * 128:(h + 1) * 128],
                                                                        identity=self.identb[:]), r=[iqd, self.identb], w=[pt])
                S.op("dve", lambda e, g0=g0, ptb=ptb: e.tensor_copy(
                    out=QDb[:].rearrange("p b (h t) -> p h b t", h=8)[:, g0:g0 + 4, :, :],
                    in_=ptb[:, 0:512].rearrange("p (h b t) -> p h b t", h=4, b=32)), r=[pt], w=[QDb])
            EW = self.B[0]
            S.op("dve", lambda e: e.tensor_tensor(
                out=EW[:, 0:32].rearrange("p (h t) -> p h t", h=8),
                in0=self.wsm[:, 8:16].unsqueeze(2).to_broadcast([128, 8, 4]),
                in1=self.c2("tmask").unsqueeze(1).to_broadcast([128, 8, 4]), op=ALU.mult),
                r=[(self.wsm, "sg"), self.cst2, iqd], w=[EW])
            pt = self.ptr[self.ptr_i % 2]
            self.ptr_i += 1
            ptb = pt[:].bitcast(BF16)
            S.op("pe", lambda e, ptb=ptb: e.transpose(out=ptb[0:32, 0:128], in_=EW[:, 0:32], identity=self.identb[:]),
                 r=[EW, self.identb], w=[pt])
            S.op("dve", lambda e: e.memset(Wpad[:], 0.0), w=[Wpad])
            for b in range(32):
                S.op("dve", lambda e, b=b, ptb=ptb: e.tensor_copy(out=Wpad[:, b, 4 * b:4 * b + 4], in_=ptb[0:32, 4 * b:4 * b + 4]),
                     r=[pt, Wpad], w=[Wpad])
            IF = pm_[0]
            for hd in range(8):
                hp, s_ = hd // 2, hd % 2
                pSx = self.pS[self.pS_i % 2]
                self.pS_i += 1
                S.op("pe", lambda e, pSx=pSx, hp=hp, s_=s_: e.matmul(
                    pSx[:, 0:128], lhsT=self.iqT[64 * s_:64 * s_ + 64, hp, :], rhs=self.kidxT[64 * s_:64 * s_ + 64, 0:128],
                    start=True, stop=True), r=[self.iqT, self.kidxT_key(0)], w=[pSx])
                Fx = pm_[1 + hd % 2]
                S.op("act", lambda e, pSx=pSx, Fx=Fx: e.activation(out=Fx[:], in_=pSx[:, 0:128], func=AF.Relu), r=[pSx], w=[Fx])
                if hd == 0:
                    S.op("dve", lambda e, Fx=Fx: e.tensor_scalar(out=IF[:], in0=Fx[:], scalar1=self.wsm[:, 8:9], scalar2=None, op0=ALU.mult),
                         r=[Fx, (self.wsm, "sg")], w=[IF])
                else:
                    S.op("dve", lambda e, Fx=Fx, hd=hd: e.scalar_tensor_tensor(
                        out=IF[:], in0=Fx[:], scalar=self.wsm[:, 8 + hd:9 + hd], in1=IF[:], op0=ALU.mult, op1=ALU.add),
                        r=[Fx, (self.wsm, "sg"), IF], w=[IF])
            S.op("dve", lambda e: e.tensor_tensor(
                out=pm_[3][:].rearrange("p (b t) -> p b t", b=32), in0=IF[:].rearrange("p (b t) -> p b t", b=32),
                in1=self.c2("seqmask").unsqueeze(2).to_broadcast([128, 32, 4]), op=ALU.mult), r=[IF, self.cst2], w=[pm_[3]])
            S.op("dve", lambda e: e.reduce_sum(out=INEW[:, 0:4], in_=pm_[3][:].rearrange("p (b t) -> p t b", b=32), axis=AX.X),
                 r=[pm_[3]], w=[INEW])
            S.op("dve", lambda e: e.tensor_tensor(out=INEW[:, 0:4], in0=INEW[:, 0:4], in1=self.c2("causal4"), op=ALU.add),
                 r=[INEW, self.cst2], w=[INEW])
            if DBG_STOP == "s_pre":
                S.emit()
                return
            with ExitStack() as st3:
                sb3 = lambda n, s, dt: self.sb(n, s, dt, st3)
                stf = [sb3("stf%d" % i, [128, 4, 2, 512], F32) for i in range(2)]
                sbfb = sb3("sbfb", [128, 8, 512], BF16)
                qdall = sb3("qdall", [128, 8, 128], BF16)
                kdt = sb3("kdt", [128, 1024], BF16)
                qz = sb3("qz", [128, 8, 128], BF16)
                kz = sb3("kz", [128, 1024], BF16)
                PTs = [sb3("PTs%d" % i, [128, 128], BF16) for i in range(4)]
                obank = [self.pmm[0], self.pmm[1], self.pO[0], self.pO[1]]
                rqT, rkT, rv_bf, rk_r = self.rqT, self.rkT, self.rv_bf, self.B[1]
                for h in range(4):
                    pSx = self.pS[h % 2]
                    for kt in range(2):
                        S.op("pe", lambda e, pSx=pSx, h=h, kt=kt: e.matmul(
                            pSx[:, 0:128], lhsT=rkT[:, 2 * h + kt, :], rhs=rqT[:, 2 * h + kt, :], start=(kt == 0), stop=(kt == 1)),
                            r=[self.RQK], w=[pSx])
                    S.op("dve", lambda e, pSx=pSx, h=h: e.tensor_tensor(
                        out=PTs[h][:], in0=pSx[:, 0:128], in1=self.c2("decayT", h * 128, 128), op=ALU.mult),
                        r=[pSx, self.cst2], w=[PTs[h]])
                    S.op("pe", lambda e, h=h: e.matmul(obank[h][:], lhsT=PTs[h][:], rhs=rv_bf[:, h, :], start=True, stop=False),
                         r=[PTs[h], rv_bf], w=[obank[h]])
                    S.op("dve", lambda e, h=h: e.tensor_tensor(
                        out=qdall[:, 2 * h:2 * h + 2, :], in0=rqT[:, 2 * h:2 * h + 2, :],
                        in1=self.c2("rowdec", h * 128, 128).unsqueeze(1).to_broadcast([128, 2, 128]), op=ALU.mult),
                        r=[self.RQK, self.cst2], w=[qdall])
                    S.op("dve", lambda e, h=h: e.tensor_scalar(
                        out=kdt[:, h * 256:(h + 1) * 256], in0=rk_r[:, h * 256:(h + 1) * 256], scalar1=self.c2("kdec", h, 1),
                        scalar2=None, op0=ALU.mult), r=[rk_r, self.cst2], w=[kdt])
                for b in range(cfg.NSEQ):
                    sf = stf[b % 2]
                    S.op("sp", lambda e, sf=sf, b=b: e.dma_start(
                        out=sf[:], in_=d["state_in"][b].rearrange("h (kt p) v -> p h kt v", p=128)), w=[sf], dma=True)
                    S.op("act", lambda e, sf=sf: e.copy(out=sbfb[:, 0:4, :], in_=sf[:, 0:2, :, :].rearrange("p h k v -> p (h k) v")),
                         r=[sf], w=[(sbfb, 0)])
                    S.op("dve", lambda e, sf=sf: e.tensor_copy(out=sbfb[:, 4:8, :], in_=sf[:, 2:4, :, :].rearrange("p h k v -> p (h k) v")),
                         r=[sf], w=[(sbfb, 1)])
                    S.op("dve", lambda e, b=b: e.tensor_tensor(
                        out=qz[:], in0=qdall[:],
                        in1=self.c2("onesw", 124 - 4 * b, 128).unsqueeze(1).to_broadcast([128, 8, 128]), op=ALU.mult),
                        r=[qdall, self.cst2], w=[qz])
                    S.op("dve", lambda e, b=b: e.tensor_scalar(out=kz[:], in0=kdt[:], scalar1=self.c2("seqmask", b, 1), scalar2=None,
                                                               op0=ALU.mult), r=[kdt, self.cst2], w=[kz])
                    for h in range(4):
                        for kt in range(2):
                            S.op("pe", lambda e, h=h, kt=kt, b=b: e.matmul(
                                obank[h][:], lhsT=qz[:, 2 * h + kt, :], rhs=sbfb[:, 2 * h + kt, :], start=False,
                                stop=(b == cfg.NSEQ - 1 and kt == 1)),
                                r=[qz, (sbfb, h // 2)], w=[obank[h]])
                    for h in range(4):
                        for kt in range(2):
                            pSx = self.pS[self.pS_i % 2]
                            self.pS_i += 1
                            S.op("pe", lambda e, pSx=pSx, h=h, kt=kt: e.matmul(
                                pSx[:], lhsT=kz[:, h * 256 + kt * 128:h * 256 + (kt + 1) * 128], rhs=rv_bf[:, h, :], start=True, stop=True),
                                r=[kz, rv_bf], w=[pSx])
                            S.op("dve", lambda e, pSx=pSx, sf=sf, h=h, kt=kt: e.scalar_tensor_tensor(
                                out=sf[:, h, kt, :], in0=sf[:, h, kt, :], scalar=float(gC4[h]), in1=pSx[:], op0=ALU.mult, op1=ALU.add),
                                r=[pSx, sf], w=[sf])
                    S.op("pool", lambda e, sf=sf, b=b: e.dma_start(
                        out=d["state_out"][b].rearrange("h (kt p) v -> p h kt v", p=128), in_=sf[:]), r=[sf], dma=True)
                for h in range(4):
                    yh = self.F[h % 2]
                    S.op("act", lambda e, h=h, yh=yh: e.copy(out=yh[:], in_=obank[h][:]), r=[obank[h]], w=[yh])
                    self.layer_norm(yh[:], yh[:], 512, self.GI[:, h * 512:(h + 1) * 512], self.GI[:, 2048 + h * 512:2048 + (h + 1) * 512],
                                    [yh], [yh], [self.GI])
                    S.op("dve", lambda e, yh=yh, h=h: e.tensor_tensor(
                        out=self.ret_o[:, h * 512:(h + 1) * 512], in0=yh[:], in1=self.sgate[:, h * 512:(h + 1) * 512], op=ALU.mult),
                        r=[yh, (self.sgate, h)], w=[(self.sgate, h)])
                S.emit()
            if DBG_STOP == "s_ret":
                return
            with ExitStack() as st3:
                sb3 = lambda n, s, dt: self.sb(n, s, dt, st3)
                IALL = TV("v_IALL", self.ARENA[:, 0:PAST])
                PG = [sb3("PG%d" % i, [128, 2048], F32) for i in range(2)]
                slabT = [sb3("slabT%d" % i, [128, 512], BF16) for i in range(2)]
                rS = [sb3("rS%d" % i, [32, 512], BF16) for i in range(2)]
                ptq = [sb3("ptq%d" % q, [128, 32], I32) for q in range(4)]
                ptf = sb3("ptf", [128, 32], F32)
                S.op("pool", lambda e: e.memset(IALL[:], 0.0), w=[IALL])
                S.op("dve", lambda e: e.tensor_copy(out=ptf[:], in_=ptc[:]), r=[ptc], w=[ptf])
                for q in range(4):
                    S.op("dve", lambda e, q=q: e.tensor_scalar(out=pm_[0][:, 0:32], in0=ptf[:], scalar1=4.0, scalar2=float(q),
                                                               op0=ALU.mult, op1=ALU.add), r=[ptf], w=[pm_[0]])
                    S.op("dve", lambda e, q=q: e.tensor_copy(out=ptq[q][:], in_=pm_[0][:, 0:32]), r=[pm_[0]], w=[ptq[q]])
                kq = d["cache_kidx"].rearrange("n (q f) -> (n q) f", q=4)
                W4 = 4 * NP
                it = 0
                for q in range(4):
                    for b in range(cfg.NSEQ):
                        pg = PG[it % 2]
                        it += 1
                        S.op("pool", lambda e, pg=pg, q=q, b=b: e.indirect_dma_start(
                            out=pg[:, :], out_offset=None, in_=kq,
                            in_offset=bass.IndirectOffsetOnAxis(ap=ptq[q][:, b:b + 1], axis=0)),
                            r=[ptq[q]], w=[pg], dma=True, dsem="pg%d" % (it % 2))
                        for g4 in range(4):
                            pt = self.ptr[self.ptr_i % 2]
                            self.ptr_i += 1
                            for i in range(4):
                                s_ = g4 * 4 + i
                                S.op("pe", lambda e, pt=pt, pg=pg, i=i, s_=s_: e.transpose(
                                    out=pt[:, i * NP:(i + 1) * NP], in_=pg[0:NP, s_ * 128:(s_ + 1) * 128],
                                    identity=self.c("ident")[0:NP, 0:NP]), r=[pg, self.cst], w=[pt])
                            sT = slabT[g4 % 2]
                            S.op("act", lambda e, pt=pt, sT=sT: e.copy(out=sT[:, 0:W4], in_=pt[:, 0:W4]), r=[pt], w=[sT])
                            for half in range(2):
                                for ii in range(2):
                                    i = 2 * half + ii
                                    for o2 in range(2):
                                        pSx = self.pS[o2]
                                        S.op("pe", lambda e, pSx=pSx, sT=sT, i=i, ii=ii, o2=o2, b=b: e.matmul(
                                            pSx[0:32, ii * NP:(ii + 1) * NP],
                                            lhsT=QDb[64 * o2:64 * o2 + 64, b, :], rhs=sT[64 * o2:64 * o2 + 64, i * NP:(i + 1) * NP],
                                            start=True, stop=True, skip_group_check=True), r=[QDb, sT], w=[pSx])
                                rSx = rS[half]
                                for o2 in range(2):
                                    pSx = self.pS[o2]
                                    S.op("act", lambda e, pSx=pSx, rSx=rSx, o2=o2: e.activation(
                                        out=rSx[:, 0:W4].rearrange("p (i o n) -> p i o n", i=2, o=2)[:, :, o2, :],
                                        in_=pSx[0:32, 0:2 * NP].rearrange("p (i n) -> p i n", i=2), func=AF.Relu),
                                        r=[pSx], w=[(rSx, o2)])
                                rkeys_ = [(rSx, 0), (rSx, 1)]
                                pOx = self.pO[half]
                                S.op("pe", lambda e, pOx=pOx, rSx=rSx, b=b: e.matmul(
                                    pOx[:, 0:W4], lhsT=Wpad[:, b, :], rhs=rSx[:, 0:W4], start=True, stop=True), r=[Wpad] + rkeys_, w=[pOx])
                                u0 = (q * 16 + g4 * 4 + 2 * half) * 2
                                S.op("dve", lambda e, pOx=pOx, u0=u0: e.tensor_tensor(
                                    out=IALL[:, u0 * NP:u0 * NP + W4], in0=IALL[:, u0 * NP:u0 * NP + W4], in1=pOx[:, 0:W4], op=ALU.add),
                                    r=[pOx, IALL], w=[IALL])
                for r_ in range(NR if DBG_STOP != "s_idx" else 0):
                    sl = slice(r_ * 8, r_ * 8 + 8)
                    S.op("dve", lambda e, sl=sl: e.max(out=VALS[:, sl], in_=IALL[:]), r=[IALL], w=[VALS])
                    S.op("dve", lambda e, sl=sl: e.max_index(out=IDX[:, sl], in_max=VALS[:, sl], in_values=IALL[:]), r=[IALL, VALS], w=[IDX])
                    if r_ < NR - 1:
                        S.op("dve", lambda e, sl=sl: e.match_replace(out=IALL[:], in_to_replace=VALS[:, sl], in_values=IALL[:], imm_value=NEG),
                             r=[IALL, VALS], w=[IALL])
                S.emit()
            if DBG_STOP in ("s_idx", "s_topk"):
                return
            with ExitStack() as st3:
                sb3 = lambda n, s, dt: self.sb(n, s, dt, st3)
                ROWi = sb3("ROWi", [128, NSL], I32)
                SELP = sb3("SELP", [128, NSL], F32)
                SELN = sb3("SELN", [128, 8], F32)
                tI = [sb3("tI%d" % i, [128, NSL], I32) for i in range(2)]
                tF = [sb3("tF%d" % i, [128, NSL], F32) for i in range(4)]
                ptrow = sb3("ptrow", [128, 128], I32)
                ptrf = sb3("ptrf", [128, 128], F32)
                SCALL = sb3("SCALL", [128, NSL, 8], F32)
                PALL = sb3("PALL", [128, NSL, 8], F32)
                KR = [sb3("KR%d" % i, [128, 256], F32) for i in range(2)]
                ACC = sb3("ACC", [128, 8, 128], F32)
                DEN = sb3("DEN", [128, 8], F32)
                cnts = sb3("cnts", [128, 16], F32)
                IDXi = IDX[:].bitcast(I32)
                S.op("sp", lambda e: e.dma_start(out=ptrow[:, 0:NP], in_=d["ptrow"]), w=[ptrow], dma=True)
                S.op("dve", lambda e: e.tensor_copy(out=ptrf[:, 0:NP], in_=ptrow[:, 0:NP]), r=[ptrow], w=[ptrf])
                S.op("dve", lambda e: e.tensor_single_scalar(out=tI[0][:], in_=IDXi, scalar=NP - 1, op=ALU.bitwise_and), r=[IDX], w=[tI[0]])
                S.op("dve", lambda e: e.tensor_single_scalar(out=tI[1][:], in_=IDXi, scalar=lgNP, op=ALU.logical_shift_right), r=[IDX], w=[tI[1]])
                S.op("dve", lambda e: e.tensor_copy(out=tF[0][:], in_=tI[0][:]), r=[tI[0]], w=[tF[0]])
                S.op("dve", lambda e: e.tensor_copy(out=tF[1][:], in_=tI[1][:]), r=[tI[1]], w=[tF[1]])
                CH = max(1, min(NSL, 4096 // NP))
                GI = self.GI
                for c0 in range(0, NSL, CH):
                    cn = min(CH, NSL - c0)
                    OH = GI[:, 0:cn * NP].rearrange("p (j n) -> p j n", j=cn)
                    S.op("dve", lambda e, OH=OH, c0=c0, cn=cn: e.tensor_tensor(
                        out=OH, in0=tF[0][:, c0:c0 + cn].unsqueeze(2).to_broadcast([128, cn, NP]),
                        in1=self.c("iota128", 0, NP).unsqueeze(1).to_broadcast([128, cn, NP]), op=ALU.is_equal),
                        r=[tF[0], self.cst, GI], w=[GI])
                    S.op("dve", lambda e, OH=OH, cn=cn: e.tensor_tensor(
                        out=OH, in0=OH, in1=ptrf[:, 0:NP].unsqueeze(1).to_broadcast([128, cn, NP]), op=ALU.mult), r=[GI, ptrf], w=[GI])
                    S.op("dve", lambda e, OH=OH, c0=c0, cn=cn: e.reduce_sum(out=tF[2][:, c0:c0 + cn], in_=OH, axis=AX.X), r=[GI], w=[tF[2]])
                S.op("dve", lambda e: e.scalar_tensor_tensor(out=tF[2][:], in0=tF[2][:], scalar=128.0, in1=tF[1][:], op0=ALU.mult, op1=ALU.add),
                     r=[tF[2], tF[1]], w=[tF[2]])
                S.op("dve", lambda e: e.tensor_copy(out=ROWi[:], in_=tF[2][:]), r=[tF[2]], w=[ROWi])
                CR = tF[3]
                for n in range(4):
                    if n == 0:
                        S.op("dve", lambda e, n=n: e.tensor_scalar(out=CR[:], in0=VALS[:], scalar1=INEW[:, n:n + 1], scalar2=None, op0=ALU.is_lt),
                             r=[VALS, INEW], w=[CR])
                    else:
                        S.op("dve", lambda e, n=n: e.tensor_scalar(out=tF[0][:], in0=VALS[:], scalar1=INEW[:, n:n + 1], scalar2=None, op0=ALU.is_lt),
                             r=[VALS, INEW], w=[tF[0]])
                        S.op("dve", lambda e: e.tensor_tensor(out=CR[:], in0=CR[:], in1=tF[0][:], op=ALU.add), r=[CR, tF[0]], w=[CR])
                S.op("dve", lambda e: e.tensor_tensor(out=CR[:], in0=CR[:], in1=self.c2("iota256", 0, NSL), op=ALU.add), r=[CR, self.cst2], w=[CR])
                S.op("dve", lambda e: e.tensor_scalar(out=SELP[:], in0=CR[:], scalar1=float(K) - 0.5, scalar2=None, op0=ALU.is_lt), r=[CR], w=[SELP])
                for n in range(4):
                    S.op("dve", lambda e, n=n: e.tensor_scalar(out=tF[0][:], in0=VALS[:], scalar1=INEW[:, n:n + 1], scalar2=None, op0=ALU.is_gt),
                         r=[VALS, INEW], w=[tF[0]])
                    S.op("dve", lambda e, n=n: e.reduce_sum(out=cnts[:, n:n + 1], in_=tF[0][:], axis=AX.X), r=[tF[0]], w=[(cnts, n)])
                    S.op("dve", lambda e, n=n: e.tensor_scalar(out=cnts[:, 8:12], in0=INEW[:, 0:4], scalar1=INEW[:, n:n + 1], scalar2=None, op0=ALU.is_gt),
                         r=[INEW], w=[(cnts, "t")])
                    S.op("dve", lambda e, n=n: e.reduce_sum(out=cnts[:, 4 + n:5 + n], in_=cnts[:, 8:12], axis=AX.X), r=[(cnts, "t")], w=[(cnts, 4 + n)])
                ck = [(cnts, i) for i in range(8)]
                S.op("dve", lambda e: e.tensor_tensor(out=cnts[:, 12:16], in0=cnts[:, 0:4], in1=cnts[:, 4:8], op=ALU.add), r=ck, w=[(cnts, "s")])
                S.op("dve", lambda e: e.tensor_scalar(out=SELN[:, 0:4], in0=cnts[:, 12:16], scalar1=float(K) - 0.5, scalar2=None, op0=ALU.is_lt),
                     r=[(cnts, "s")], w=[SELN])
                S.op("dve", lambda e: e.tensor_scalar(out=SELN[:, 4:8], in0=INEW[:, 0:4], scalar1=-1e29, scalar2=None, op0=ALU.is_gt), r=[INEW], w=[(SELN, 1)])
                S.op("dve", lambda e: e.tensor_tensor(out=SELN[:, 0:4], in0=SELN[:, 0:4], in1=SELN[:, 4:8], op=ALU.mult), r=[SELN, (SELN, 1)], w=[SELN])
                kflat, vflat = d["cache_k"], d["cache_v"]
                AQ4 = AQS[:].rearrange("p (g r f) -> p g r f", g=2, r=4)
                PRD = GI[:, 0:1024]
                for j in range(NSL):
                    kr = KR[j % 2]
                    S.op("pool", lambda e, kr=kr, j=j: e.indirect_dma_start(
                        out=kr[:], out_offset=None, in_=kflat, in_offset=bass.IndirectOffsetOnAxis(ap=ROWi[:, j:j + 1], axis=0)),
                        r=[ROWi], w=[kr], dma=True, dsem="kr%d" % (j % 2))
                    S.op("dve", lambda e, kr=kr: e.tensor_tensor(
                        out=PRD.rearrange("p (g r f) -> p g r f", g=2, r=4), in0=AQ4,
                        in1=kr[:].rearrange("p (g f) -> p g f", g=2).unsqueeze(2).to_broadcast([128, 2, 4, 128]), op=ALU.mult),
                        r=[AQS, kr, GI], w=[GI])
                    S.op("dve", lambda e, j=j: e.reduce_sum(out=SCALL[:, j, :], in_=PRD.rearrange("p (h f) -> p h f", h=8), axis=AX.X),
                         r=[GI], w=[SCALL])
                S.op("act", lambda e: e.activation(out=PALL[:], in_=SCALL[:], func=AF.Exp, scale=128.0 ** -0.5), r=[SCALL], w=[PALL])
                S.op("dve", lambda e: e.tensor_tensor(out=PALL[:], in0=PALL[:], in1=SELP[:].unsqueeze(2).to_broadcast([128, NSL, 8]), op=ALU.mult),
                     r=[PALL, SELP], w=[PALL])
                S.op("dve", lambda e: e.reduce_sum(out=DEN[:], in_=PALL[:].rearrange("p j h -> p h j"), axis=AX.X), r=[PALL], w=[DEN])
                S.op("pool", lambda e: e.memset(ACC[:], 0.0), w=[ACC])
                for j in range(NSL):
                    vr = KR[j % 2]
                    S.op("pool", lambda e, vr=vr, j=j: e.indirect_dma_start(
                        out=vr[:], out_offset=None, in_=vflat, in_offset=bass.IndirectOffsetOnAxis(ap=ROWi[:, j:j + 1], axis=0)),
                        r=[ROWi], w=[vr], dma=True, dsem="kr%d" % (j % 2))
                    S.op("dve", lambda e, vr=vr, j=j: e.tensor_tensor(
                        out=PRD.rearrange("p (g r f) -> p g r f", g=2, r=4),
                        in0=vr[:].rearrange("p (g f) -> p g f", g=2).unsqueeze(2).to_broadcast([128, 2, 4, 128]),
                        in1=PALL[:, j, :].rearrange("p (g r) -> p g r", g=2).unsqueeze(3).to_broadcast([128, 2, 4, 128]), op=ALU.mult),
                        r=[vr, PALL, GI], w=[GI])
                    S.op("dve", lambda e: e.tensor_tensor(out=ACC[:].rearrange("p h f -> p (h f)"), in0=ACC[:].rearrange("p h f -> p (h f)"),
                                                          in1=PRD, op=ALU.add), r=[GI, ACC], w=[ACC])
                MQ = self.mch
                S.op("dve", lambda e: e.tensor_tensor(
                    out=MQ[:, 0:128].rearrange("p (b t) -> p b t", b=32),
                    in0=self.c2("seqmask").unsqueeze(2).to_broadcast([128, 32, 4]),
                    in1=SELN[:, 0:4].unsqueeze(1).to_broadcast([128, 32, 4]), op=ALU.mult), r=[self.cst2, SELN], w=[MQ])
                self.transposes_into([(MQ[:, 0:128], [MQ])], self.mT[:, 0:1, :], (self.mT, 0))
                for g in range(2):
                    pSx = self.pS[self.pS_i % 2]
                    self.pS_i += 1
                    S.op("pe", lambda e, pSx=pSx, g=g: e.matmul(pSx[:], lhsT=self.KT[:, 0, g, :], rhs=self.aqT[:, 4 * g:4 * g + 4, :],
                                                                 start=True, stop=True), r=[self.KT_key(0), self.aqT], w=[pSx])
                    Ex, Px = self.Ebuf[g], self.Pbuf[g]
                    S.op("act", lambda e, pSx=pSx, Ex=Ex: e.activation(out=Ex[:], in_=pSx[:], func=AF.Exp, scale=128.0 ** -0.5), r=[pSx], w=[Ex])
                    S.op("dve", lambda e, Ex=Ex, Px=Px: e.tensor_tensor(
                        out=Px[:], in0=Ex[:].rearrange("p (r t) -> p r t", r=4),
                        in1=self.mT[:, 0, :].unsqueeze(1).to_broadcast([128, 4, 128]), op=ALU.mult), r=[Ex, (self.mT, 0)], w=[Px])
                    for r in range(4):
                        pOx = self.pO[r // 2]
                        S.op("pe", lambda e, pOx=pOx, Px=Px, r=r, g=g: e.matmul(
                            pOx[:, (r % 2) * 129:(r % 2) * 129 + 129], lhsT=Px[:, r, :], rhs=self.V[:, 0, g, :],
                            start=(r % 2 == 0), stop=True, skip_group_check=True), r=[Px, self.V_key(0), (self.V_key(0), "one")], w=[pOx])
                    for r in range(4):
                        pOx = self.pO[r // 2]
                        o0 = (r % 2) * 129
                        hd = 4 * g + r
                        S.op("dve", lambda e, pOx=pOx, o0=o0, hd=hd: e.tensor_tensor(
                            out=self.rsum[:, hd:hd + 1], in0=pOx[:, o0 + 128:o0 + 129], in1=DEN[:, hd:hd + 1], op=ALU.add),
                            r=[pOx, DEN], w=[(self.rsum, hd)])
                        S.op("dve", lambda e, hd=hd: e.reciprocal(out=self.rsum[:, hd:hd + 1], in_=self.rsum[:, hd:hd + 1]),
                             r=[(self.rsum, hd)], w=[(self.rsum, hd)])
                        S.op("dve", lambda e, pOx=pOx, o0=o0, hd=hd: e.tensor_tensor(
                            out=ACC[:, hd, :], in0=ACC[:, hd, :], in1=pOx[:, o0:o0 + 128], op=ALU.add), r=[pOx, ACC], w=[ACC])
                        S.op("dve", lambda e, hd=hd: e.tensor_scalar(
                            out=self.att_o[:, hd, :], in0=ACC[:, hd, :], scalar1=self.rsum[:, hd:hd + 1], scalar2=None, op0=ALU.mult),
                            r=[ACC, (self.rsum, hd)], w=[(self.att_o, hd)])
                self.xsrc = d["xs"]
                self.finish(d["ps"], d["ys"])
                S.emit()

    def build(self):
        with ExitStack() as st:
            self.st = st
            self.declare()
            self.S = Sched(self.nc, st)
            self.phase_prep()
            self.alloc_common()
            if self.cfg.prompt:
                self.phase_prompt()
            if self.cfg.sample:
                self.phase_sample()
        return self.nc


def rope_tables(pos):
    pos = np.asarray(pos).astype(np.float32)
    cs, sn = [], []
    for half in (128, 64, 32):
        inv = (np.float32(10000.0) ** (-np.arange(half, dtype=np.float32) / np.float32(half))).astype(np.float32)
        ang = (pos[:, None] * inv[None, :]).astype(np.float32)
        cs.append(np.cos(ang).astype(np.float32))
        sn.append(np.sin(ang).astype(np.float32))
    return np.concatenate(cs, 1), np.concatenate(sn, 1)


def const_table(lightbias, ikg, ikb):
    t = np.zeros((128, NCST), np.float32)

    def put(name, arr):
        o, n = CST[name]
        t[:, o:o + n] = arr

    put("ident", np.eye(128, dtype=np.float32))
    lg = np.log1p(-np.exp2(-5.0 - np.arange(4, dtype=np.float32))).astype(np.float32)
    i = np.arange(128, dtype=np.float32)
    dT = np.zeros((128, 4, 128), np.float32)
    for h in range(4):
        diff = i[None, :] - i[:, None]
        dT[:, h, :] = np.where(diff >= 0, np.exp(lg[h] * np.maximum(diff, 0.0)), 0.0) / 16.0
    put("decayT", dT.reshape(128, 512))
    rd = np.stack([np.exp(lg[h] * (i + 1.0)) for h in range(4)], 0).reshape(1, 512)
    put("rowdec", np.broadcast_to(rd, (128, 512)))
    put("kdec", np.stack([np.exp(lg[h] * (127.0 - i)) / 16.0 for h in range(4)], 1))
    q = np.arange(128)
    put("causal", np.where(q[None, :] <= q[:, None], 0.0, NEG).astype(np.float32))
    put("iota16", np.broadcast_to(np.arange(16, dtype=np.float32)[None], (128, 16)))
    put("ikg", np.broadcast_to(ikg[None], (128, 64)))
    put("ikb", np.broadcast_to(ikb[None], (128, 64)))
    put("lightbias", np.full((128, 1), lightbias, np.float32))
    put("iota128", np.broadcast_to(np.arange(128, dtype=np.float32)[None], (128, 128)))
    return t


def param_table(gn_g, gn_b, ln1_g, ln1_b, ln2_g, ln2_b):
    row = np.concatenate([gn_g, gn_b, ln1_g, ln1_b, ln2_g, ln2_b]).astype(np.float32)
    return np.ascontiguousarray(np.broadcast_to(row[None], (128, 8192)))


_PROG_CACHE = {}


def get_prog(cfg_key, cfg):
    if cfg_key not in _PROG_CACHE:
        _PROG_CACHE[cfg_key] = Prog(cfg).build()
    return _PROG_CACHE[cfg_key]


def const_table2(dec_seq=4):
    t = np.zeros((128, NCS2), np.float32)

    def put(name, arr):
        o, n = CS2[name]
        t[:, o:o + n] = arr

    lg = np.log1p(-np.exp2(-5.0 - np.arange(4, dtype=np.float32))).astype(np.float32)
    r = np.arange(128)
    bq, tq = r // dec_seq, r % dec_seq
    dT = np.zeros((128, 4, 128), np.float32)
    same = bq[:, None] == bq[None, :]
    diff = (tq[None, :] - tq[:, None]).astype(np.float32)
    for h in range(4):
        dT[:, h, :] = np.where(same & (diff >= 0), np.exp(lg[h] * np.maximum(diff, 0.0)), 0.0) / 16.0
    put("decayT", dT.reshape(128, 512))
    rd = np.stack([np.exp(lg[h] * (tq.astype(np.float32) + 1.0)) for h in range(4)], 0).reshape(1, 512)
    put("rowdec", np.broadcast_to(rd, (128, 512)))
    put("kdec", np.stack([np.exp(lg[h] * (dec_seq - 1.0 - tq.astype(np.float32))) / 16.0 for h in range(4)], 1))
    put("tmask", (tq[:, None] == np.arange(4)[None, :]).astype(np.float32))
    put("seqmask", (bq[:, None] == np.arange(32)[None, :]).astype(np.float32))
    put("causal4", np.where(np.arange(4)[None, :] <= tq[:, None], 0.0, NEG).astype(np.float32))
    ow = np.zeros((128, 252), np.float32)
    ow[:, 124:128] = 1.0
    put("onesw", ow)
    put("iota256", np.broadcast_to(np.arange(256, dtype=np.float32)[None], (128, 256)))
    return t


FUSED = True
SEQ, NB, DEC_B, DEC_S, PASTL = 4096, 4, 32, 4, 16384


def _weights_map(inp):
    sk = inp["peer_subkeys"][0].reshape(16, 128, 128)
    return dict(
        params=param_table(inp["gn_g"][0], inp["gn_b"][0], inp["ln1_g"][0], inp["ln1_b"][0], inp["ln2_g"][0], inp["ln2_b"][0]),
        w_in=inp["w_in"][0], w_ret_o=inp["w_ret_o"][0], w_att_o=inp["w_att_o"][0], w_out=inp["w_out"][0],
        w_ple_gate=inp["w_ple_gate"][0], w_ple=inp["w_ple"][0], peer_wq=inp["peer_wq"][0],
        skT=np.ascontiguousarray(sk.transpose(2, 0, 1).reshape(128, 2048)),
        peer_u=inp["peer_u"][0], peer_v=inp["peer_v"][0])


def _sample_map(inp, real):
    nphys = inp["cache_k"].shape[1]
    if real:
        pt = np.asarray(inp["page_table"]).astype(np.int32)
        return dict(
            xs=inp["x_sample"].reshape(128, D), ps=inp["p_sample"][0].reshape(128, 256),
            cache_k=inp["cache_k"][0].reshape(nphys * 128, 256), cache_v=inp["cache_v"][0].reshape(nphys * 128, 256),
            cache_kidx=inp["cache_kidx"][0].reshape(nphys, 8192), state_in=inp["state_ret"][0],
            ptcol=np.ascontiguousarray(pt.T), ptrow=np.ascontiguousarray(np.repeat(pt, DEC_S, axis=0)), cst2=const_table2(DEC_S))
    z = lambda *s: np.zeros(s, np.float32)
    return dict(xs=z(128, D), ps=z(128, 256), cache_k=z(nphys * 128, 256), cache_v=z(nphys * 128, 256),
                cache_kidx=z(nphys, 8192), state_in=z(DEC_B, 4, 256, 512),
                ptcol=np.zeros((PASTL // 128, DEC_B), np.int32), ptrow=np.zeros((128, PASTL // 128), np.int32),
                cst2=const_table2(DEC_S))


def _cs_table(pos_blocks):
    cosb, sinb = rope_tables(np.concatenate(pos_blocks))
    n = len(pos_blocks)
    return np.ascontiguousarray(np.concatenate([cosb.reshape(n, 128, 224), sinb.reshape(n, 128, 224)], axis=2))


def kernel(**inp):
    inp = {k: np.asarray(v) for k, v in inp.items()}
    nphys = inp["cache_k"].shape[1]
    NL = NO = 16
    ikg, ikb = inp["idx_k_g"][0], inp["idx_k_b"][0]
    wmap = _weights_map(inp)
    spos = PASTL + (np.arange(128) % DEC_S)
    in_maps = []
    for c in range(8):
        b, half = c // 2, c % 2
        xb = np.zeros((NL + NO, 128, D), np.float32)
        if half == 1:
            xb[:NL] = inp["x_prompt"][b, 0:2048].reshape(NL, 128, D)
        xb[NL:] = inp["x_prompt"][b, half * 2048:(half + 1) * 2048].reshape(NO, 128, D)
        pos = [np.arange(i * 128, (i + 1) * 128) for i in range(NL)] + \
              [half * 2048 + np.arange(i * 128, (i + 1) * 128) for i in range(NO)] + [spos]
        m = dict(xb=xb, pb=np.ascontiguousarray(inp["p_prompt"][0, b, half * 2048:(half + 1) * 2048].reshape(NO, 128, 256)),
                 csb=_cs_table(pos), cst=const_table(0.0 if half == 1 else NEG, ikg, ikb))
        m.update(wmap)
        if FUSED:
            m.update(_sample_map(inp, real=(c == 0)))
        in_maps.append(m)
    if FUSED:
        cfg = Cfg(NL=NL, NO=NO, TOPK=256, sample=True, NSEQ=DEC_B, PAST=PASTL, NPHYS=nphys, TOPK_S=256)
        res = run_bass_kernel_spmd(get_prog(("fused", nphys), cfg), in_maps, core_ids=list(range(8))).results
        rs = res[0]
    else:
        cfg = Cfg(NL=NL, NO=NO, TOPK=256, sample=False)
        res = run_bass_kernel_spmd(get_prog(("prompt",), cfg), in_maps, core_ids=list(range(8))).results
        cfg_s = Cfg(sample=True, NSEQ=DEC_B, PAST=PASTL, NPHYS=nphys, TOPK_S=256, prompt=False)
        ms = dict(csb=_cs_table([spos]), cst=const_table(0.0, ikg, ikb))
        ms.update(wmap)
        ms.update(_sample_map(inp, real=True))
        rs = run_bass_kernel_spmd(get_prog(("sample", nphys), cfg_s), [ms], core_ids=[0]).results[0]
    y = np.empty((NB, SEQ, D), np.float32)
    kp = np.empty((1, NB, SEQ, 2, 128), np.float32)
    vp = np.empty((1, NB, SEQ, 2, 128), np.float32)
    ikp = np.empty((1, NB, SEQ, 64), np.float32)
    rp = np.empty((1, NB, 4, 256, 512), np.float32)
    for c in range(8):
        b, half = c // 2, c % 2
        sl = slice(half * 2048, (half + 1) * 2048)
        y[b, sl] = res[c]["y"].reshape(2048, D)
        kp[0, b, sl] = res[c]["kout"].reshape(2048, 2, 128)
        vp[0, b, sl] = res[c]["vout"].reshape(2048, 2, 128)
        ikp[0, b, sl] = res[c]["kidxout"].reshape(2048, 64)
        if half == 1:
            rp[0, b] = res[c]["stout"]
    ys = np.ascontiguousarray(rs["ys"].reshape(DEC_B, DEC_S, D))
    ks = np.ascontiguousarray(rs["ks"].reshape(1, DEC_B, DEC_S, 2, 128))
    vs = np.ascontiguousarray(rs["vs"].reshape(1, DEC_B, DEC_S, 2, 128))
    iks = np.ascontiguousarray(rs["kidxs"].reshape(1, DEC_B, DEC_S, 64))
    rsm = np.ascontiguousarray(rs["state_out"].reshape(1, DEC_B, 4, 256, 512))
    return (y, ys, kp, vp, ikp, rp, ks, vs, iks, rsm)
```

```python
import numpy as np
from contextlib import ExitStack
import concourse.bass as bass
import concourse.mybir as mybir
from concourse.bass_utils import run_bass_kernel_spmd

F32 = mybir.dt.float32
BF16 = mybir.dt.bfloat16
I32 = mybir.dt.int32
U32 = mybir.dt.uint32
AF = mybir.ActivationFunctionType
ALU = mybir.AluOpType
AX = mybir.AxisListType

D = 1024
NCOLS = 10312
O_RQ, O_RK, O_RV, O_RG, O_AQ, O_AK, O_AV, O_IQ, O_IK, O_IW, O_GA, O_GB = (
    0, 1024, 2048, 4096, 6144, 7168, 7424, 7680, 8192, 8256, 8264, 9288)
ALPHA = 2.0 ** 0.25
LN_EPS = 1e-5
NEG = -1e30
NIT = 20
import os as _os
DBG_STOP = _os.environ.get('DBG_STOP', '')


class _Sem:
    __slots__ = ("h", "id")

    def __init__(self, h, i):
        self.h = h
        self.id = i


class Sched:
    ENG = ("pe", "act", "dve", "pool", "sp")
    LIMIT = 30000

    def __init__(self, nc, stack, n_dma_sems=40):
        self.nc = nc
        self.stack = stack
        self._nsem = 0
        self.q = {e: [] for e in self.ENG}
        self.esem = {e: self._newsem("e_" + e) for e in self.ENG if e != "sp"}
        self.ecnt = {e: 0 for e in self.ENG}
        self.waited = {e: {} for e in self.ENG}
        self.lastw = {}
        self.readers = {}
        self.dpool = [self._newsem("d%d" % i) for i in range(n_dma_sems)]
        self.dcnt = {s.id: 0 for s in self.dpool}
        self.drr = 0
        self.drr_pool = 0
        self.named_dsem = {}
        self.nops = 0
        self.all_esems = list(self.esem.values())

    def _newsem(self, name):
        h = self.stack.enter_context(self.nc.semaphore(name + "_%d" % self._nsem))
        s = _Sem(h, self._nsem)
        self._nsem += 1
        return s

    @staticmethod
    def _key(x):
        if isinstance(x, str):
            return x
        if isinstance(x, tuple):
            return Sched._key(x[0]) + "/" + str(x[1])
        n = getattr(x, "name", None)
        if n is None:
            n = getattr(getattr(x, "tensor", None), "name", None)
        assert n is not None, x
        return n

    def dsem(self, name):
        if name not in self.named_dsem:
            s = self._newsem("n_" + name)
            self.named_dsem[name] = s
            self.dcnt[s.id] = 0
        return self.named_dsem[name]

    def op(self, eng, fn, r=(), w=(), dma=False, dsem=None):
        self.nops += 1
        rk, wk = [], []
        for (lst, xs) in ((rk, r), (wk, w)):
            for x in xs:
                ak = getattr(x, "alias_keys", None)
                if ak:
                    lst.extend(ak)
                else:
                    lst.append(self._key(x))
        deps = []
        for k in rk:
            deps.extend(self.lastw.get(k, ()))
        for k in wk:
            deps.extend(self.lastw.get(k, ()))
            deps.extend(self.readers.get(k, ()))
        if dma:
            if dsem is None:
                half = len(self.dpool) // 2
                if eng == "pool":
                    s = self.dpool[half + self.drr_pool % half]
                    self.drr_pool += 1
                else:
                    s = self.dpool[self.drr % half]
                    self.drr += 1
            else:
                s = self.dsem(dsem) if isinstance(dsem, str) else dsem
            if self.dcnt[s.id] > 0:
                deps.append((s, 16 * self.dcnt[s.id], "dma"))
            self.dcnt[s.id] += 1
            tok = (s, 16 * self.dcnt[s.id], "dma")
        else:
            if self.ecnt[eng] >= self.LIMIT:
                self.esem[eng] = self._newsem("e_" + eng)
                self.all_esems.append(self.esem[eng])
                self.ecnt[eng] = 0
            self.ecnt[eng] += 1
            tok = (self.esem[eng], self.ecnt[eng], eng)
        need = {}
        for (s, v, e) in deps:
            if eng == "pe" and e == "pe":
                continue
            if self.waited[eng].get(s.id, 0) >= v:
                continue
            if need.get(s.id, (None, 0))[1] < v:
                need[s.id] = (s, v)
        waits = []
        for sid, (s, v) in need.items():
            self.waited[eng][sid] = v
            waits.append((s, v))
        self.q[eng].append((waits, fn, tok, dma))
        for k in wk:
            self.lastw[k] = [tok]
            self.readers[k] = []
        for k in rk:
            if k not in wk:
                self.readers.setdefault(k, []).append(tok)
        return tok

    def barrier(self):
        fin = []
        for s in list(self.dpool) + list(self.named_dsem.values()):
            if self.dcnt[s.id] > 0:
                fin.append((s, 16 * self.dcnt[s.id]))
        for e in ("pe", "act", "dve", "pool"):
            if self.ecnt[e] > 0:
                fin.append((self.esem[e], self.ecnt[e]))
        for e in self.ENG:
            waits = []
            for (s, v) in fin:
                if self.waited[e].get(s.id, 0) < v:
                    self.waited[e][s.id] = v
                    waits.append((s, v))
            if waits:
                self.q[e].append((waits, None, None, False))
        self.lastw = {}
        self.readers = {}

    def emit(self):
        self.barrier()
        q = self.q

        def run(eng_name, engine):
            for (waits, fn, tok, dma) in q[eng_name]:
                for (s, v) in waits:
                    engine.wait_ge(s.h, v)
                if fn is None:
                    continue
                ins = fn(engine)
                ins.then_inc(tok[0].h, 16 if dma else 1)

        with self.nc.Block() as block:
            @block.tensor
            def _(e):
                run("pe", e)

            @block.scalar
            def _(e):
                run("act", e)

            @block.vector
            def _(e):
                run("dve", e)

            @block.gpsimd
            def _(e):
                run("pool", e)

            @block.sync
            def _(e):
                run("sp", e)
        self.q = {e: [] for e in self.ENG}


class Cfg:
    def __init__(self, NL=16, NO=16, TOPK=256, sample=True, NSEQ=32, PAST=16384, NPHYS=5121,
                 TOPK_S=256, prompt=True):
        self.prompt = prompt
        if not prompt:
            NL = NO = 0
        self.NL, self.NO, self.TOPK = NL, NO, TOPK
        self.NBLK = NL + NO
        self.sample = sample
        self.NSEQ, self.PAST, self.NPHYS, self.TOPK_S = NSEQ, PAST, NPHYS, TOPK_S
        self.NPAGE = PAST // 128


GROUPS = [("rq", O_RQ, 1024), ("rk", O_RK, 1024), ("rv", O_RV, 2048), ("rg", O_RG, 2048),
          ("aq", O_AQ, 1024), ("kv", O_AK, 512), ("iq", O_IQ, 512), ("ikw", O_IK, 72),
          ("ga", O_GA, 1024), ("gb", O_GB, 1024)]
LIGHT_GROUPS = ("rk", "rv", "kv", "ikw")


def chunk_table():
    ch = []
    idx = {}

    def add(name, src, r0, nkt, c0, ncols):
        idx.setdefault(name, []).append(len(ch))
        ch.append(dict(name=name, src=src, r0=r0, nkt=nkt, c0=c0, nc=ncols))

    for (g, c0, n) in GROUPS:
        for c in range(0, n, 512):
            add("in_" + g, "w_in", 0, 8, c0 + c, min(512, n - c))
    for chh in range(2):
        for kh in range(2):
            add("ret_o%d" % chh, "w_ret_o", kh * 1024, 8, chh * 512, 512)
    for nm, src in (("att_o", "w_att_o"), ("out", "w_out"), ("pg", "w_ple_gate")):
        for chh in range(2):
            add(nm, src, 0, 8, chh * 512, 512)
    for c in range(4):
        add("wq", "peer_wq", 0, 8, c * 512, 512)
    add("ple", "w_ple", 0, 2, 0, 1024)
    add("skT", "skT", 0, 1, 0, 2048)
    return ch, idx


def _layout(items):
    lay = {}
    o = 0
    for name, n in items:
        lay[name] = (o, n)
        o += n
    return lay, o


def cst_layout():
    return _layout((("ident", 128), ("decayT", 512), ("rowdec", 512), ("kdec", 4), ("causal", 128),
                    ("iota16", 16), ("ikg", 64), ("ikb", 64), ("lightbias", 1), ("iota128", 128)))


CS2, NCS2 = _layout((("decayT", 512), ("rowdec", 512), ("kdec", 4), ("tmask", 4), ("seqmask", 32),
                     ("causal4", 4), ("onesw", 252), ("iota256", 256)))
CST, NCST = cst_layout()


class TV:
    def __init__(self, name, ap, alias_keys=None):
        self.name = name
        self.base = ap
        self.alias_keys = alias_keys

    def __getitem__(self, k):
        return self.base[k]


class Prog:
    def __init__(self, cfg):
        self.cfg = cfg
        self.nc = bass.Bass("TRN2", target_bir_lowering=False)
        self.chunks, self.cidx = chunk_table()

    def declare(self):
        nc, cfg = self.nc, self.cfg
        di = lambda n, s, d=F32: nc.dram_tensor(n, list(s), d, kind="ExternalInput").ap()
        do = lambda n, s, d=F32: nc.dram_tensor(n, list(s), d, kind="ExternalOutput").ap()
        self.d = d = {}
        if cfg.prompt:
            d["xb"] = di("xb", [cfg.NBLK, 128, D])
            d["pb"] = di("pb", [cfg.NO, 128, 256])
        d["csb"] = di("csb", [cfg.NBLK + 1, 128, 448])
        d["cst"] = di("cst", [128, NCST])
        d["params"] = di("params", [128, 8192])
        d["w_in"] = di("w_in", [D, NCOLS])
        d["w_ret_o"] = di("w_ret_o", [2048, D])
        d["w_att_o"] = di("w_att_o", [D, D])
        d["w_out"] = di("w_out", [D, D])
        d["w_ple_gate"] = di("w_ple_gate", [D, D])
        d["w_ple"] = di("w_ple", [256, D])
        d["peer_wq"] = di("peer_wq", [D, 2048])
        d["skT"] = di("skT", [128, 2048])
        d["peer_u"] = di("peer_u", [16384, D])
        d["peer_v"] = di("peer_v", [16384, D])
        if cfg.prompt:
            d["y"] = do("y", [cfg.NO, 128, D])
            d["kout"] = do("kout", [cfg.NO, 128, 256])
            d["vout"] = do("vout", [cfg.NO, 128, 256])
            d["kidxout"] = do("kidxout", [cfg.NO, 128, 64])
            d["stout"] = do("stout", [4, 256, 512])
        if cfg.sample:
            d["xs"] = di("xs", [128, D])
            d["ps"] = di("ps", [128, 256])
            d["cache_k"] = di("cache_k", [cfg.NPHYS * 128, 256])
            d["cache_v"] = di("cache_v", [cfg.NPHYS * 128, 256])
            d["cache_kidx"] = di("cache_kidx", [cfg.NPHYS, 8192])
            d["state_in"] = di("state_in", [cfg.NSEQ, 4, 256, 512])
            d["ptcol"] = di("ptcol", [cfg.NPAGE, cfg.NSEQ], I32)
            d["ptrow"] = di("ptrow", [128, cfg.NPAGE], I32)
            d["cst2"] = di("cst2", [128, NCS2])
            d["ys"] = do("ys", [128, D])
            d["ks"] = do("ks", [128, 256])
            d["vs"] = do("vs", [128, 256])
            d["kidxs"] = do("kidxs", [128, 64])
            d["state_out"] = do("state_out", [cfg.NSEQ, 4, 256, 512])
        self.wsc = nc.dram_tensor("wsc", [len(self.chunks), 128, 4096], BF16).ap()
        self.pub = nc.dram_tensor("pub", [16384, D], BF16).ap()
        self.pvb = nc.dram_tensor("pvb", [16384, D], BF16).ap()

    def sb(self, name, shape, dtype, stack=None):
        return (stack or self.st).enter_context(self.nc.sbuf_tensor("s_" + name, list(shape), dtype))

    def ps(self, name, shape, dtype):
        return self.st.enter_context(self.nc.psum_tensor("p_" + name, list(shape), dtype))

    def c(self, name, lo=0, n=None):
        o, sz = CST[name]
        n = sz - lo if n is None else n
        return self.cst[:, o + lo:o + lo + n]

    def load_chunk(self, ci):
        S = self.S
        ch = self.chunks[ci]
        n = ch["nkt"] * ch["nc"]
        slot = self.wr_next % len(self.wring)
        self.wr_next += 1
        wt = self.wring[slot]
        S.op("sp", lambda e, wt=wt, ci=ci, n=n: e.dma_start(out=wt[:, 0:n], in_=self.wsc[ci, :, 0:n]),
             w=[wt], dma=True, dsem="wr%d" % slot)
        return wt

    def pmm_next(self):
        p = self.pmm[self.pmm_i % 2]
        self.pmm_i += 1
        return p

    def transposes(self, srcs, dst3, alt=0, dkey=None):
        S = self.S
        for g0 in range(0, len(srcs), 8):
            grp = srcs[g0:g0 + 8]
            pt = self.ptr[self.ptr_i % 2]
            self.ptr_i += 1
            ptb = pt[:].bitcast(BF16)
            for i, (ap, rkeys) in enumerate(grp):
                ccols = ap.shape[1]
                S.op("pe", lambda e, ap=ap, i=i, ptb=ptb, ccols=ccols: e.transpose(
                    out=ptb[0:ccols, i * 128:(i + 1) * 128], in_=ap, identity=self.identb[:]),
                    r=list(rkeys) + [self.identb], w=[pt])
            ccols = grp[0][0].shape[1]
            n = len(grp)
            dview = dst3[0:ccols, g0:g0 + n, :]
            sview = ptb[0:ccols, 0:n * 128].rearrange("p (a b) -> p a b", a=n)
            eng = "act" if (alt + g0 // 8) % 2 == 0 else "dve"
            if eng == "act":
                S.op("act", lambda e, dview=dview, sview=sview: e.copy(out=dview, in_=sview), r=[pt], w=[dkey or dst3])
            else:
                S.op("dve", lambda e, dview=dview, sview=sview: e.tensor_copy(out=dview, in_=sview), r=[pt], w=[dkey or dst3])

    def rope(self, src3, dst3, H, half, cos, sin, rkeys, wkeys):
        S = self.S
        n = H * half
        R = [r[:, 0:n].rearrange("p (h f) -> p h f", h=H) for r in self.R]
        cb = cos.unsqueeze(1).to_broadcast([128, H, half])
        sbb = sin.unsqueeze(1).to_broadcast([128, H, half])
        x1 = src3[:, :, 0:half]
        x2 = src3[:, :, half:2 * half]
        M = ALU.mult
        S.op("dve", lambda e: e.tensor_tensor(out=R[0], in0=x1, in1=cb, op=M), r=rkeys + [self.cs], w=[self.R[0]])
        S.op("pool", lambda e: e.tensor_tensor(out=R[1], in0=x2, in1=sbb, op=M), r=rkeys + [self.cs], w=[self.R[1]])
        S.op("dve", lambda e: e.tensor_tensor(out=R[2], in0=x2, in1=cb, op=M), r=rkeys + [self.cs], w=[self.R[2]])
        S.op("pool", lambda e: e.tensor_tensor(out=R[3], in0=x1, in1=sbb, op=M), r=rkeys + [self.cs], w=[self.R[3]])
        S.op("dve", lambda e: e.tensor_tensor(out=dst3[:, :, 0:half], in0=R[0], in1=R[1], op=ALU.subtract),
             r=[self.R[0], self.R[1]], w=wkeys)
        S.op("dve", lambda e: e.tensor_tensor(out=dst3[:, :, half:2 * half], in0=R[2], in1=R[3], op=ALU.add),
             r=[self.R[2], self.R[3]], w=wkeys)

    def layer_norm(self, src, dst, n, g, b, rkeys, wkeys, gkeys):
        S = self.S
        st = self.lnst
        S.op("dve", lambda e: e.reduce_sum(out=st[:, 0:1], in_=src, axis=AX.X), r=rkeys, w=[(st, 0)])
        S.op("act", lambda e: e.activation(out=self.junka[:, 0:n], in_=src, func=AF.Square, accum_out=st[:, 1:2]),
             r=rkeys, w=[(st, 1), self.junka])
        S.op("dve", lambda e: e.tensor_scalar(out=st[:, 2:3], in0=st[:, 0:1], scalar1=1.0 / n, scalar2=None, op0=ALU.mult),
             r=[(st, 0)], w=[(st, 2)])
        S.op("dve", lambda e: e.tensor_tensor(out=st[:, 3:4], in0=st[:, 2:3], in1=st[:, 2:3], op=ALU.mult),
             r=[(st, 2)], w=[(st, 3)])
        S.op("dve", lambda e: e.scalar_tensor_tensor(out=st[:, 4:5], in0=st[:, 1:2], scalar=1.0 / n, in1=st[:, 3:4],
                                                     op0=ALU.mult, op1=ALU.subtract),
             r=[(st, 1), (st, 3)], w=[(st, 4)])
        S.op("dve", lambda e: e.tensor_scalar(out=st[:, 4:5], in0=st[:, 4:5], scalar1=LN_EPS, scalar2=None, op0=ALU.add),
             r=[(st, 4)], w=[(st, 4)])
        S.op("act", lambda e: e.activation(out=st[:, 5:6], in_=st[:, 4:5], func=AF.Sqrt), r=[(st, 4)], w=[(st, 5)])
        S.op("dve", lambda e: e.reciprocal(out=st[:, 6:7], in_=st[:, 5:6]), r=[(st, 5)], w=[(st, 6)])
        S.op("dve", lambda e: e.tensor_scalar(out=dst, in0=src, scalar1=st[:, 2:3], scalar2=st[:, 6:7],
                                              op0=ALU.subtract, op1=ALU.mult),
             r=rkeys + [(st, 2), (st, 6)], w=wkeys)
        if g is not None:
            S.op("dve", lambda e: e.tensor_tensor(out=dst, in0=dst, in1=g, op=ALU.mult), r=wkeys + gkeys, w=wkeys)
            S.op("dve", lambda e: e.tensor_tensor(out=dst, in0=dst, in1=b, op=ALU.add), r=wkeys + gkeys, w=wkeys)

    def phase_prep(self):
        S, d = self.S, self.d
        with ExitStack() as st2:
            sf = [self.sb("prep_f%d" % i, [128, 4096], F32, st2) for i in range(4)]
            sbf = [self.sb("prep_b%d" % i, [128, 4096], BF16, st2) for i in range(4)]
            for ci, ch in enumerate(self.chunks):
                f, bt = sf[ci % 4], sbf[ci % 4]
                n = ch["nkt"] * ch["nc"]
                if ch["src"] == "skT":
                    src = d["skT"]
                    dstv = f[:, 0:n]
                else:
                    src = d[ch["src"]][ch["r0"]:ch["r0"] + ch["nkt"] * 128, ch["c0"]:ch["c0"] + ch["nc"]].rearrange(
                        "(kt p) n -> p kt n", p=128)
                    dstv = f[:, 0:n].rearrange("p (kt n) -> p kt n", kt=ch["nkt"])
                S.op("sp", lambda e, dstv=dstv, src=src: e.dma_start(out=dstv, in_=src), w=[f], dma=True)
                if ci % 2 == 0:
                    S.op("act", lambda e, bt=bt, f=f, n=n: e.copy(out=bt[:, 0:n], in_=f[:, 0:n]), r=[f], w=[bt])
                else:
                    S.op("dve", lambda e, bt=bt, f=f, n=n: e.tensor_copy(out=bt[:, 0:n], in_=f[:, 0:n]), r=[f], w=[bt])
                S.op("pool", lambda e, bt=bt, ci=ci, n=n: e.dma_start(out=self.wsc[ci, :, 0:n], in_=bt[:, 0:n]),
                     r=[bt], w=["wsc"], dma=True)
            k = len(self.chunks)
            for (srcn, dst) in (("peer_u", self.pub), ("peer_v", self.pvb)):
                sv = d[srcn].rearrange("(p a) n -> p (a n)", p=128)
                dv = dst.rearrange("(p a) n -> p (a n)", p=128)
                for c in range(32):
                    f, bt = sf[k % 4], sbf[k % 4]
                    S.op("sp", lambda e, f=f, sv=sv, c=c: e.dma_start(out=f[:], in_=sv[:, c * 4096:(c + 1) * 4096]), w=[f], dma=True)
                    if k % 2 == 0:
                        S.op("act", lambda e, bt=bt, f=f: e.copy(out=bt[:], in_=f[:]), r=[f], w=[bt])
                    else:
                        S.op("dve", lambda e, bt=bt, f=f: e.tensor_copy(out=bt[:], in_=f[:]), r=[f], w=[bt])
                    S.op("pool", lambda e, bt=bt, dv=dv, c=c: e.dma_start(out=dv[:, c * 4096:(c + 1) * 4096], in_=bt[:]),
                         r=[bt], w=["ptab"], dma=True)
                    k += 1
            S.emit()

    def alloc_common(self):
        S, d = self.S, self.d
        sb, ps = self.sb, self.ps
        self.cst = sb("cst", [128, NCST], F32)
        self.identb = sb("identb", [128, 128], BF16)
        self.ARENA = sb("ARENA", [128, 16384], F32)
        self._ar_off = 0

        def carve(name, shape, dtype):
            n = int(np.prod(shape[1:]))
            n32 = n if dtype in (F32, I32, U32) else n // 2
            ap = self.ARENA[:, self._ar_off:self._ar_off + n32]
            self._ar_off += n32
            assert self._ar_off <= 16384
            if dtype != F32:
                ap = ap.bitcast(dtype)
            if len(shape) == 3:
                ap = ap.rearrange("p (a b) -> p a b", a=shape[1])
            return TV("v_" + name, ap)
        self.carve = carve
        self.wring = [carve("wring%d" % i, [128, 4096], BF16) for i in range(2)]
        self.wr_next = 0
        self.pmm = [ps("pmm%d" % i, [128, 512], F32) for i in range(2)]
        self.ptr = [ps("ptr%d" % i, [128, 512], F32) for i in range(2)]
        self.pS = [ps("pS%d" % i, [128, 512], F32) for i in range(2)]
        self.pO = [ps("pO%d" % i, [128, 512], F32) for i in range(2)]
        self.pmm_i = self.ptr_i = self.pS_i = 0
        self.junka = sb("junka", [128, 1024], BF16)
        self.junkd = sb("junkd", [128, 1024], BF16)
        self.lnst = sb("lnst", [128, 8], F32)
        self.R = [carve("R%d" % i, [128, 512], F32) for i in range(4)]
        self.F = [carve("F%d" % i, [128, 512], F32) for i in range(2)]
        self.cs = sb("cs", [128, 448], F32)
        self.X1 = sb("X1", [128, D], F32)
        self.XY = sb("XY", [128, D], F32)
        self.xblk = self.XY
        self.Y = self.XY
        self.stA = self.X1
        self.stB = self.XY
        self.xT = sb("xT", [128, 8, 128], BF16)
        self.TT2 = self.xT
        self.hkv = sb("hkv", [128, 512], F32)
        self.pblk = self.hkv
        self.hikw = sb("hikw", [128, 72], F32)
        self.rv_bf = sb("rv_bf", [128, 4, 512], BF16)
        self.sgate = sb("sgate", [128, 2048], BF16)
        self.ret_o = self.sgate
        self.sga = sb("sga", [128, D], BF16)
        self.sgb = sb("sgb", [128, D], BF16)
        self.B = [carve("B%d" % i, [128, 1024], BF16) for i in range(4)]
        self.akf = sb("akf", [128, 256], F32)
        self.ikf = sb("ikf", [128, 64], F32)
        self.ikd = sb("ikd", [128, 128], BF16)
        self.RQK = sb("RQK", [128, 16, 128], BF16)
        self.rqT = self.RQK[:, 0:8, :]
        self.rkT = self.RQK[:, 8:16, :]
        self.qT = self.RQK
        self.aqT = sb("aqT", [128, 8, 128], BF16)
        self.iqT = sb("iqT", [128, 4, 128], BF16)
        self.PT = [sb("PT%d" % i, [128, 128], BF16) for i in range(2)]
        self.qd = sb("qd", [128, 2, 128], BF16)
        self.sbf = sb("sbf", [128, 2, 512], BF16)
        self.kd = sb("kd", [128, 256], BF16)
        self.att_o = sb("att_o", [128, 8, 128], BF16)
        self.TT = carve("TT", [128, 16, 128], BF16)
        self.GI = carve("GI", [128, 4096], F32)
        self.ppT = sb("ppT", [128, 2, 128], BF16)
        self.wsm = sb("wsm", [128, 32], F32)
        self.bis = sb("bis", [128, 16], F32)
        self.cnt4 = sb("cnt4", [128, 16], F32)
        self.Ebuf = [sb("Ebuf%d" % i, [128, 512], BF16) for i in range(2)]
        self.Pbuf = [sb("Pbuf%d" % i, [128, 4, 128], BF16) for i in range(2)]
        self.mch = sb("mch", [128, 512], BF16)
        self.rsum = sb("rsum", [128, 8], F32)
        self.V16 = sb("V16", [128, 16, 16], F32)
        self.I16 = sb("I16", [128, 16, 16], U32)
        self.TMPC = sb("TMPC", [128, 256], F32)
        self.S16 = sb("S16", [128, 8, 16], F32)
        self.SEL = sb("SEL", [128, 8, 16], U32)
        self.pm = [sb("pm%d" % i, [128, 128], F32) for i in range(12)]
        self.pmi = [sb("pmi%d" % i, [128, 128], I32) for i in range(3)]
        ur_off = self._ar_off
        self.UR = [carve("UR%d" % i, [128, D], BF16) for i in range(4)] + [sb("URx%d" % i, [128, D], BF16) for i in range(2)]
        self.wring.append(TV("v_wring2", self.ARENA[:, ur_off:ur_off + 2048].bitcast(BF16),
                             alias_keys=["v_UR%d" % i for i in range(4)]))
        self.VR = self.UR
        self.Zb = [sb("Zb%d" % i, [128, 256], BF16) for i in range(2)]
        for i in range(2):
            S.op("pool", lambda e, i=i: e.memset(self.Zb[i][:], 0.0), w=[self.Zb[i]])
        S.op("sp", lambda e: e.dma_start(out=self.cst[:], in_=d["cst"]), w=[self.cst], dma=True)
        S.op("dve", lambda e: e.tensor_copy(out=self.identb[:], in_=self.c("ident")), r=[self.cst], w=[self.identb])

    def inproj(self, light):
        S = self.S
        xT = self.xT
        for (g, c0, n) in GROUPS:
            if light and g not in LIGHT_GROUPS:
                continue
            for k, ci in enumerate(self.cidx["in_" + g]):
                ch = self.chunks[ci]
                ncol = ch["nc"]
                wt = self.load_chunk(ci)
                pm = self.pmm_next()
                for kt in range(8):
                    S.op("pe", lambda e, pm=pm, wt=wt, kt=kt, ncol=ncol: e.matmul(
                        pm[:, 0:ncol], lhsT=xT[:, kt, :], rhs=wt[:, kt * ncol:(kt + 1) * ncol],
                        start=(kt == 0), stop=(kt == 7)), r=[xT, wt], w=[pm])
                sl = slice(k * 512, k * 512 + ncol)
                if g in ("rq", "aq"):
                    S.op("act", lambda e, pm=pm, sl=sl, ncol=ncol: e.copy(out=self.stA[:, sl], in_=pm[:, 0:ncol]), r=[pm], w=[self.stA])
                elif g in ("rk", "iq"):
                    S.op("act", lambda e, pm=pm, sl=sl, ncol=ncol: e.copy(out=self.stB[:, sl], in_=pm[:, 0:ncol]), r=[pm], w=[self.stB])
                elif g == "rv":
                    S.op("act", lambda e, pm=pm, k=k: e.copy(out=self.rv_bf[:, k, :], in_=pm[:, 0:512]), r=[pm], w=[self.rv_bf])
                elif g == "rg":
                    S.op("act", lambda e, pm=pm, sl=sl: e.activation(out=self.sgate[:, sl], in_=pm[:, 0:512], func=AF.Silu),
                         r=[pm], w=[(self.sgate, k)])
                elif g == "kv":
                    S.op("act", lambda e, pm=pm: e.copy(out=self.hkv[:], in_=pm[:, 0:512]), r=[pm], w=[self.hkv])
                elif g == "ikw":
                    S.op("act", lambda e, pm=pm: e.copy(out=self.hikw[:], in_=pm[:, 0:72]), r=[pm], w=[self.hikw])
                elif g == "ga":
                    S.op("act", lambda e, pm=pm, sl=sl: e.activation(out=self.sga[:, sl], in_=pm[:, 0:512], func=AF.Sigmoid),
                         r=[pm], w=[self.sga])
                elif g == "gb":
                    S.op("act", lambda e, pm=pm, sl=sl: e.activation(out=self.sgb[:, sl], in_=pm[:, 0:512], func=AF.Sigmoid),
                         r=[pm], w=[self.sgb])
            self.post_group(g, light)

    def post_group(self, g, light):
        S = self.S
        cs = self.cs
        cos256, sin256 = cs[:, 0:128], cs[:, 224:352]
        cos128, sin128 = cs[:, 128:192], cs[:, 352:416]
        cos64, sin64 = cs[:, 192:224], cs[:, 416:448]
        if g == "rq":
            self.rope(self.stA[:].rearrange("p (h f) -> p h f", h=4), self.B[0][:].rearrange("p (h f) -> p h f", h=4),
                      4, 128, cos256, sin256, [self.stA], [self.B[0]])
            srcs = [(self.B[0][:, i * 128:(i + 1) * 128], [self.B[0]]) for i in range(8)]
            self.transposes(srcs, self.rqT, 0)
        elif g == "rk":
            self.rope(self.stB[:].rearrange("p (h f) -> p h f", h=4), self.B[1][:].rearrange("p (h f) -> p h f", h=4),
                      4, 128, cos256, sin256, [self.stB], [self.B[1]])
            if not light:
                srcs = [(self.B[1][:, i * 128:(i + 1) * 128], [self.B[1]]) for i in range(8)]
                self.transposes(srcs, self.rkT, 1)
        elif g == "aq":
            self.rope(self.stA[:].rearrange("p (h f) -> p h f", h=8), self.B[2][:].rearrange("p (h f) -> p h f", h=8),
                      8, 64, cos128, sin128, [self.stA], [self.B[2]])
            srcs = [(self.B[2][:, i * 128:(i + 1) * 128], [self.B[2]]) for i in range(8)]
            self.transposes(srcs, self.aqT, 0)
        elif g == "kv":
            bi = self.cur_blk
            self.rope(self.hkv[:, 0:256].rearrange("p (h f) -> p h f", h=2), self.akf[:].rearrange("p (h f) -> p h f", h=2),
                      2, 64, cos128, sin128, [self.hkv], [self.akf])
            akb = self.B[3][:, 0:256]
            S.op("act", lambda e: e.copy(out=akb, in_=self.akf[:]), r=[self.akf], w=[(self.B[3], "ak")])
            srcs = [(self.B[3][:, i * 128:(i + 1) * 128], [(self.B[3], "ak")]) for i in range(2)]
            self.transposes_into(srcs, self.KT_dst(bi), self.KT_key(bi))
            vd = self.V_dst(bi)
            S.op("act", lambda e: e.copy(out=vd[:, :, 0:128], in_=self.hkv[:, 256:512].rearrange("p (g f) -> p g f", g=2)),
                 r=[self.hkv], w=[self.V_key(bi)])
            S.op("pool", lambda e: e.memset(vd[:, :, 128:129], 1.0), w=[(self.V_key(bi), "one")])
            if self.out_k is not None:
                ok_, ov_ = self.out_k, self.out_v
                S.op("pool", lambda e: e.dma_start(out=ok_, in_=self.akf[:]), r=[self.akf], dma=True)
                S.op("pool", lambda e: e.dma_start(out=ov_, in_=self.hkv[:, 256:512]), r=[self.hkv], dma=True)
        elif g == "iq":
            self.rope(self.stB[:, 0:512].rearrange("p (h f) -> p h f", h=8), self.stA[:, 0:512].rearrange("p (h f) -> p h f", h=8),
                      8, 32, cos64, sin64, [self.stB], [self.stA])
        elif g == "ikw":
            bi = self.cur_blk
            self.layer_norm(self.hikw[:, 0:64], self.F[0][:, 0:64], 64, self.c("ikg"), self.c("ikb"),
                            [self.hikw], [self.F[0]], [self.cst])
            self.rope(self.F[0][:, 0:64].rearrange("p (h f) -> p h f", h=1), self.ikf[:].rearrange("p (h f) -> p h f", h=1),
                      1, 32, cos64, sin64, [self.F[0]], [self.ikf])
            S.op("act", lambda e: e.copy(out=self.ikd[:, 0:64], in_=self.ikf[:]), r=[self.ikf], w=[(self.ikd, 0)])
            S.op("act", lambda e: e.copy(out=self.ikd[:, 64:128], in_=self.ikf[:]), r=[self.ikf], w=[(self.ikd, 1)])
            self.transposes_into([(self.ikd[:], [(self.ikd, 0), (self.ikd, 1)])], self.kidxT_dst(bi), self.kidxT_key(bi))
            if self.out_kidx is not None:
                oki_ = self.out_kidx
                S.op("pool", lambda e: e.dma_start(out=oki_, in_=self.ikf[:]), r=[self.ikf], dma=True)
            if not light:
                w8 = self.hikw[:, 64:72]
                wsm = self.wsm
                S.op("act", lambda e: e.activation(out=wsm[:, 0:8], in_=w8, func=AF.Abs, scale=8.0 ** -0.5),
                     r=[self.hikw], w=[(wsm, "aw")])
                S.op("act", lambda e: e.activation(out=wsm[:, 8:16], in_=w8, func=AF.Sign), r=[self.hikw], w=[(wsm, "sg")])
                iqb = self.B[3][:, 256:768]
                S.op("dve", lambda e: e.tensor_tensor(
                    out=iqb.rearrange("p (h f) -> p h f", h=8), in0=self.stA[:, 0:512].rearrange("p (h f) -> p h f", h=8),
                    in1=wsm[:, 0:8].unsqueeze(2).to_broadcast([128, 8, 64]), op=ALU.mult),
                    r=[self.stA, (wsm, "aw")], w=[(self.B[3], "iq")])
                srcs = [(self.B[3][:, 256 + i * 128:256 + (i + 1) * 128], [(self.B[3], "iq")]) for i in range(4)]
                self.transposes(srcs, self.iqT, 1)

    def transposes_into(self, srcs, dst3, dkey):
        S = self.S
        pt = self.ptr[self.ptr_i % 2]
        self.ptr_i += 1
        ptb = pt[:].bitcast(BF16)
        for i, (ap, rkeys) in enumerate(srcs):
            ccols = ap.shape[1]
            S.op("pe", lambda e, ap=ap, i=i, ccols=ccols: e.transpose(
                out=ptb[0:ccols, i * 128:(i + 1) * 128], in_=ap, identity=self.identb[:]),
                r=list(rkeys) + [self.identb], w=[pt])
        n = len(srcs)
        ccols = srcs[0][0].shape[1]
        sview = ptb[0:ccols, 0:n * 128].rearrange("p (a b) -> p a b", a=n)
        S.op("dve", lambda e: e.tensor_copy(out=dst3[0:ccols], in_=sview), r=[pt], w=[dkey])

    def KT_dst(self, bi):
        return self.KT[:, bi, :, :]

    def KT_key(self, bi):
        return (self.KT, bi)

    def V_dst(self, bi):
        return self.V[:, bi, :, :]

    def V_key(self, bi):
        return (self.V, bi)

    def kidxT_dst(self, bi):
        return self.kidxT[:, bi * 128:(bi + 1) * 128].unsqueeze(1)

    def kidxT_key(self, bi):
        return (self.kidxT, bi)

    def block_front(self, xsrc, csi):
        S, d = self.S, self.d
        self.xsrc = xsrc
        S.op("sp", lambda e: e.dma_start(out=self.xblk[:], in_=xsrc), w=[self.xblk], dma=True)
        S.op("sp", lambda e: e.dma_start(out=self.cs[:], in_=d["csb"][csi]), w=[self.cs], dma=True)
        xb16 = self.B[2]
        S.op("act", lambda e: e.copy(out=xb16[:], in_=self.xblk[:]), r=[self.xblk], w=[xb16])
        srcs = [(xb16[:, i * 128:(i + 1) * 128], [xb16]) for i in range(8)]
        self.transposes(srcs, self.xT, 0)

    def retention(self, owned, decayT, rowdec, kdec, gC, state_f32, skey, sample_b=None):
        S = self.S
        rqT, rkT, rv_bf = self.rqT, self.rkT, self.rv_bf
        rk_r = self.B[1]
        if owned:
            for h in range(4):
                pSx = self.pS[self.pS_i % 2]
                self.pS_i += 1
                for kt in range(2):
                    S.op("pe", lambda e, pSx=pSx, h=h, kt=kt: e.matmul(
                        pSx[:, 0:128], lhsT=rkT[:, 2 * h + kt, :], rhs=rqT[:, 2 * h + kt, :], start=(kt == 0), stop=(kt == 1)),
                        r=[rkT, rqT], w=[pSx])
                PTx = self.PT[h % 2]
                S.op("dve", lambda e, pSx=pSx, PTx=PTx, h=h: e.tensor_tensor(
                    out=PTx[:], in0=pSx[:, 0:128], in1=decayT[:, h * 128:(h + 1) * 128], op=ALU.mult),
                    r=[pSx, self.cst], w=[PTx])
                S.op("dve", lambda e, h=h: e.tensor_tensor(
                    out=self.qd[:], in0=rqT[:, 2 * h:2 * h + 2, :],
                    in1=rowdec[:, h * 128:(h + 1) * 128].unsqueeze(1).to_broadcast([128, 2, 128]), op=ALU.mult),
                    r=[rqT, self.cst], w=[self.qd])
                S.op("act", lambda e, h=h: e.copy(out=self.sbf[:], in_=state_f32[:, h, :, :]), r=[(skey, h)], w=[self.sbf])
                pm = self.pmm_next()
                S.op("pe", lambda e, pm=pm, PTx=PTx, h=h: e.matmul(pm[:], lhsT=PTx[:], rhs=rv_bf[:, h, :], start=True, stop=False),
                     r=[PTx, rv_bf], w=[pm])
                for kt in range(2):
                    S.op("pe", lambda e, pm=pm, kt=kt: e.matmul(pm[:], lhsT=self.qd[:, kt, :], rhs=self.sbf[:, kt, :],
                                                                start=False, stop=(kt == 1)),
                         r=[self.qd, self.sbf], w=[pm])
                yh = self.F[h % 2]
                g = self.GI[:, h * 512:(h + 1) * 512]
                b = self.GI[:, 2048 + h * 512:2048 + (h + 1) * 512]
                S.op("act", lambda e, pm=pm, yh=yh: e.copy(out=yh[:], in_=pm[:]), r=[pm], w=[yh])
                self.layer_norm(yh[:], yh[:], 512, g, b, [yh], [yh], [self.GI])
                S.op("dve", lambda e, yh=yh, h=h: e.tensor_tensor(
                    out=self.ret_o[:, h * 512:(h + 1) * 512], in0=yh[:], in1=self.sgate[:, h * 512:(h + 1) * 512], op=ALU.mult),
                    r=[yh, (self.sgate, h)], w=[(self.sgate, h)])
        for h in range(4):
            S.op("dve", lambda e, h=h: e.tensor_scalar(
                out=self.kd[:], in0=rk_r[:, h * 256:(h + 1) * 256], scalar1=kdec[:, h:h + 1], scalar2=None, op0=ALU.mult),
                r=[rk_r, self.cst], w=[self.kd])
            for kt in range(2):
                pm = self.pmm_next()
                S.op("pe", lambda e, pm=pm, h=h, kt=kt: e.matmul(
                    pm[:], lhsT=self.kd[:, kt * 128:(kt + 1) * 128], rhs=rv_bf[:, h, :], start=True, stop=True),
                    r=[self.kd, rv_bf], w=[pm])
                S.op("dve", lambda e, pm=pm, h=h, kt=kt: e.scalar_tensor_tensor(
                    out=state_f32[:, h, kt, :], in0=state_f32[:, h, kt, :], scalar=float(gC[h]), in1=pm[:],
                    op0=ALU.mult, op1=ALU.add), r=[pm, (skey, h)], w=[(skey, h)])

    def attention(self, bi):
        S, cfg = self.S, self.cfg
        nk = bi + 1
        N = nk * 128
        GI, wsm, bis = self.GI, self.wsm, self.bis
        nkc = (nk + 3) // 4
        for c in range(nkc):
            ncols = min(512, N - c * 512)
            kb0 = c * 4
            kkeys = [self.kidxT_key(kb) for kb in range(kb0, min(nk, kb0 + 4))]
            for hd in range(8):
                hp, s = hd // 2, hd % 2
                pSx = self.pS[self.pS_i % 2]
                self.pS_i += 1
                S.op("pe", lambda e, pSx=pSx, hp=hp, s=s, c=c, ncols=ncols: e.matmul(
                    pSx[:, 0:ncols], lhsT=self.iqT[64 * s:64 * s + 64, hp, :],
                    rhs=self.kidxT[64 * s:64 * s + 64, c * 512:c * 512 + ncols], start=True, stop=True),
                    r=[self.iqT] + kkeys, w=[pSx])
                Fx = self.F[hd % 2]
                S.op("act", lambda e, pSx=pSx, Fx=Fx, ncols=ncols: e.activation(out=Fx[:, 0:ncols], in_=pSx[:, 0:ncols], func=AF.Relu),
                     r=[pSx], w=[Fx])
                gsl = GI[:, c * 512:c * 512 + ncols]
                if hd == 0:
                    S.op("dve", lambda e, Fx=Fx, gsl=gsl, ncols=ncols: e.tensor_scalar(
                        out=gsl, in0=Fx[:, 0:ncols], scalar1=wsm[:, 8:9], scalar2=None, op0=ALU.mult),
                        r=[Fx, (wsm, "sg")], w=[GI])
                else:
                    S.op("dve", lambda e, Fx=Fx, gsl=gsl, ncols=ncols, hd=hd: e.scalar_tensor_tensor(
                        out=gsl, in0=Fx[:, 0:ncols], scalar=wsm[:, 8 + hd:9 + hd], in1=gsl, op0=ALU.mult, op1=ALU.add),
                        r=[Fx, (wsm, "sg"), GI], w=[GI])
        S.op("dve", lambda e: e.tensor_reduce(out=bis[:, 0:1], in_=GI[:, 0:N], axis=AX.X, op=ALU.max, apply_absolute_value=True),
             r=[GI], w=[bis])
        S.op("dve", lambda e: e.tensor_scalar(out=bis[:, 1:2], in0=bis[:, 0:1], scalar1=1.0, scalar2=None, op0=ALU.add),
             r=[bis], w=[bis])
        S.op("dve", lambda e: e.tensor_scalar(out=bis[:, 2:3], in0=bis[:, 1:2], scalar1=-1.0, scalar2=None, op0=ALU.mult),
             r=[bis], w=[bis])
        S.op("dve", lambda e: e.memset(bis[:, 3:4], 0.0), w=[bis])
        if cfg.NL > 0:
            S.op("dve", lambda e: e.tensor_scalar(out=GI[:, 0:cfg.NL * 128], in0=GI[:, 0:cfg.NL * 128],
                                                  scalar1=self.c("lightbias"), scalar2=None, op0=ALU.add),
                 r=[GI, self.cst], w=[GI])
        S.op("dve", lambda e: e.tensor_tensor(out=GI[:, bi * 128:N], in0=GI[:, bi * 128:N], in1=self.c("causal"), op=ALU.add),
             r=[GI, self.cst], w=[GI])
        nch = (N + 1023) // 1024
        thr = float(N - 2 * cfg.TOPK)
        for it in range(NIT):
            for ch in range(nch):
                n = min(1024, N - ch * 1024)
                S.op("act", lambda e, ch=ch, n=n: e.activation(
                    out=self.junka[:, 0:n], in_=GI[:, ch * 1024:ch * 1024 + n], func=AF.Sign, scale=-1.0, bias=bis[:, 3:4],
                    accum_out=self.cnt4[:, ch:ch + 1]), r=[GI, bis], w=[(self.cnt4, ch), self.junka])
            ckeys = [(self.cnt4, ch) for ch in range(nch)]
            if nch > 1:
                S.op("dve", lambda e: e.reduce_sum(out=bis[:, 4:5], in_=self.cnt4[:, 0:nch], axis=AX.X), r=ckeys, w=[bis])
                ssrc = bis[:, 4:5]
            else:
                ssrc = self.cnt4[:, 0:1]
            S.op("dve", lambda e, ssrc=ssrc: e.tensor_scalar(out=bis[:, 5:6], in0=ssrc, scalar1=thr, scalar2=None, op0=ALU.is_le),
                 r=ckeys + [bis], w=[bis])
            S.op("dve", lambda e: e.tensor_tensor(out=bis[:, 6:7], in0=bis[:, 3:4], in1=bis[:, 2:3], op=ALU.subtract), r=[bis], w=[bis])
            S.op("dve", lambda e: e.tensor_tensor(out=bis[:, 7:8], in0=bis[:, 1:2], in1=bis[:, 3:4], op=ALU.subtract), r=[bis], w=[bis])
            S.op("dve", lambda e: e.scalar_tensor_tensor(out=bis[:, 2:3], in0=bis[:, 6:7], scalar=bis[:, 5:6], in1=bis[:, 2:3],
                                                         op0=ALU.mult, op1=ALU.add), r=[bis], w=[bis])
            S.op("dve", lambda e: e.scalar_tensor_tensor(out=bis[:, 1:2], in0=bis[:, 7:8], scalar=bis[:, 5:6], in1=bis[:, 3:4],
                                                         op0=ALU.mult, op1=ALU.add), r=[bis], w=[bis])
            S.op("dve", lambda e: e.tensor_scalar(out=bis[:, 3:4], in0=bis[:, 2:3], scalar1=bis[:, 1:2], scalar2=0.5,
                                                  op0=ALU.add, op1=ALU.mult), r=[bis], w=[bis])
        for c in range(nkc):
            ncols = min(512, N - c * 512)
            nb = ncols // 128
            S.op("dve", lambda e, c=c, ncols=ncols: e.tensor_scalar(
                out=self.mch[:, 0:ncols], in0=GI[:, c * 512:c * 512 + ncols], scalar1=bis[:, 2:3], scalar2=None, op0=ALU.is_ge),
                r=[GI, bis], w=[self.mch])
            srcs = [(self.mch[:, i * 128:(i + 1) * 128], [self.mch]) for i in range(nb)]
            self.transposes_into(srcs, self.mT[:, c * 4:c * 4 + nb, :], (self.mT, c))
        for g in range(2):
            for kb in range(nk):
                pSx = self.pS[self.pS_i % 2]
                self.pS_i += 1
                S.op("pe", lambda e, pSx=pSx, kb=kb, g=g: e.matmul(
                    pSx[:], lhsT=self.KT[:, kb, g, :], rhs=self.aqT[:, 4 * g:4 * g + 4, :], start=True, stop=True),
                    r=[self.KT_key(kb), self.aqT], w=[pSx])
                Ex = self.Ebuf[kb % 2]
                S.op("act", lambda e, pSx=pSx, Ex=Ex: e.activation(out=Ex[:], in_=pSx[:], func=AF.Exp, scale=128.0 ** -0.5),
                     r=[pSx], w=[Ex])
                Px = self.Pbuf[kb % 2]
                S.op("dve", lambda e, Ex=Ex, Px=Px, kb=kb: e.tensor_tensor(
                    out=Px[:], in0=Ex[:].rearrange("p (r t) -> p r t", r=4),
                    in1=self.mT[:, kb, :].unsqueeze(1).to_broadcast([128, 4, 128]), op=ALU.mult),
                    r=[Ex, (self.mT, kb // 4)], w=[Px])
                for r in range(4):
                    pOx = self.pO[r // 2]
                    S.op("pe", lambda e, pOx=pOx, Px=Px, r=r, kb=kb, g=g: e.matmul(
                        pOx[:, (r % 2) * 129:(r % 2) * 129 + 129], lhsT=Px[:, r, :], rhs=self.V[:, kb, g, :],
                        start=(kb == 0 and r % 2 == 0), stop=(kb == nk - 1), skip_group_check=True),
                        r=[Px, self.V_key(kb), (self.V_key(kb), "one")], w=[pOx])
            for r in range(4):
                pOx = self.pO[r // 2]
                o0 = (r % 2) * 129
                hd = 4 * g + r
                S.op("dve", lambda e, pOx=pOx, o0=o0, hd=hd: e.reciprocal(out=self.rsum[:, hd:hd + 1], in_=pOx[:, o0 + 128:o0 + 129]),
                     r=[pOx], w=[(self.rsum, hd)])
                S.op("dve", lambda e, pOx=pOx, o0=o0, hd=hd: e.tensor_scalar(
                    out=self.att_o[:, hd, :], in0=pOx[:, o0:o0 + 128], scalar1=self.rsum[:, hd:hd + 1], scalar2=None, op0=ALU.mult),
                    r=[pOx, (self.rsum, hd)], w=[(self.att_o, hd)])

    def mm_chunk(self, name, k, lhsT3, lkeys, nkt=8, first=True, last=True, pm=None, rhs_off=0, rhs_w=512):
        S = self.S
        ci = self.cidx[name][k]
        ncol = self.chunks[ci]["nc"]
        wt = self.load_chunk(ci)
        if pm is None:
            pm = self.pmm_next()
        for kt in range(nkt):
            S.op("pe", lambda e, pm=pm, wt=wt, kt=kt: e.matmul(
                pm[:, 0:rhs_w], lhsT=lhsT3[:, kt, :], rhs=wt[:, kt * ncol + rhs_off:kt * ncol + rhs_off + rhs_w],
                start=(first and kt == 0), stop=(last and kt == nkt - 1)), r=lkeys + [wt], w=[pm])
        return pm

    def load_params(self, col0, n, dst):
        S, d = self.S, self.d
        S.op("sp", lambda e: e.dma_start(out=dst, in_=d["params"][:, col0:col0 + n]), w=[self.GI], dma=True)

    def finish(self, psrc, ydst):
        S, d = self.S, self.d
        GI = self.GI
        srcs = [(self.ret_o[:, i * 128:(i + 1) * 128], [(self.sgate, i // 4)]) for i in range(16)]
        self.transposes(srcs, self.TT, 0)
        srcs = [(self.att_o[:, i, :], [(self.att_o, i)]) for i in range(8)]
        self.transposes(srcs, self.TT2, 1)
        branch = self.B[0]
        for chh in range(2):
            sl = slice(chh * 512, (chh + 1) * 512)
            pm = self.mm_chunk("ret_o%d" % chh, 0, self.TT[:, 0:8, :], [self.TT], first=True, last=False)
            self.mm_chunk("ret_o%d" % chh, 1, self.TT[:, 8:16, :], [self.TT], first=False, last=True, pm=pm)
            S.op("dve", lambda e, pm=pm, sl=sl: e.tensor_tensor(out=self.F[0][:], in0=pm[:], in1=self.sga[:, sl], op=ALU.mult),
                 r=[pm, self.sga], w=[self.F[0]])
            pm2 = self.mm_chunk("att_o", chh, self.TT2, [self.TT2])
            S.op("dve", lambda e, pm2=pm2, sl=sl: e.tensor_tensor(out=self.F[1][:], in0=pm2[:], in1=self.sgb[:, sl], op=ALU.mult),
                 r=[pm2, self.sgb], w=[self.F[1]])
            S.op("dve", lambda e, sl=sl: e.tensor_tensor(out=branch[:, sl], in0=self.F[0][:], in1=self.F[1][:], op=ALU.add),
                 r=[self.F[0], self.F[1]], w=[branch])
        srcs = [(branch[:, i * 128:(i + 1) * 128], [branch]) for i in range(8)]
        self.transposes(srcs, self.TT2, 0)
        self.load_params(4096, 2048, GI[:, 0:2048])
        X1 = self.X1
        xsrc = self.xsrc
        S.op("sp", lambda e: e.dma_start(out=self.xblk[:], in_=xsrc), w=[self.xblk], dma=True)
        for chh in range(2):
            sl = slice(chh * 512, (chh + 1) * 512)
            pm = self.mm_chunk("out", chh, self.TT2, [self.TT2])
            S.op("dve", lambda e, pm=pm, sl=sl: e.scalar_tensor_tensor(
                out=X1[:, sl], in0=self.xblk[:, sl], scalar=ALPHA, in1=pm[:], op0=ALU.mult, op1=ALU.add),
                r=[pm, self.xblk], w=[X1])
        self.layer_norm(X1[:], X1[:], D, GI[:, 0:1024], GI[:, 1024:2048], [X1], [X1], [GI])
        x1b = self.B[2]
        S.op("act", lambda e: e.copy(out=x1b[:], in_=X1[:]), r=[X1], w=[x1b])
        srcs = [(x1b[:, i * 128:(i + 1) * 128], [x1b]) for i in range(8)]
        x1T = self.TT[:, 0:8, :]
        self.transposes(srcs, x1T, 1, dkey=self.TT)
        self.x1T = x1T
        S.op("sp", lambda e: e.dma_start(out=self.pblk[:, 0:256], in_=psrc), w=[self.pblk], dma=True)
        pb16 = self.B[0][:, 0:256]
        S.op("act", lambda e: e.copy(out=pb16, in_=self.pblk[:, 0:256]), r=[self.pblk], w=[self.B[0]])
        srcs = [(self.B[0][:, i * 128:(i + 1) * 128], [self.B[0]]) for i in range(2)]
        self.transposes(srcs, self.ppT, 0)
        Y = self.Y
        ci_ple = self.cidx["ple"][0]
        for chh in range(2):
            sl = slice(chh * 512, (chh + 1) * 512)
            pm = self.mm_chunk("pg", chh, x1T, [self.TT])
            S.op("act", lambda e, pm=pm: e.activation(out=self.F[0][:], in_=pm[:], func=AF.Sigmoid), r=[pm], w=[self.F[0]])
            pm2 = self.mm_chunk("ple", 0, self.ppT, [self.ppT], nkt=2, rhs_off=chh * 512)
            S.op("dve", lambda e, pm2=pm2: e.tensor_tensor(out=self.F[1][:], in0=pm2[:], in1=self.F[0][:], op=ALU.mult),
                 r=[pm2, self.F[0]], w=[self.F[1]])
            S.op("dve", lambda e, sl=sl: e.scalar_tensor_tensor(out=Y[:, sl], in0=X1[:, sl], scalar=ALPHA, in1=self.F[1][:],
                                                                op0=ALU.mult, op1=ALU.add),
                 r=[X1, self.F[1]], w=[Y])
        if DBG_STOP != "nopeer":
            self.peer()
        self.load_params(6144, 2048, GI[:, 0:2048])
        self.layer_norm(Y[:], Y[:], D, GI[:, 0:1024], GI[:, 1024:2048], [Y], [Y], [GI])
        S.op("pool", lambda e: e.dma_start(out=ydst, in_=Y[:]), r=[Y], dma=True)

    def peer(self):
        S, d = self.S, self.d
        GI, x1T, X1, Y = self.GI, self.x1T, self.X1, self.Y
        qT = self.qT
        for c4 in range(4):
            ci = self.cidx["wq"][c4]
            wt = self.load_chunk(ci)
            pm = self.pmm_next()
            for ct in range(4):
                for kt in range(8):
                    S.op("pe", lambda e, pm=pm, wt=wt, ct=ct, kt=kt: e.matmul(
                        pm[:, ct * 128:(ct + 1) * 128], lhsT=wt[:, kt * 512 + ct * 128:kt * 512 + (ct + 1) * 128],
                        rhs=x1T[:, kt, :], start=(kt == 0), stop=(kt == 7), skip_group_check=True), r=[wt, self.TT], w=[pm])
            S.op("act", lambda e, pm=pm, c4=c4: e.copy(out=qT[:, c4 * 4:(c4 + 1) * 4, :], in_=pm[:].rearrange("p (a b) -> p a b", a=4)),
                 r=[pm], w=[qT])
        wt = self.load_chunk(self.cidx["skT"][0])
        SC = GI[:, 0:2048]
        for j4 in range(4):
            pm = self.pmm_next()
            for jj in range(4):
                j = j4 * 4 + jj
                S.op("pe", lambda e, pm=pm, wt=wt, j=j, jj=jj: e.matmul(
                    pm[:, jj * 128:(jj + 1) * 128], lhsT=qT[:, j, :], rhs=wt[:, j * 128:(j + 1) * 128], start=True, stop=True,
                    skip_group_check=True), r=[qT, wt], w=[pm])
            S.op("act", lambda e, pm=pm, j4=j4: e.copy(out=GI[:, j4 * 512:(j4 + 1) * 512], in_=pm[:]), r=[pm], w=[GI])
        V16, I16 = self.V16, self.I16
        TMPS = self.pm[11]
        for j in range(16):
            scj = GI[:, j * 128:(j + 1) * 128]
            S.op("dve", lambda e, j=j, scj=scj: e.max(out=V16[:, j, 0:8], in_=scj), r=[GI], w=[(V16, j)])
            S.op("dve", lambda e, j=j, scj=scj: e.max_index(out=I16[:, j, 0:8], in_max=V16[:, j, 0:8], in_values=scj),
                 r=[GI, (V16, j)], w=[(I16, j)])
            S.op("dve", lambda e, j=j, scj=scj: e.match_replace(out=TMPS[:], in_to_replace=V16[:, j, 0:8], in_values=scj, imm_value=NEG),
                 r=[GI, (V16, j)], w=[TMPS])
            S.op("dve", lambda e, j=j: e.max(out=V16[:, j, 8:16], in_=TMPS[:]), r=[TMPS], w=[(V16, j)])
            S.op("dve", lambda e, j=j: e.max_index(out=I16[:, j, 8:16], in_max=V16[:, j, 8:16], in_values=TMPS[:]),
                 r=[TMPS, (V16, j)], w=[(I16, j)])
        vkeys = [(V16, j) for j in range(16)]
        ikeys = [(I16, j) for j in range(16)]
        V4 = V16[:].rearrange("p (h s) k -> p h s k", s=2)
        CAND = GI[:, 2048:4096]
        S.op("dve", lambda e: e.tensor_tensor(
            out=CAND.rearrange("p (h a b) -> p h a b", h=8, a=16),
            in0=V4[:, :, 0, :].unsqueeze(3).to_broadcast([128, 8, 16, 16]),
            in1=V4[:, :, 1, :].unsqueeze(2).to_broadcast([128, 8, 16, 16]), op=ALU.add), r=vkeys + [GI], w=[GI])
        S16, SEL, TMPC = self.S16, self.SEL, self.TMPC
        for h in range(8):
            ch = GI[:, 2048 + h * 256:2048 + (h + 1) * 256]
            S.op("dve", lambda e, h=h, ch=ch: e.max(out=S16[:, h, 0:8], in_=ch), r=[GI], w=[(S16, h)])
            S.op("dve", lambda e, h=h, ch=ch: e.max_index(out=SEL[:, h, 0:8], in_max=S16[:, h, 0:8], in_values=ch),
                 r=[GI, (S16, h)], w=[(SEL, h)])
            S.op("dve", lambda e, h=h, ch=ch: e.match_replace(out=TMPC[:], in_to_replace=S16[:, h, 0:8], in_values=ch, imm_value=NEG),
                 r=[GI, (S16, h)], w=[TMPC])
            S.op("dve", lambda e, h=h: e.max(out=S16[:, h, 8:16], in_=TMPC[:]), r=[TMPC], w=[(S16, h)])
            S.op("dve", lambda e, h=h: e.max_index(out=SEL[:, h, 8:16], in_max=S16[:, h, 8:16], in_values=TMPC[:]),
                 r=[TMPC, (S16, h)], w=[(SEL, h)])
        skeys = [(S16, h) for h in range(8)]
        selkeys = [(SEL, h) for h in range(8)]
        pm_ = self.pm
        E16, Z, RZ, G16 = pm_[0], self.wsm[:, 16:24], self.wsm[:, 24:32], pm_[1]
        E3 = E16[:].rearrange("p (h k) -> p h k", h=8)
        S.op("dve", lambda e: e.tensor_tensor(out=E3, in0=S16[:], in1=S16[:, :, 0:1].to_broadcast([128, 8, 16]), op=ALU.subtract),
             r=skeys, w=[E16])
        S.op("act", lambda e: e.activation(out=E16[:], in_=E16[:], func=AF.Exp), r=[E16], w=[E16])
        S.op("dve", lambda e: e.reduce_sum(out=Z, in_=E3, axis=AX.X), r=[E16], w=[(self.wsm, "z")])
        S.op("dve", lambda e: e.reciprocal(out=RZ, in_=Z), r=[(self.wsm, "z")], w=[(self.wsm, "rz")])
        S.op("dve", lambda e: e.tensor_tensor(out=G16[:].rearrange("p (h k) -> p h k", h=8), in0=E3,
                                              in1=RZ.unsqueeze(2).to_broadcast([128, 8, 16]), op=ALU.mult),
             r=[E16, (self.wsm, "rz")], w=[G16])
        SELi = SEL[:].rearrange("p h k -> p (h k)").bitcast(I32)
        Ai, Bi_, Ei = self.pmi
        Af, Bf, I1f, I2f, I1s, I2s = pm_[2], pm_[3], pm_[4], pm_[5], pm_[6], pm_[7]
        S.op("dve", lambda e: e.tensor_single_scalar(out=Ai[:], in_=SELi, scalar=4, op=ALU.logical_shift_right), r=selkeys, w=[Ai])
        S.op("dve", lambda e: e.tensor_single_scalar(out=Bi_[:], in_=SELi, scalar=15, op=ALU.bitwise_and), r=selkeys, w=[Bi_])
        S.op("dve", lambda e: e.tensor_copy(out=Af[:], in_=Ai[:]), r=[Ai], w=[Af])
        S.op("dve", lambda e: e.tensor_copy(out=Bf[:], in_=Bi_[:]), r=[Bi_], w=[Bf])
        I4 = I16[:].rearrange("p j k -> p (j k)").bitcast(I32).rearrange("p (h s k) -> p h s k", h=8, s=2)
        S.op("dve", lambda e: e.tensor_copy(out=I1f[:].rearrange("p (h k) -> p h k", h=8), in_=I4[:, :, 0, :]),
             r=ikeys, w=[I1f])
        S.op("dve", lambda e: e.tensor_copy(out=I2f[:].rearrange("p (h k) -> p h k", h=8), in_=I4[:, :, 1, :]),
             r=ikeys, w=[I2f])
        OH = GI[:, 0:2048].rearrange("p (h j a) -> p h j a", h=8, j=16)
        PR = GI[:, 2048:4096].rearrange("p (h j a) -> p h j a", h=8, j=16)
        iot = self.c("iota16").unsqueeze(1).unsqueeze(1).to_broadcast([128, 8, 16, 16])
        for (Xf, If, Is) in ((Af, I1f, I1s), (Bf, I2f, I2s)):
            S.op("dve", lambda e, Xf=Xf: e.tensor_tensor(
                out=OH, in0=Xf[:].rearrange("p (h j) -> p h j", h=8).unsqueeze(3).to_broadcast([128, 8, 16, 16]), in1=iot,
                op=ALU.is_equal), r=[Xf, self.cst, GI], w=[GI])
            S.op("dve", lambda e, If=If: e.tensor_tensor(
                out=PR, in0=OH, in1=If[:].rearrange("p (h a) -> p h a", h=8).unsqueeze(2).to_broadcast([128, 8, 16, 16]),
                op=ALU.mult), r=[If, GI], w=[GI])
            S.op("dve", lambda e, Is=Is: e.reduce_sum(out=Is[:].rearrange("p (h j) -> p h j", h=8), in_=PR, axis=AX.X),
                 r=[GI], w=[Is])
        S.op("dve", lambda e: e.scalar_tensor_tensor(out=I1s[:], in0=I1s[:], scalar=128.0, in1=I2s[:], op0=ALU.mult, op1=ALU.add),
             r=[I1s, I2s], w=[I1s])
        S.op("dve", lambda e: e.tensor_copy(out=Ei[:], in_=I1s[:]), r=[I1s], w=[Ei])
        ACTP = pm_[8]
        for j in range(128):
            ur = self.UR[j % 6]
            S.op("pool", lambda e, ur=ur, j=j: e.indirect_dma_start(
                out=ur[:], out_offset=None, in_=self.pub, in_offset=bass.IndirectOffsetOnAxis(ap=Ei[:, j:j + 1], axis=0)),
                r=[Ei], w=[ur], dma=True, dsem="ur%d" % (j % 6))
            S.op("dve", lambda e, ur=ur, j=j: e.scalar_tensor_tensor(
                out=self.junkd[:], in0=ur[:], scalar=1.0, in1=X1[:], op0=ALU.mult, op1=ALU.mult,
                accum_out=ACTP[:, j:j + 1]), r=[ur, X1], w=[(ACTP, j), self.junkd])
        akeys = [(ACTP, j) for j in range(128)]
        COEF = pm_[10]
        S.op("act", lambda e: e.activation(out=pm_[9][:], in_=ACTP[:], func=AF.Gelu), r=akeys, w=[pm_[9]])
        S.op("dve", lambda e: e.tensor_tensor(out=COEF[:], in0=pm_[9][:], in1=G16[:], op=ALU.mult), r=[pm_[9], G16], w=[COEF])
        for j in range(128):
            vr = self.VR[j % 6]
            S.op("pool", lambda e, vr=vr, j=j: e.indirect_dma_start(
                out=vr[:], out_offset=None, in_=self.pvb, in_offset=bass.IndirectOffsetOnAxis(ap=Ei[:, j:j + 1], axis=0)),
                r=[Ei], w=[vr], dma=True, dsem="ur%d" % (j % 6))
            S.op("dve", lambda e, vr=vr, j=j: e.scalar_tensor_tensor(
                out=Y[:], in0=vr[:], scalar=COEF[:, j:j + 1], in1=Y[:], op0=ALU.mult, op1=ALU.add), r=[vr, COEF, Y], w=[Y])

    def phase_prompt(self):
        S, d, cfg = self.S, self.d, self.cfg
        with ExitStack() as st2:
            self.KT = self.sb("KT", [128, cfg.NBLK, 2, 128], BF16, st2)
            self.V = self.sb("V", [128, cfg.NBLK, 2, 129], BF16, st2)
            self.kidxT = self.sb("kidxT", [128, cfg.NBLK * 128], BF16, st2)
            self.mT = self.sb("mT", [128, cfg.NBLK, 128], BF16, st2)
            state = self.sb("state", [128, 4, 2, 512], F32, st2)
            for h in range(4):
                S.op("pool", lambda e, h=h: e.memset(state[:, h, :, :], 0.0), w=[(state, h)])
            lg = np.log1p(-np.exp2(-5.0 - np.arange(4, dtype=np.float32))).astype(np.float32)
            gC = np.exp(lg * 128.0)
            for bi in range(cfg.NBLK):
                light = bi < cfg.NL
                self.cur_blk = bi
                oi = bi - cfg.NL
                if light:
                    self.out_k = self.out_v = self.out_kidx = None
                else:
                    self.out_k, self.out_v, self.out_kidx = d["kout"][oi], d["vout"][oi], d["kidxout"][oi]
                self.block_front(d["xb"][bi], bi)
                if DBG_STOP == "front":
                    continue
                if not light:
                    self.load_params(0, 4096, self.GI[:, 0:4096])
                self.inproj(light)
                if DBG_STOP == "inproj":
                    continue
                self.retention(not light, self.c("decayT"), self.c("rowdec"), self.c("kdec"), gC, state, state)
                if DBG_STOP == "ret":
                    continue
                if not light:
                    self.attention(bi)
                    if DBG_STOP == "att":
                        continue
                    self.finish(d["pb"][oi], d["y"][oi])
            for h in range(4):
                S.op("pool", lambda e, h=h: e.dma_start(
                    out=d["stout"][h].rearrange("(kt p) v -> p kt v", p=128), in_=state[:, h, :, :]),
                    r=[(state, h)], dma=True)
            S.emit()

    def c2(self, name, lo=0, n=None):
        o, sz = CS2[name]
        n = sz - lo if n is None else n
        return self.cst2[:, o + lo:o + lo + n]

    def phase_sample(self):
        S, d, cfg = self.S, self.d, self.cfg
        NP, PAST, K = cfg.NPAGE, cfg.PAST, cfg.TOPK_S
        NR = (K + 7) // 8
        NSL = NR * 8
        lgNP = int(round(np.log2(NP)))
        assert (1 << lgNP) == NP and NSL <= 256
        lg = np.log1p(-np.exp2(-5.0 - np.arange(4, dtype=np.float32))).astype(np.float32)
        gC4 = np.exp(lg * 4.0)
        pm_ = self.pm
        with ExitStack() as st2:
            sb2 = lambda n, s, dt: self.sb(n, s, dt, st2)
            self.KT = sb2("KTs", [128, 1, 2, 128], BF16)
            self.V = sb2("Vs", [128, 1, 2, 129], BF16)
            self.kidxT = sb2("kidxTs", [128, 128], BF16)
            self.mT = sb2("mTs", [128, 1, 128], BF16)
            self.cst2 = sb2("cst2", [128, NCS2], F32)
            QDb = sb2("QDb", [128, 32, 32], BF16)
            Wpad = sb2("Wpad", [32, 32, 128], BF16)
            AQS = sb2("AQS", [128, 1024], BF16)
            VALS = sb2("VALS", [128, NSL], F32)
            IDX = sb2("IDX", [128, NSL], U32)
            INEW = sb2("INEW", [128, 8], F32)
            ptc = sb2("ptc", [128, 32], I32)
            S.op("sp", lambda e: e.dma_start(out=self.cst2[:], in_=d["cst2"]), w=[self.cst2], dma=True)
            S.op("dve", lambda e: e.memset(ptc[:], 0), w=[ptc])
            S.op("sp", lambda e: e.dma_start(out=ptc[0:NP, :], in_=d["ptcol"]), w=[ptc], dma=True)
            self.cur_blk = 0
            self.out_k, self.out_v, self.out_kidx = d["ks"], d["vs"], d["kidxs"]
            self.block_front(d["xs"], cfg.NBLK)
            self.load_params(0, 4096, self.GI[:, 0:4096])
            self.inproj(False)
            S.op("act", lambda e: e.copy(out=AQS[:], in_=self.B[2][:]), r=[self.B[2]], w=[AQS])
            iqd = self.B[0]
            iqs = self.B[3][:, 256:768].rearrange("p (h f) -> p h f", h=8)
            iqd4 = iqd[:].rearrange("p (h s f) -> p h s f", h=8, s=2)
            for s_ in range(2):
                S.op("dve", lambda e, s_=s_: e.tensor_copy(out=iqd4[:, :, s_, :], in_=iqs), r=[(self.B[3], "iq")], w=[iqd])
            for g0 in range(0, 8, 4):
                pt = self.ptr[self.ptr_i % 2]
                self.ptr_i += 1
                ptb = pt[:].bitcast(BF16)
                for i in range(4):
                    h = g0 + i
                    S.op("pe", lambda e, h=h, i=i, ptb=ptb: e.transpose(out=ptb[:, i * 128:(i + 1) * 128], in_=iqd[:, h * 128:(h + 1) * 128],
                                                                        identity=self.identb[:]), r=[iqd, self.identb], w=[pt])
                S.op("dve", lambda e, g0=g0, ptb=ptb: e.tensor_copy(
                    out=QDb[:].rearrange("p b (h t) -> p h b t", h=8)[:, g0:g0 + 4, :, :],
                    in_=ptb[:, 0:512].rearrange("p (h b t) -> p h b t", h=4, b=32)), r=[pt], w=[QDb])
            EW = self.B[0]
            S.op("dve", lambda e: e.tensor_tensor(
                out=EW[:, 0:32].rearrange("p (h t) -> p h t", h=8),
                in0=self.wsm[:, 8:16].unsqueeze(2).to_broadcast([128, 8, 4]),
                in1=self.c2("tmask").unsqueeze(1).to_broadcast([128, 8, 4]), op=ALU.mult),
                r=[(self.wsm, "sg"), self.cst2, iqd], w=[EW])
            pt = self.ptr[self.ptr_i % 2]
            self.ptr_i += 1
            ptb = pt[:].bitcast(BF16)
            S.op("pe", lambda e, ptb=ptb: e.transpose(out=ptb[0:32, 0:128], in_=EW[:, 0:32], identity=self.identb[:]),
                 r=[EW, self.identb], w=[pt])
            S.op("dve", lambda e: e.memset(Wpad[:], 0.0), w=[Wpad])
            for b in range(32):
                S.op("dve", lambda e, b=b, ptb=ptb: e.tensor_copy(out=Wpad[:, b, 4 * b:4 * b + 4], in_=ptb[0:32, 4 * b:4 * b + 4]),
                     r=[pt, Wpad], w=[Wpad])
            IF = pm_[0]
            for hd in range(8):
                hp, s_ = hd // 2, hd % 2
                pSx = self.pS[self.pS_i % 2]
                self.pS_i += 1
                S.op("pe", lambda e, pSx=pSx, hp=hp, s_=s_: e.matmul(
                    pSx[:, 0:128], lhsT=self.iqT[64 * s_:64 * s_ + 64, hp, :], rhs=self.kidxT[64 * s_:64 * s_ + 64, 0:128],
                    start=True, stop=True), r=[self.iqT, self.kidxT_key(0)], w=[pSx])
                Fx = pm_[1 + hd % 2]
                S.op("act", lambda e, pSx=pSx, Fx=Fx: e.activation(out=Fx[:], in_=pSx[:, 0:128], func=AF.Relu), r=[pSx], w=[Fx])
                if hd == 0:
                    S.op("dve", lambda e, Fx=Fx: e.tensor_scalar(out=IF[:], in0=Fx[:], scalar1=self.wsm[:, 8:9], scalar2=None, op0=ALU.mult),
                         r=[Fx, (self.wsm, "sg")], w=[IF])
                else:
                    S.op("dve", lambda e, Fx=Fx, hd=hd: e.scalar_tensor_tensor(
                        out=IF[:], in0=Fx[:], scalar=self.wsm[:, 8 + hd:9 + hd], in1=IF[:], op0=ALU.mult, op1=ALU.add),
                        r=[Fx, (self.wsm, "sg"), IF], w=[IF])
            S.op("dve", lambda e: e.tensor_tensor(
                out=pm_[3][:].rearrange("p (b t) -> p b t", b=32), in0=IF[:].rearrange("p (b t) -> p b t", b=32),
                in1=self.c2("seqmask").unsqueeze(2).to_broadcast([128, 32, 4]), op=ALU.mult), r=[IF, self.cst2], w=[pm_[3]])
            S.op("dve", lambda e: e.reduce_sum(out=INEW[:, 0:4], in_=pm_[3][:].rearrange("p (b t) -> p t b", b=32), axis=AX.X),
                 r=[pm_[3]], w=[INEW])
            S.op("dve", lambda e: e.tensor_tensor(out=INEW[:, 0:4], in0=INEW[:, 0:4], in1=self.c2("causal4"), op=ALU.add),
                 r=[INEW, self.cst2], w=[INEW])
            if DBG_STOP == "s_pre":
                S.emit()
                return
            with ExitStack() as st3:
                sb3 = lambda n, s, dt: self.sb(n, s, dt, st3)
                stf = [sb3("stf%d" % i, [128, 4, 2, 512], F32) for i in range(2)]
                sbfb = sb3("sbfb", [128, 8, 512], BF16)
                qdall = sb3("qdall", [128, 8, 128], BF16)
                kdt = sb3("kdt", [128, 1024], BF16)
                qz = sb3("qz", [128, 8, 128], BF16)
                kz = sb3("kz", [128, 1024], BF16)
                PTs = [sb3("PTs%d" % i, [128, 128], BF16) for i in range(4)]
                obank = [self.pmm[0], self.pmm[1], self.pO[0], self.pO[1]]
                rqT, rkT, rv_bf, rk_r = self.rqT, self.rkT, self.rv_bf, self.B[1]
                for h in range(4):
                    pSx = self.pS[h % 2]
                    for kt in range(2):
                        S.op("pe", lambda e, pSx=pSx, h=h, kt=kt: e.matmul(
                            pSx[:, 0:128], lhsT=rkT[:, 2 * h + kt, :], rhs=rqT[:, 2 * h + kt, :], start=(kt == 0), stop=(kt == 1)),
                            r=[self.RQK], w=[pSx])
                    S.op("dve", lambda e, pSx=pSx, h=h: e.tensor_tensor(
                        out=PTs[h][:], in0=pSx[:, 0:128], in1=self.c2("decayT", h * 128, 128), op=ALU.mult),
                        r=[pSx, self.cst2], w=[PTs[h]])
                    S.op("pe", lambda e, h=h: e.matmul(obank[h][:], lhsT=PTs[h][:], rhs=rv_bf[:, h, :], start=True, stop=False),
                         r=[PTs[h], rv_bf], w=[obank[h]])
                    S.op("dve", lambda e, h=h: e.tensor_tensor(
                        out=qdall[:, 2 * h:2 * h + 2, :], in0=rqT[:, 2 * h:2 * h + 2, :],
                        in1=self.c2("rowdec", h * 128, 128).unsqueeze(1).to_broadcast([128, 2, 128]), op=ALU.mult),
                        r=[self.RQK, self.cst2], w=[qdall])
                    S.op("dve", lambda e, h=h: e.tensor_scalar(
                        out=kdt[:, h * 256:(h + 1) * 256], in0=rk_r[:, h * 256:(h + 1) * 256], scalar1=self.c2("kdec", h, 1),
                        scalar2=None, op0=ALU.mult), r=[rk_r, self.cst2], w=[kdt])
                for b in range(cfg.NSEQ):
                    sf = stf[b % 2]
                    S.op("sp", lambda e, sf=sf, b=b: e.dma_start(
                        out=sf[:], in_=d["state_in"][b].rearrange("h (kt p) v -> p h kt v", p=128)), w=[sf], dma=True)
                    S.op("act", lambda e, sf=sf: e.copy(out=sbfb[:, 0:4, :], in_=sf[:, 0:2, :, :].rearrange("p h k v -> p (h k) v")),
                         r=[sf], w=[(sbfb, 0)])
                    S.op("dve", lambda e, sf=sf: e.tensor_copy(out=sbfb[:, 4:8, :], in_=sf[:, 2:4, :, :].rearrange("p h k v -> p (h k) v")),
                         r=[sf], w=[(sbfb, 1)])
                    S.op("dve", lambda e, b=b: e.tensor_tensor(
                        out=qz[:], in0=qdall[:],
                        in1=self.c2("onesw", 124 - 4 * b, 128).unsqueeze(1).to_broadcast([128, 8, 128]), op=ALU.mult),
                        r=[qdall, self.cst2], w=[qz])
                    S.op("dve", lambda e, b=b: e.tensor_scalar(out=kz[:], in0=kdt[:], scalar1=self.c2("seqmask", b, 1), scalar2=None,
                                                               op0=ALU.mult), r=[kdt, self.cst2], w=[kz])
                    for h in range(4):
                        for kt in range(2):
                            S.op("pe", lambda e, h=h, kt=kt, b=b: e.matmul(
                                obank[h][:], lhsT=qz[:, 2 * h + kt, :], rhs=sbfb[:, 2 * h + kt, :], start=False,
                                stop=(b == cfg.NSEQ - 1 and kt == 1)),
                                r=[qz, (sbfb, h // 2)], w=[obank[h]])
                    for h in range(4):
                        for kt in range(2):
                            pSx = self.pS[self.pS_i % 2]
                            self.pS_i += 1
                            S.op("pe", lambda e, pSx=pSx, h=h, kt=kt: e.matmul(
                                pSx[:], lhsT=kz[:, h * 256 + kt * 128:h * 256 + (kt + 1) * 128], rhs=rv_bf[:, h, :], start=True, stop=True),
                                r=[kz, rv_bf], w=[pSx])
                            S.op("dve", lambda e, pSx=pSx, sf=sf, h=h, kt=kt: e.scalar_tensor_tensor(
                                out=sf[:, h, kt, :], in0=sf[:, h, kt, :], scalar=float(gC4[h]), in1=pSx[:], op0=ALU.mult, op1=ALU.add),
                                r=[pSx, sf], w=[sf])
                    S.op("pool", lambda e, sf=sf, b=b: e.dma_start(
                        out=d["state_out"][b].rearrange("h (kt p) v -> p h kt v", p=128), in_=sf[:]), r=[sf], dma=True)
                for h in range(4):
                    yh = self.F[h % 2]
                    S.op("act", lambda e, h=h, yh=yh: e.copy(out=yh[:], in_=obank[h][:]), r=[obank[h]], w=[yh])
                    self.layer_norm(yh[:], yh[:], 512, self.GI[:, h * 512:(h + 1) * 512], self.GI[:, 2048 + h * 512:2048 + (h + 1) * 512],
                                    [yh], [yh], [self.GI])
                    S.op("dve", lambda e, yh=yh, h=h: e.tensor_tensor(
                        out=self.ret_o[:, h * 512:(h + 1) * 512], in0=yh[:], in1=self.sgate[:, h * 512:(h + 1) * 512], op=ALU.mult),
                        r=[yh, (self.sgate, h)], w=[(self.sgate, h)])
                S.emit()
            if DBG_STOP == "s_ret":
                return
            with ExitStack() as st3:
                sb3 = lambda n, s, dt: self.sb(n, s, dt, st3)
                IALL = TV("v_IALL", self.ARENA[:, 0:PAST])
                PG = [sb3("PG%d" % i, [128, 2048], F32) for i in range(2)]
                slabT = [sb3("slabT%d" % i, [128, 512], BF16) for i in range(2)]
                rS = [sb3("rS%d" % i, [32, 512], BF16) for i in range(2)]
                ptq = [sb3("ptq%d" % q, [128, 32], I32) for q in range(4)]
                ptf = sb3("ptf", [128, 32], F32)
                S.op("pool", lambda e: e.memset(IALL[:], 0.0), w=[IALL])
                S.op("dve", lambda e: e.tensor_copy(out=ptf[:], in_=ptc[:]), r=[ptc], w=[ptf])
                for q in range(4):
                    S.op("dve", lambda e, q=q: e.tensor_scalar(out=pm_[0][:, 0:32], in0=ptf[:], scalar1=4.0, scalar2=float(q),
                                                               op0=ALU.mult, op1=ALU.add), r=[ptf], w=[pm_[0]])
                    S.op("dve", lambda e, q=q: e.tensor_copy(out=ptq[q][:], in_=pm_[0][:, 0:32]), r=[pm_[0]], w=[ptq[q]])
                kq = d["cache_kidx"].rearrange("n (q f) -> (n q) f", q=4)
                W4 = 4 * NP
                it = 0
                for q in range(4):
                    for b in range(cfg.NSEQ):
                        pg = PG[it % 2]
                        it += 1
                        S.op("pool", lambda e, pg=pg, q=q, b=b: e.indirect_dma_start(
                            out=pg[:, :], out_offset=None, in_=kq,
                            in_offset=bass.IndirectOffsetOnAxis(ap=ptq[q][:, b:b + 1], axis=0)),
                            r=[ptq[q]], w=[pg], dma=True, dsem="pg%d" % (it % 2))
                        for g4 in range(4):
                            pt = self.ptr[self.ptr_i % 2]
                            self.ptr_i += 1
                            for i in range(4):
                                s_ = g4 * 4 + i
                                S.op("pe", lambda e, pt=pt, pg=pg, i=i, s_=s_: e.transpose(
                                    out=pt[:, i * NP:(i + 1) * NP], in_=pg[0:NP, s_ * 128:(s_ + 1) * 128],
                                    identity=self.c("ident")[0:NP, 0:NP]), r=[pg, self.cst], w=[pt])
                            sT = slabT[g4 % 2]
                            S.op("act", lambda e, pt=pt, sT=sT: e.copy(out=sT[:, 0:W4], in_=pt[:, 0:W4]), r=[pt], w=[sT])
                            for half in range(2):
                                for ii in range(2):
                                    i = 2 * half + ii
                                    for o2 in range(2):
                                        pSx = self.pS[o2]
                                        S.op("pe", lambda e, pSx=pSx, sT=sT, i=i, ii=ii, o2=o2, b=b: e.matmul(
                                            pSx[0:32, ii * NP:(ii + 1) * NP],
                                            lhsT=QDb[64 * o2:64 * o2 + 64, b, :], rhs=sT[64 * o2:64 * o2 + 64, i * NP:(i + 1) * NP],
                                            start=True, stop=True, skip_group_check=True), r=[QDb, sT], w=[pSx])
                                rSx = rS[half]
                                for o2 in range(2):
                                    pSx = self.pS[o2]
                                    S.op("act", lambda e, pSx=pSx, rSx=rSx, o2=o2: e.activation(
                                        out=rSx[:, 0:W4].rearrange("p (i o n) -> p i o n", i=2, o=2)[:, :, o2, :],
                                        in_=pSx[0:32, 0:2 * NP].rearrange("p (i n) -> p i n", i=2), func=AF.Relu),
                                        r=[pSx], w=[(rSx, o2)])
                                rkeys_ = [(rSx, 0), (rSx, 1)]
                                pOx = self.pO[half]
                                S.op("pe", lambda e, pOx=pOx, rSx=rSx, b=b: e.matmul(
                                    pOx[:, 0:W4], lhsT=Wpad[:, b, :], rhs=rSx[:, 0:W4], start=True, stop=True), r=[Wpad] + rkeys_, w=[pOx])
                                u0 = (q * 16 + g4 * 4 + 2 * half) * 2
                                S.op("dve", lambda e, pOx=pOx, u0=u0: e.tensor_tensor(
                                    out=IALL[:, u0 * NP:u0 * NP + W4], in0=IALL[:, u0 * NP:u0 * NP + W4], in1=pOx[:, 0:W4], op=ALU.add),
                                    r=[pOx, IALL], w=[IALL])
                for r_ in range(NR if DBG_STOP != "s_idx" else 0):
                    sl = slice(r_ * 8, r_ * 8 + 8)
                    S.op("dve", lambda e, sl=sl: e.max(out=VALS[:, sl], in_=IALL[:]), r=[IALL], w=[VALS])
                    S.op("dve", lambda e, sl=sl: e.max_index(out=IDX[:, sl], in_max=VALS[:, sl], in_values=IALL[:]), r=[IALL, VALS], w=[IDX])
                    if r_ < NR - 1:
                        S.op("dve", lambda e, sl=sl: e.match_replace(out=IALL[:], in_to_replace=VALS[:, sl], in_values=IALL[:], imm_value=NEG),
                             r=[IALL, VALS], w=[IALL])
                S.emit()
            if DBG_STOP in ("s_idx", "s_topk"):
                return
            with ExitStack() as st3:
                sb3 = lambda n, s, dt: self.sb(n, s, dt, st3)
                ROWi = sb3("ROWi", [128, NSL], I32)
                SELP = sb3("SELP", [128, NSL], F32)
                SELN = sb3("SELN", [128, 8], F32)
                tI = [sb3("tI%d" % i, [128, NSL], I32) for i in range(2)]
                tF = [sb3("tF%d" % i, [128, NSL], F32) for i in range(4)]
                ptrow = sb3("ptrow", [128, 128], I32)
                ptrf = sb3("ptrf", [128, 128], F32)
                SCALL = sb3("SCALL", [128, NSL, 8], F32)
                PALL = sb3("PALL", [128, NSL, 8], F32)
                KR = [sb3("KR%d" % i, [128, 256], F32) for i in range(6)]
                ACC = sb3("ACC", [128, 8, 128], F32)
                DEN = sb3("DEN", [128, 8], F32)
                cnts = sb3("cnts", [128, 16], F32)
                IDXi = IDX[:].bitcast(I32)
                S.op("sp", lambda e: e.dma_start(out=ptrow[:, 0:NP], in_=d["ptrow"]), w=[ptrow], dma=True)
                S.op("dve", lambda e: e.tensor_copy(out=ptrf[:, 0:NP], in_=ptrow[:, 0:NP]), r=[ptrow], w=[ptrf])
                S.op("dve", lambda e: e.tensor_single_scalar(out=tI[0][:], in_=IDXi, scalar=NP - 1, op=ALU.bitwise_and), r=[IDX], w=[tI[0]])
                S.op("dve", lambda e: e.tensor_single_scalar(out=tI[1][:], in_=IDXi, scalar=lgNP, op=ALU.logical_shift_right), r=[IDX], w=[tI[1]])
                S.op("dve", lambda e: e.tensor_copy(out=tF[0][:], in_=tI[0][:]), r=[tI[0]], w=[tF[0]])
                S.op("dve", lambda e: e.tensor_copy(out=tF[1][:], in_=tI[1][:]), r=[tI[1]], w=[tF[1]])
                CH = max(1, min(NSL, 4096 // NP))
                GI = self.GI
                for c0 in range(0, NSL, CH):
                    cn = min(CH, NSL - c0)
                    OH = GI[:, 0:cn * NP].rearrange("p (j n) -> p j n", j=cn)
                    S.op("dve", lambda e, OH=OH, c0=c0, cn=cn: e.tensor_tensor(
                        out=OH, in0=tF[0][:, c0:c0 + cn].unsqueeze(2).to_broadcast([128, cn, NP]),
                        in1=self.c("iota128", 0, NP).unsqueeze(1).to_broadcast([128, cn, NP]), op=ALU.is_equal),
                        r=[tF[0], self.cst, GI], w=[GI])
                    S.op("dve", lambda e, OH=OH, cn=cn: e.tensor_tensor(
                        out=OH, in0=OH, in1=ptrf[:, 0:NP].unsqueeze(1).to_broadcast([128, cn, NP]), op=ALU.mult), r=[GI, ptrf], w=[GI])
                    S.op("dve", lambda e, OH=OH, c0=c0, cn=cn: e.reduce_sum(out=tF[2][:, c0:c0 + cn], in_=OH, axis=AX.X), r=[GI], w=[tF[2]])
                S.op("dve", lambda e: e.scalar_tensor_tensor(out=tF[2][:], in0=tF[2][:], scalar=128.0, in1=tF[1][:], op0=ALU.mult, op1=ALU.add),
                     r=[tF[2], tF[1]], w=[tF[2]])
                S.op("dve", lambda e: e.tensor_copy(out=ROWi[:], in_=tF[2][:]), r=[tF[2]], w=[ROWi])
                CR = tF[3]
                for n in range(4):
                    if n == 0:
                        S.op("dve", lambda e, n=n: e.tensor_scalar(out=CR[:], in0=VALS[:], scalar1=INEW[:, n:n + 1], scalar2=None, op0=ALU.is_lt),
                             r=[VALS, INEW], w=[CR])
                    else:
                        S.op("dve", lambda e, n=n: e.tensor_scalar(out=tF[0][:], in0=VALS[:], scalar1=INEW[:, n:n + 1], scalar2=None, op0=ALU.is_lt),
                             r=[VALS, INEW], w=[tF[0]])
                        S.op("dve", lambda e: e.tensor_tensor(out=CR[:], in0=CR[:], in1=tF[0][:], op=ALU.add), r=[CR, tF[0]], w=[CR])
                S.op("dve", lambda e: e.tensor_tensor(out=CR[:], in0=CR[:], in1=self.c2("iota256", 0, NSL), op=ALU.add), r=[CR, self.cst2], w=[CR])
                S.op("dve", lambda e: e.tensor_scalar(out=SELP[:], in0=CR[:], scalar1=float(K) - 0.5, scalar2=None, op0=ALU.is_lt), r=[CR], w=[SELP])
                for n in range(4):
                    S.op("dve", lambda e, n=n: e.tensor_scalar(out=tF[0][:], in0=VALS[:], scalar1=INEW[:, n:n + 1], scalar2=None, op0=ALU.is_gt),
                         r=[VALS, INEW], w=[tF[0]])
                    S.op("dve", lambda e, n=n: e.reduce_sum(out=cnts[:, n:n + 1], in_=tF[0][:], axis=AX.X), r=[tF[0]], w=[(cnts, n)])
                    S.op("dve", lambda e, n=n: e.tensor_scalar(out=cnts[:, 8:12], in0=INEW[:, 0:4], scalar1=INEW[:, n:n + 1], scalar2=None, op0=ALU.is_gt),
                         r=[INEW], w=[(cnts, "t")])
                    S.op("dve", lambda e, n=n: e.reduce_sum(out=cnts[:, 4 + n:5 + n], in_=cnts[:, 8:12], axis=AX.X), r=[(cnts, "t")], w=[(cnts, 4 + n)])
                ck = [(cnts, i) for i in range(8)]
                S.op("dve", lambda e: e.tensor_tensor(out=cnts[:, 12:16], in0=cnts[:, 0:4], in1=cnts[:, 4:8], op=ALU.add), r=ck, w=[(cnts, "s")])
                S.op("dve", lambda e: e.tensor_scalar(out=SELN[:, 0:4], in0=cnts[:, 12:16], scalar1=float(K) - 0.5, scalar2=None, op0=ALU.is_lt),
                     r=[(cnts, "s")], w=[SELN])
                S.op("dve", lambda e: e.tensor_scalar(out=SELN[:, 4:8], in0=INEW[:, 0:4], scalar1=-1e29, scalar2=None, op0=ALU.is_gt), r=[INEW], w=[(SELN, 1)])
                S.op("dve", lambda e: e.tensor_tensor(out=SELN[:, 0:4], in0=SELN[:, 0:4], in1=SELN[:, 4:8], op=ALU.mult), r=[SELN, (SELN, 1)], w=[SELN])
                kflat, vflat = d["cache_k"], d["cache_v"]
                AQ4 = AQS[:].rearrange("p (g r f) -> p g r f", g=2, r=4)
                PRD = GI[:, 0:1024]
                for j in range(NSL):
                    kr = KR[j % 6]
                    S.op("pool", lambda e, kr=kr, j=j: e.indirect_dma_start(
                        out=kr[:], out_offset=None, in_=kflat, in_offset=bass.IndirectOffsetOnAxis(ap=ROWi[:, j:j + 1], axis=0)),
                        r=[ROWi], w=[kr], dma=True, dsem="kr%d" % (j % 6))
                    S.op("dve", lambda e, kr=kr: e.tensor_tensor(
                        out=PRD.rearrange("p (g r f) -> p g r f", g=2, r=4), in0=AQ4,
                        in1=kr[:].rearrange("p (g f) -> p g f", g=2).unsqueeze(2).to_broadcast([128, 2, 4, 128]), op=ALU.mult),
                        r=[AQS, kr, GI], w=[GI])
                    S.op("dve", lambda e, j=j: e.reduce_sum(out=SCALL[:, j, :], in_=PRD.rearrange("p (h f) -> p h f", h=8), axis=AX.X),
                         r=[GI], w=[SCALL])
                S.op("act", lambda e: e.activation(out=PALL[:], in_=SCALL[:], func=AF.Exp, scale=128.0 ** -0.5), r=[SCALL], w=[PALL])
                S.op("dve", lambda e: e.tensor_tensor(out=PALL[:], in0=PALL[:], in1=SELP[:].unsqueeze(2).to_broadcast([128, NSL, 8]), op=ALU.mult),
                     r=[PALL, SELP], w=[PALL])
                S.op("dve", lambda e: e.reduce_sum(out=DEN[:], in_=PALL[:].rearrange("p j h -> p h j"), axis=AX.X), r=[PALL], w=[DEN])
                S.op("pool", lambda e: e.memset(ACC[:], 0.0), w=[ACC])
                for j in range(NSL):
                    vr = KR[j % 6]
                    S.op("pool", lambda e, vr=vr, j=j: e.indirect_dma_start(
                        out=vr[:], out_offset=None, in_=vflat, in_offset=bass.IndirectOffsetOnAxis(ap=ROWi[:, j:j + 1], axis=0)),
                        r=[ROWi], w=[vr], dma=True, dsem="kr%d" % (j % 6))
                    S.op("dve", lambda e, vr=vr, j=j: e.tensor_tensor(
                        out=PRD.rearrange("p (g r f) -> p g r f", g=2, r=4),
                        in0=vr[:].rearrange("p (g f) -> p g f", g=2).unsqueeze(2).to_broadcast([128, 2, 4, 128]),
                        in1=PALL[:, j, :].rearrange("p (g r) -> p g r", g=2).unsqueeze(3).to_broadcast([128, 2, 4, 128]), op=ALU.mult),
                        r=[vr, PALL, GI], w=[GI])
                    S.op("dve", lambda e: e.tensor_tensor(out=ACC[:].rearrange("p h f -> p (h f)"), in0=ACC[:].rearrange("p h f -> p (h f)"),
                                                          in1=PRD, op=ALU.add), r=[GI, ACC], w=[ACC])
                MQ = self.mch
                S.op("dve", lambda e: e.tensor_tensor(
                    out=MQ[:, 0:128].rearrange("p (b t) -> p b t", b=32),
                    in0=self.c2("seqmask").unsqueeze(2).to_broadcast([128, 32, 4]),
                    in1=SELN[:, 0:4].unsqueeze(1).to_broadcast([128, 32, 4]), op=ALU.mult), r=[self.cst2, SELN], w=[MQ])
                self.transposes_into([(MQ[:, 0:128], [MQ])], self.mT[:, 0:1, :], (self.mT, 0))
                for g in range(2):
                    pSx = self.pS[self.pS_i % 2]
                    self.pS_i += 1
                    S.op("pe", lambda e, pSx=pSx, g=g: e.matmul(pSx[:], lhsT=self.KT[:, 0, g, :], rhs=self.aqT[:, 4 * g:4 * g + 4, :],
                                                                 start=True, stop=True), r=[self.KT_key(0), self.aqT], w=[pSx])
                    Ex, Px = self.Ebuf[g], self.Pbuf[g]
                    S.op("act", lambda e, pSx=pSx, Ex=Ex: e.activation(out=Ex[:], in_=pSx[:], func=AF.Exp, scale=128.0 ** -0.5), r=[pSx], w=[Ex])
                    S.op("dve", lambda e, Ex=Ex, Px=Px: e.tensor_tensor(
                        out=Px[:], in0=Ex[:].rearrange("p (r t) -> p r t", r=4),
                        in1=self.mT[:, 0, :].unsqueeze(1).to_broadcast([128, 4, 128]), op=ALU.mult), r=[Ex, (self.mT, 0)], w=[Px])
                    for r in range(4):
                        pOx = self.pO[r // 2]
                        S.op("pe", lambda e, pOx=pOx, Px=Px, r=r, g=g: e.matmul(
                            pOx[:, (r % 2) * 129:(r % 2) * 129 + 129], lhsT=Px[:, r, :], rhs=self.V[:, 0, g, :],
                            start=(r % 2 == 0), stop=True, skip_group_check=True), r=[Px, self.V_key(0), (self.V_key(0), "one")], w=[pOx])
                    for r in range(4):
                        pOx = self.pO[r // 2]
                        o0 = (r % 2) * 129
                        hd = 4 * g + r
                        S.op("dve", lambda e, pOx=pOx, o0=o0, hd=hd: e.tensor_tensor(
                            out=self.rsum[:, hd:hd + 1], in0=pOx[:, o0 + 128:o0 + 129], in1=DEN[:, hd:hd + 1], op=ALU.add),
                            r=[pOx, DEN], w=[(self.rsum, hd)])
                        S.op("dve", lambda e, hd=hd: e.reciprocal(out=self.rsum[:, hd:hd + 1], in_=self.rsum[:, hd:hd + 1]),
                             r=[(self.rsum, hd)], w=[(self.rsum, hd)])
                        S.op("dve", lambda e, pOx=pOx, o0=o0, hd=hd: e.tensor_tensor(
                            out=ACC[:, hd, :], in0=ACC[:, hd, :], in1=pOx[:, o0:o0 + 128], op=ALU.add), r=[pOx, ACC], w=[ACC])
                        S.op("dve", lambda e, hd=hd: e.tensor_scalar(
                            out=self.att_o[:, hd, :], in0=ACC[:, hd, :], scalar1=self.rsum[:, hd:hd + 1], scalar2=None, op0=ALU.mult),
                            r=[ACC, (self.rsum, hd)], w=[(self.att_o, hd)])
                self.xsrc = d["xs"]
                self.finish(d["ps"], d["ys"])
                S.emit()

    def build(self):
        with ExitStack() as st:
            self.st = st
            self.declare()
            self.S = Sched(self.nc, st)
            self.phase_prep()
            self.alloc_common()
            if self.cfg.prompt:
                self.phase_prompt()
            if self.cfg.sample:
                self.phase_sample()
        return self.nc


def rope_tables(pos):
    pos = np.asarray(pos).astype(np.float32)
    cs, sn = [], []
    for half in (128, 64, 32):
        inv = (np.float32(10000.0) ** (-np.arange(half, dtype=np.float32) / np.float32(half))).astype(np.float32)
        ang = (pos[:, None] * inv[None, :]).astype(np.float32)
        cs.append(np.cos(ang).astype(np.float32))
        sn.append(np.sin(ang).astype(np.float32))
    return np.concatenate(cs, 1), np.concatenate(sn, 1)


def const_table(lightbias, ikg, ikb):
    t = np.zeros((128, NCST), np.float32)

    def put(name, arr):
        o, n = CST[name]
        t[:, o:o + n] = arr

    put("ident", np.eye(128, dtype=np.float32))
    lg = np.log1p(-np.exp2(-5.0 - np.arange(4, dtype=np.float32))).astype(np.float32)
    i = np.arange(128, dtype=np.float32)
    dT = np.zeros((128, 4, 128), np.float32)
    for h in range(4):
        diff = i[None, :] - i[:, None]
        dT[:, h, :] = np.where(diff >= 0, np.exp(lg[h] * np.maximum(diff, 0.0)), 0.0) / 16.0
    put("decayT", dT.reshape(128, 512))
    rd = np.stack([np.exp(lg[h] * (i + 1.0)) for h in range(4)], 0).reshape(1, 512)
    put("rowdec", np.broadcast_to(rd, (128, 512)))
    put("kdec", np.stack([np.exp(lg[h] * (127.0 - i)) / 16.0 for h in range(4)], 1))
    q = np.arange(128)
    put("causal", np.where(q[None, :] <= q[:, None], 0.0, NEG).astype(np.float32))
    put("iota16", np.broadcast_to(np.arange(16, dtype=np.float32)[None], (128, 16)))
    put("ikg", np.broadcast_to(ikg[None], (128, 64)))
    put("ikb", np.broadcast_to(ikb[None], (128, 64)))
    put("lightbias", np.full((128, 1), lightbias, np.float32))
    put("iota128", np.broadcast_to(np.arange(128, dtype=np.float32)[None], (128, 128)))
    return t


def param_table(gn_g, gn_b, ln1_g, ln1_b, ln2_g, ln2_b):
    row = np.concatenate([gn_g, gn_b, ln1_g, ln1_b, ln2_g, ln2_b]).astype(np.float32)
    return np.ascontiguousarray(np.broadcast_to(row[None], (128, 8192)))


_PROG_CACHE = {}


def get_prog(cfg_key, cfg):
    if cfg_key not in _PROG_CACHE:
        _PROG_CACHE[cfg_key] = Prog(cfg).build()
    return _PROG_CACHE[cfg_key]


def const_table2(dec_seq=4):
    t = np.zeros((128, NCS2), np.float32)

    def put(name, arr):
        o, n = CS2[name]
        t[:, o:o + n] = arr

    lg = np.log1p(-np.exp2(-5.0 - np.arange(4, dtype=np.float32))).astype(np.float32)
    r = np.arange(128)
    bq, tq = r // dec_seq, r % dec_seq
    dT = np.zeros((128, 4, 128), np.float32)
    same = bq[:, None] == bq[None, :]
    diff = (tq[None, :] - tq[:, None]).astype(np.float32)
    for h in range(4):
        dT[:, h, :] = np.where(same & (diff >= 0), np.exp(lg[h] * np.maximum(diff, 0.0)), 0.0) / 16.0
    put("decayT", dT.reshape(128, 512))
    rd = np.stack([np.exp(lg[h] * (tq.astype(np.float32) + 1.0)) for h in range(4)], 0).reshape(1, 512)
    put("rowdec", np.broadcast_to(rd, (128, 512)))
    put("kdec", np.stack([np.exp(lg[h] * (dec_seq - 1.0 - tq.astype(np.float32))) / 16.0 for h in range(4)], 1))
    put("tmask", (tq[:, None] == np.arange(4)[None, :]).astype(np.float32))
    put("seqmask", (bq[:, None] == np.arange(32)[None, :]).astype(np.float32))
    put("causal4", np.where(np.arange(4)[None, :] <= tq[:, None], 0.0, NEG).astype(np.float32))
    ow = np.zeros((128, 252), np.float32)
    ow[:, 124:128] = 1.0
    put("onesw", ow)
    put("iota256", np.broadcast_to(np.arange(256, dtype=np.float32)[None], (128, 256)))
    return t


FUSED = True
SEQ, NB, DEC_B, DEC_S, PASTL = 4096, 4, 32, 4, 16384


def _weights_map(inp):
    sk = inp["peer_subkeys"][0].reshape(16, 128, 128)
    return dict(
        params=param_table(inp["gn_g"][0], inp["gn_b"][0], inp["ln1_g"][0], inp["ln1_b"][0], inp["ln2_g"][0], inp["ln2_b"][0]),
        w_in=inp["w_in"][0], w_ret_o=inp["w_ret_o"][0], w_att_o=inp["w_att_o"][0], w_out=inp["w_out"][0],
        w_ple_gate=inp["w_ple_gate"][0], w_ple=inp["w_ple"][0], peer_wq=inp["peer_wq"][0],
        skT=np.ascontiguousarray(sk.transpose(2, 0, 1).reshape(128, 2048)),
        peer_u=inp["peer_u"][0], peer_v=inp["peer_v"][0])


def _sample_map(inp, real):
    nphys = inp["cache_k"].shape[1]
    if real:
        pt = np.asarray(inp["page_table"]).astype(np.int32)
        return dict(
            xs=inp["x_sample"].reshape(128, D), ps=inp["p_sample"][0].reshape(128, 256),
            cache_k=inp["cache_k"][0].reshape(nphys * 128, 256), cache_v=inp["cache_v"][0].reshape(nphys * 128, 256),
            cache_kidx=inp["cache_kidx"][0].reshape(nphys, 8192), state_in=inp["state_ret"][0],
            ptcol=np.ascontiguousarray(pt.T), ptrow=np.ascontiguousarray(np.repeat(pt, DEC_S, axis=0)), cst2=const_table2(DEC_S))
    z = lambda *s: np.zeros(s, np.float32)
    return dict(xs=z(128, D), ps=z(128, 256), cache_k=z(nphys * 128, 256), cache_v=z(nphys * 128, 256),
                cache_kidx=z(nphys, 8192), state_in=z(DEC_B, 4, 256, 512),
                ptcol=np.zeros((PASTL // 128, DEC_B), np.int32), ptrow=np.zeros((128, PASTL // 128), np.int32),
                cst2=const_table2(DEC_S))


def _cs_table(pos_blocks):
    cosb, sinb = rope_tables(np.concatenate(pos_blocks))
    n = len(pos_blocks)
    return np.ascontiguousarray(np.concatenate([cosb.reshape(n, 128, 224), sinb.reshape(n, 128, 224)], axis=2))


def kernel(**inp):
    inp = {k: np.asarray(v) for k, v in inp.items()}
    nphys = inp["cache_k"].shape[1]
    NL = NO = 16
    ikg, ikb = inp["idx_k_g"][0], inp["idx_k_b"][0]
    wmap = _weights_map(inp)
    spos = PASTL + (np.arange(128) % DEC_S)
    in_maps = []
    for c in range(8):
        b, half = c // 2, c % 2
        xb = np.zeros((NL + NO, 128, D), np.float32)
        if half == 1:
            xb[:NL] = inp["x_prompt"][b, 0:2048].reshape(NL, 128, D)
        xb[NL:] = inp["x_prompt"][b, half * 2048:(half + 1) * 2048].reshape(NO, 128, D)
        pos = [np.arange(i * 128, (i + 1) * 128) for i in range(NL)] + \
              [half * 2048 + np.arange(i * 128, (i + 1) * 128) for i in range(NO)] + [spos]
        m = dict(xb=xb, pb=np.ascontiguousarray(inp["p_prompt"][0, b, half * 2048:(half + 1) * 2048].reshape(NO, 128, 256)),
                 csb=_cs_table(pos), cst=const_table(0.0 if half == 1 else NEG, ikg, ikb))
        m.update(wmap)
        if FUSED:
            m.update(_sample_map(inp, real=(c == 0)))
        in_maps.append(m)
    if FUSED:
        cfg = Cfg(NL=NL, NO=NO, TOPK=256, sample=True, NSEQ=DEC_B, PAST=PASTL, NPHYS=nphys, TOPK_S=256)
        res = run_bass_kernel_spmd(get_prog(("fused", nphys), cfg), in_maps, core_ids=list(range(8))).results
        rs = res[0]
    else:
        cfg = Cfg(NL=NL, NO=NO, TOPK=256, sample=False)
        res = run_bass_kernel_spmd(get_prog(("prompt",), cfg), in_maps, core_ids=list(range(8))).results
        cfg_s = Cfg(sample=True, NSEQ=DEC_B, PAST=PASTL, NPHYS=nphys, TOPK_S=256, prompt=False)
        ms = dict(csb=_cs_table([spos]), cst=const_table(0.0, ikg, ikb))
        ms.update(wmap)
        ms.update(_sample_map(inp, real=True))
        rs = run_bass_kernel_spmd(get_prog(("sample", nphys), cfg_s), [ms], core_ids=[0]).results[0]
    y = np.empty((NB, SEQ, D), np.float32)
    kp = np.empty((1, NB, SEQ, 2, 128), np.float32)
    vp = np.empty((1, NB, SEQ, 2, 128), np.float32)
    ikp = np.empty((1, NB, SEQ, 64), np.float32)
    rp = np.empty((1, NB, 4, 256, 512), np.float32)
    for c in range(8):
        b, half = c // 2, c % 2
        sl = slice(half * 2048, (half + 1) * 2048)
        y[b, sl] = res[c]["y"].reshape(2048, D)
        kp[0, b, sl] = res[c]["kout"].reshape(2048, 2, 128)
        vp[0, b, sl] = res[c]["vout"].reshape(2048, 2, 128)
        ikp[0, b, sl] = res[c]["kidxout"].reshape(2048, 64)
        if half == 1:
            rp[0, b] = res[c]["stout"]
    ys = np.ascontiguousarray(rs["ys"].reshape(DEC_B, DEC_S, D))
    ks = np.ascontiguousarray(rs["ks"].reshape(1, DEC_B, DEC_S, 2, 128))
    vs = np.ascontiguousarray(rs["vs"].reshape(1, DEC_B, DEC_S, 2, 128))
    iks = np.ascontiguousarray(rs["kidxs"].reshape(1, DEC_B, DEC_S, 64))
    rsm = np.ascontiguousarray(rs["state_out"].reshape(1, DEC_B, 4, 256, 512))
    return (y, ys, kp, vp, ikp, rp, ks, vs, iks, rsm)
```
